# Optimizing a Trainium2 kernel written in Bass

```python
import math
import jax
import jax.numpy as jnp
from jax import lax
import numpy as np

D_MODEL = 2048
BATCH = 16
SEQ = 256
DEPTH = 2
DEC_BATCH = 4
DEC_SEQ = 4096
PAST_LEN = 512

GRID_W = 64
POS_BASE = 10000.0
N_EVEN = (DEPTH + 1) // 2
N_ODD = DEPTH // 2
N_DIR = 2
EPS = 1e-6

MIX_W_EVEN = 2 * D_MODEL
S5_W = MIX_W_EVEN // 4
S5_GROUP_CH = 16
S5_GROUPS = S5_W // S5_GROUP_CH
S5_STATE = 64
GLA_DV_W = MIX_W_EVEN - S5_W
GLA_HEADS = 6
GLA_DV = GLA_DV_W // GLA_HEADS
GLA_DK = GLA_DV // 2
GLA_DK_W = GLA_HEADS * GLA_DK
GLA_RANK = 16
GLA_NORMALIZER = 16.0
GLA_CHUNK = 64
GLA_LOG_DECAY_MIN = -1.0
EVEN_SIZES = (S5_W, S5_W, GLA_DK_W, GLA_DK_W, GLA_DV_W, GLA_DV_W, N_DIR * GLA_RANK)
EVEN_IN = 2 * S5_W + 2 * GLA_DK_W + 2 * GLA_DV_W + N_DIR * GLA_RANK

RWKV_W = D_MODEL
RWKV_HEAD = 64
RWKV_HEADS = RWKV_W // RWKV_HEAD
RWKV_DECAY_RANK = 96
RWKV_ICLR_RANK = 96
RWKV_LNX_EPS = 64e-5
ODD_SIZES = (RWKV_W, RWKV_W, RWKV_W, RWKV_W, N_DIR * RWKV_DECAY_RANK, N_DIR * RWKV_ICLR_RANK)
ODD_IN = 4 * RWKV_W + N_DIR * (RWKV_DECAY_RANK + RWKV_ICLR_RANK)

kernel_name = 'bidir_s5_gla_rwkv7_prefix_diffusion_step'


def _split_cols(t, sizes):
    offsets, acc = [], 0
    for s in sizes[:-1]:
        acc += s
        offsets.append(acc)
    return jnp.split(t, offsets, axis=-1)


def _rev(t):
    return jnp.flip(t, axis=1)


def _rms(x, w):
    xf = x.astype(jnp.float32)
    return xf * lax.rsqrt(jnp.mean(xf * xf, axis=-1, keepdims=True) + EPS) * w.astype(jnp.float32)


def _adaln(cond, w, b):
    m = jax.nn.silu(cond.astype(jnp.float32)) @ w.astype(jnp.float32) + b.astype(jnp.float32)
    shift, scale, gate = jnp.split(m, 3, axis=-1)
    return shift[..., None, :], scale[..., None, :], gate[..., None, :]


def _grid_pos_embed(n_tokens):
    rows = n_tokens // GRID_W
    row_id = jnp.broadcast_to(jnp.arange(rows, dtype=jnp.float32)[:, None], (rows, GRID_W)).reshape(-1)
    col_id = jnp.broadcast_to(jnp.arange(GRID_W, dtype=jnp.float32)[None, :], (rows, GRID_W)).reshape(-1)
    quarter = D_MODEL // 4
    omega = 1.0 / (POS_BASE ** (jnp.arange(quarter, dtype=jnp.float32) / quarter))
    def axis_emb(pos):
        ang = pos[:, None] * omega[None, :]
        return jnp.concatenate([jnp.sin(ang), jnp.cos(ang)], axis=-1)
    return jnp.concatenate([axis_emb(row_id), axis_emb(col_id)], axis=-1)


def _complex_affine_combine(e1, e2):
    a1r, a1i, b1r, b1i = e1
    a2r, a2i, b2r, b2i = e2
    return (a1r * a2r - a1i * a2i,
            a1r * a2i + a1i * a2r,
            a2r * b1r - a2i * b1i + b2r,
            a2r * b1i + a2i * b1r + b2i)


def _s5_direction(u, lam_re, lam_im, log_step, b_re, b_im, c_re, c_im, h0_re, h0_im):
    lam_re = lam_re.astype(jnp.float32)
    lam_im = lam_im.astype(jnp.float32)
    dt = jnp.exp(log_step.astype(jnp.float32))[:, None]
    mag = jnp.exp(lam_re * dt)
    ab_re, ab_im = mag * jnp.cos(lam_im * dt), mag * jnp.sin(lam_im * dt)
    den = lam_re * lam_re + lam_im * lam_im
    f_re = ((ab_re - 1.0) * lam_re + ab_im * lam_im) / den
    f_im = (ab_im * lam_re - (ab_re - 1.0) * lam_im) / den
    bb_re = f_re[..., None] * b_re - f_im[..., None] * b_im
    bb_im = f_re[..., None] * b_im + f_im[..., None] * b_re
    bu_re = jnp.einsum('blgh,gph->blgp', u, bb_re)
    bu_im = jnp.einsum('blgh,gph->blgp', u, bb_im)
    a_re = jnp.broadcast_to(ab_re, bu_re.shape)
    a_im = jnp.broadcast_to(ab_im, bu_im.shape)
    pw_re, pw_im, s_re, s_im = lax.associative_scan(_complex_affine_combine, (a_re, a_im, bu_re, bu_im), axis=1)
    h0r, h0i = h0_re[:, None], h0_im[:, None]
    h_re = s_re + pw_re * h0r - pw_im * h0i
    h_im = s_im + pw_re * h0i + pw_im * h0r
    y = jnp.einsum('blgp,ghp->blgh', h_re, c_re) - jnp.einsum('blgp,ghp->blgh', h_im, c_im)
    return y, h_re[:, -1], h_im[:, -1]


def _gla_direction(q, k, v, log_a, s0):
    bsz, L, H, _ = q.shape
    n = L // GLA_CHUNK
    def chunks(t):
        return t.reshape(bsz, n, GLA_CHUNK, H, t.shape[-1]).transpose(1, 0, 3, 2, 4)
    causal = jnp.tril(jnp.ones((GLA_CHUNK, GLA_CHUNK), dtype=bool))
    def step(S, xs):
        qc, kc, vc, gc = xs
        b = jnp.cumsum(gc, axis=-2)
        b_last = b[..., -1:, :]
        q_dec = qc * jnp.exp(b)
        k_inv = kc * jnp.exp(-b)
        k_end = kc * jnp.exp(b_last - b)
        att = jnp.where(causal, jnp.einsum('bhcd,bhsd->bhcs', q_dec, k_inv), 0.0)
        o = jnp.einsum('bhcs,bhse->bhce', att, vc) + jnp.einsum('bhcd,bhde->bhce', q_dec, S)
        S = jnp.exp(b_last[..., 0, :])[..., None] * S + jnp.einsum('bhcd,bhce->bhde', k_end, vc)
        return S, o
    S, o = lax.scan(step, s0, (chunks(q), chunks(k), chunks(v), chunks(log_a)))
    return o.transpose(1, 0, 3, 2, 4).reshape(bsz, L, H, -1), S


def _rwkv_direction(r, w, k, v, a, b, s0):
    def tm(t):
        return jnp.swapaxes(t, 0, 1)
    def step(S, xs):
        r_t, w_t, k_t, v_t, a_t, b_t = xs
        sa = jnp.einsum('bhij,bhj->bhi', S, a_t)
        S = S * w_t[:, :, None, :] + sa[..., None] * b_t[:, :, None, :] + v_t[..., None] * k_t[:, :, None, :]
        return S, jnp.einsum('bhij,bhj->bhi', S, r_t)
    S, y = lax.scan(step, s0, tuple(tm(t) for t in (r, w, k, v, a, b)))
    return tm(y), S


def _even_mixer(h, s5_re0, s5_im0, gla0, w_in, w_out, lam_re, lam_im, log_step, b_re, b_im, c_re, c_im,
                d_skip, glu_w, glu_b, dec_up, dec_b, gla_nw):
    bsz, L, _ = h.shape
    u, s5_g, q, k, v, gla_g, dec_lr = _split_cols(h @ w_in, EVEN_SIZES)
    u_g = u.reshape(bsz, L, S5_GROUPS, S5_GROUP_CH)
    y = d_skip * u
    fin_re, fin_im = [], []
    for d in range(N_DIR):
        u_d = _rev(u_g) if d else u_g
        y_d, hr, hi = _s5_direction(u_d, lam_re[d], lam_im[d], log_step[d], b_re[d], b_im[d], c_re[d], c_im[d],
                                    s5_re0[:, d].astype(jnp.float32), s5_im0[:, d].astype(jnp.float32))
        y = y + (_rev(y_d) if d else y_d).reshape(bsz, L, S5_W)
        fin_re.append(hr)
        fin_im.append(hi)
    gy = jax.nn.gelu(y)
    s5_out = gy * jax.nn.sigmoid(gy @ glu_w + glu_b) * jax.nn.silu(s5_g)
    q = q.reshape(bsz, L, GLA_HEADS, GLA_DK) * (GLA_DK ** -0.5)
    k = k.reshape(bsz, L, GLA_HEADS, GLA_DK)
    v = v.reshape(bsz, L, GLA_HEADS, GLA_DV)
    dec_lr = dec_lr.reshape(bsz, L, N_DIR, GLA_RANK)
    o = 0.0
    fin_gla = []
    for d in range(N_DIR):
        z = dec_lr[:, :, d] @ dec_up[d] + dec_b[d]
        log_a = jnp.maximum(jax.nn.log_sigmoid(z) / GLA_NORMALIZER, GLA_LOG_DECAY_MIN)
        seq = (q, k, v, log_a.reshape(bsz, L, GLA_HEADS, GLA_DK))
        if d:
            seq = tuple(_rev(t) for t in seq)
        o_d, S = _gla_direction(*seq, gla0[:, d].astype(jnp.float32))
        o = o + (_rev(o_d) if d else o_d)
        fin_gla.append(S)
    o = o * lax.rsqrt(jnp.mean(o * o, axis=-1, keepdims=True) + EPS) * gla_nw.reshape(GLA_HEADS, GLA_DV)
    gla_out = o.reshape(bsz, L, GLA_DV_W) * jax.nn.silu(gla_g)
    out = jnp.concatenate([s5_out, gla_out], axis=-1) @ w_out
    return out, jnp.stack(fin_re, axis=1), jnp.stack(fin_im, axis=1), jnp.stack(fin_gla, axis=1)


def _odd_mixer(h, rwkv0, w_in, w_out, mu, w0, w2, a0, a2, k_k, k_a, r_k, lnx_w, lnx_b):
    bsz, L, _ = h.shape
    zero = jnp.zeros_like(h[:, :1])
    h_prev = jnp.concatenate([zero, h[:, :-1]], axis=1)
    h_next = jnp.concatenate([h[:, 1:], zero], axis=1)
    xs = h + mu[0] * (h_prev - h) + mu[1] * (h_next - h)
    r, k, v, g, w_lr, a_lr = _split_cols(xs @ w_in, ODD_SIZES)
    w_lr = jnp.tanh(w_lr).reshape(bsz, L, N_DIR, RWKV_DECAY_RANK)
    a_lr = a_lr.reshape(bsz, L, N_DIR, RWKV_ICLR_RANK)
    def heads(t):
        return t.reshape(bsz, L, RWKV_HEADS, RWKV_HEAD)
    kk = heads(k * k_k)
    kk = kk / jnp.maximum(jnp.sqrt(jnp.sum(kk * kk, axis=-1, keepdims=True)), 1e-12)
    r_h, v_h = heads(r), heads(v)
    wkv, bonus = 0.0, 0.0
    finals = []
    for d in range(N_DIR):
        w_log = -jax.nn.softplus(-(w0[d] + w_lr[:, :, d] @ w2[d])) - 0.5
        decay = jnp.exp(-jnp.exp(w_log))
        a = jax.nn.sigmoid(a0[d] + a_lr[:, :, d] @ a2[d])
        k_d = heads(k * (1.0 + (a - 1.0) * k_a))
        seq = (r_h, heads(decay), k_d, v_h, -kk, kk * heads(a))
        if d:
            seq = tuple(_rev(t) for t in seq)
        y_d, S = _rwkv_direction(*seq, rwkv0[:, d].astype(jnp.float32))
        wkv = wkv + (_rev(y_d) if d else y_d)
        bonus = bonus + jnp.sum(r_h * k_d * r_k, axis=-1, keepdims=True) * v_h
        finals.append(S)
    mean = jnp.mean(wkv, axis=-1, keepdims=True)
    var = jnp.mean(jnp.square(wkv - mean), axis=-1, keepdims=True)
    ln = ((wkv - mean) * lax.rsqrt(var + RWKV_LNX_EPS) * lnx_w.reshape(RWKV_HEADS, RWKV_HEAD)
          + lnx_b.reshape(RWKV_HEADS, RWKV_HEAD))
    out = (ln + bonus).reshape(bsz, L, RWKV_W) * jax.nn.silu(g)
    return out @ w_out, jnp.stack(finals, axis=1)


def setup_inputs(seed: int = 0) -> dict:
    key = jax.random.key(seed)
    ks = jax.random.split(key, 40)
    f32 = jnp.float32
    def nrm(i, shape, scale):
        return jax.random.normal(ks[i], shape, f32) * scale
    def uni(i, shape, lo, hi):
        return jax.random.uniform(ks[i], shape, f32, lo, hi)
    return {
        'x_prompt': nrm(0, (BATCH, SEQ, D_MODEL), 1.0),
        'x_sample': nrm(1, (DEC_BATCH, DEC_SEQ, D_MODEL), 1.0),
        'state_s5_re': nrm(2, (DEC_BATCH, N_EVEN, N_DIR, S5_GROUPS, S5_STATE), 0.1),
        'state_s5_im': nrm(3, (DEC_BATCH, N_EVEN, N_DIR, S5_GROUPS, S5_STATE), 0.1),
        'state_gla': nrm(4, (DEC_BATCH, N_EVEN, N_DIR, GLA_HEADS, GLA_DK, GLA_DV), 1.0),
        'state_rwkv': nrm(5, (DEC_BATCH, N_ODD, N_DIR, RWKV_HEADS, RWKV_HEAD, RWKV_HEAD), 0.5),
        'c': nrm(6, (DEC_BATCH, D_MODEL), 1.0),
        'c_ctx': nrm(7, (D_MODEL,), 1.0),
        'norm_w': 1.0 + nrm(8, (DEPTH, D_MODEL), 0.02),
        'ada_w': nrm(9, (DEPTH, D_MODEL, 3 * D_MODEL), 0.5 * D_MODEL ** -0.5),
        'ada_b': nrm(10, (DEPTH, 3 * D_MODEL), 0.02),
        'final_norm_w': 1.0 + nrm(11, (D_MODEL,), 0.02),
        'e_w_in': nrm(12, (N_EVEN, D_MODEL, EVEN_IN), D_MODEL ** -0.5),
        'e_w_out': nrm(13, (N_EVEN, MIX_W_EVEN, D_MODEL), MIX_W_EVEN ** -0.5),
        's5_lambda_re': -0.5 + nrm(14, (N_EVEN, N_DIR, S5_GROUPS, S5_STATE), 0.01),
        's5_lambda_im': math.pi * jnp.arange(S5_STATE, dtype=f32) + nrm(15, (N_EVEN, N_DIR, S5_GROUPS, S5_STATE), 0.01),
        's5_log_step': uni(16, (N_EVEN, N_DIR, S5_GROUPS), math.log(1e-3), math.log(1e-1)),
        's5_b_re': nrm(17, (N_EVEN, N_DIR, S5_GROUPS, S5_STATE, S5_GROUP_CH), (2 * S5_GROUP_CH) ** -0.5),
        's5_b_im': nrm(18, (N_EVEN, N_DIR, S5_GROUPS, S5_STATE, S5_GROUP_CH), (2 * S5_GROUP_CH) ** -0.5),
        's5_c_re': nrm(19, (N_EVEN, N_DIR, S5_GROUPS, S5_GROUP_CH, S5_STATE), (2 * S5_STATE) ** -0.5),
        's5_c_im': nrm(20, (N_EVEN, N_DIR, S5_GROUPS, S5_GROUP_CH, S5_STATE), (2 * S5_STATE) ** -0.5),
        's5_d': nrm(21, (N_EVEN, S5_W), 1.0),
        's5_glu_w': nrm(22, (N_EVEN, S5_W, S5_W), S5_W ** -0.5),
        's5_glu_b': nrm(23, (N_EVEN, S5_W), 0.02),
        'gla_decay_up': nrm(24, (N_EVEN, N_DIR, GLA_RANK, GLA_DK_W), GLA_RANK ** -0.5),
        'gla_decay_b': nrm(25, (N_EVEN, N_DIR, GLA_DK_W), 0.1),
        'gla_norm_w': 1.0 + nrm(26, (N_EVEN, GLA_DV_W), 0.02),
        'o_w_in': nrm(27, (N_ODD, D_MODEL, ODD_IN), D_MODEL ** -0.5),
        'o_w_out': nrm(28, (N_ODD, RWKV_W, D_MODEL), RWKV_W ** -0.5),
        'rwkv_mu': uni(29, (N_ODD, 2, D_MODEL), 0.0, 0.5),
        'rwkv_w0': jnp.linspace(-6.5, -1.5, RWKV_W, dtype=f32) + nrm(30, (N_ODD, N_DIR, RWKV_W), 0.1),
        'rwkv_w2': nrm(31, (N_ODD, N_DIR, RWKV_DECAY_RANK, RWKV_W), 0.1 * RWKV_DECAY_RANK ** -0.5),
        'rwkv_a0': nrm(32, (N_ODD, N_DIR, RWKV_W), 0.1),
        'rwkv_a2': nrm(33, (N_ODD, N_DIR, RWKV_ICLR_RANK, RWKV_W), 0.1 * RWKV_ICLR_RANK ** -0.5),
        'rwkv_k_k': 0.85 + nrm(34, (N_ODD, RWKV_W), 0.05),
        'rwkv_k_a': 1.0 + nrm(35, (N_ODD, RWKV_W), 0.05),
        'rwkv_r_k': nrm(36, (N_ODD, RWKV_HEADS, RWKV_HEAD), 0.1),
        'rwkv_lnx_w': 1.0 + nrm(37, (N_ODD, RWKV_W), 0.02),
        'rwkv_lnx_b': nrm(38, (N_ODD, RWKV_W), 0.02),
    }


def reference(x_prompt, x_sample, state_s5_re, state_s5_im, state_gla, state_rwkv, c,
              c_ctx, norm_w, ada_w, ada_b, final_norm_w,
              e_w_in, e_w_out, s5_lambda_re, s5_lambda_im, s5_log_step, s5_b_re, s5_b_im, s5_c_re, s5_c_im,
              s5_d, s5_glu_w, s5_glu_b, gla_decay_up, gla_decay_b, gla_norm_w,
              o_w_in, o_w_out, rwkv_mu, rwkv_w0, rwkv_w2, rwkv_a0, rwkv_a2, rwkv_k_k, rwkv_k_a, rwkv_r_k,
              rwkv_lnx_w, rwkv_lnx_b):
    f32 = jnp.float32
    bp = x_prompt.shape[0]
    x_ctx = x_prompt
    x_lat = (x_sample.astype(f32) + _grid_pos_embed(x_sample.shape[1])[None]).astype(x_sample.dtype)
    z_s5 = jnp.zeros((bp, N_DIR, S5_GROUPS, S5_STATE), f32)
    z_gla = jnp.zeros((bp, N_DIR, GLA_HEADS, GLA_DK, GLA_DV), f32)
    z_rwkv = jnp.zeros((bp, N_DIR, RWKV_HEADS, RWKV_HEAD, RWKV_HEAD), f32)
    new_s5_re, new_s5_im, new_gla, new_rwkv = [], [], [], []
    for i in range(DEPTH):
        j = i // 2
        sh_c, sc_c, gt_c = _adaln(c_ctx, ada_w[i], ada_b[i])
        sh_l, sc_l, gt_l = _adaln(c, ada_w[i], ada_b[i])
        h_ctx = _rms(x_ctx, norm_w[i]) * (1.0 + sc_c) + sh_c
        h_lat = _rms(x_lat, norm_w[i]) * (1.0 + sc_l) + sh_l
        if i % 2 == 0:
            p = (e_w_in[j], e_w_out[j], s5_lambda_re[j], s5_lambda_im[j], s5_log_step[j], s5_b_re[j], s5_b_im[j],
                 s5_c_re[j], s5_c_im[j], s5_d[j], s5_glu_w[j], s5_glu_b[j], gla_decay_up[j], gla_decay_b[j],
                 gla_norm_w[j])
            o_ctx, fr, fi, fg = _even_mixer(h_ctx, z_s5, z_s5, z_gla, *p)
            o_lat, _, _, _ = _even_mixer(h_lat, state_s5_re[:, j], state_s5_im[:, j], state_gla[:, j], *p)
            new_s5_re.append(fr)
            new_s5_im.append(fi)
            new_gla.append(fg)
        else:
            p = (o_w_in[j], o_w_out[j], rwkv_mu[j], rwkv_w0[j], rwkv_w2[j], rwkv_a0[j], rwkv_a2[j],
                 rwkv_k_k[j], rwkv_k_a[j], rwkv_r_k[j], rwkv_lnx_w[j], rwkv_lnx_b[j])
            o_ctx, fw = _odd_mixer(h_ctx, z_rwkv, *p)
            o_lat, _ = _odd_mixer(h_lat, state_rwkv[:, j], *p)
            new_rwkv.append(fw)
        x_ctx = x_ctx + (gt_c * o_ctx).astype(x_ctx.dtype)
        x_lat = x_lat + (gt_l * o_lat).astype(x_lat.dtype)
    y_prompt = _rms(x_ctx, final_norm_w).astype(x_prompt.dtype)
    y_sample = _rms(x_lat, final_norm_w).astype(x_sample.dtype)
    return (y_prompt, y_sample, jnp.stack(new_s5_re, axis=1), jnp.stack(new_s5_im, axis=1),
            jnp.stack(new_gla, axis=1), jnp.stack(new_rwkv, axis=1))
```

```python
import math
from contextlib import ExitStack
import numpy as np
import concourse.bass as bass
import concourse.mybir as mybir
from concourse.bass_utils import run_bass_kernel_spmd

F32 = mybir.dt.float32
I32 = mybir.dt.int32
AF = mybir.ActivationFunctionType
ALU = mybir.AluOpType
AX = mybir.AxisListType

D = 2048
KD = D // 128
LP = 256
EPS = 1e-6
S5_W = 1024
G_S5 = 64
P_S5 = 64
H_S5 = 16
GH = 6
GDK = 256
GDV = 512
GDKW = GH * GDK
GDVW = GH * GDV
EVEN_IN = 11296
RH = 32
RN = 64
ODD_IN = 8576
LNX_EPS = 64e-5
TWO_PI = 2.0 * math.pi


class Buf:
    __slots__ = ("name", "writers", "readers", "dma_sem", "dma_total")

    def __init__(self, name):
        self.name = name
        self.writers = {}
        self.readers = {}
        self.dma_sem = None
        self.dma_total = 0


class _Rec:
    def __init__(self):
        self.call = None

    def __getattr__(self, name):
        def f(*a, **kw):
            self.call = (name, a, kw)
            return self
        return f


class Sched:
    COMPUTE = ("tensor", "vector", "scalar", "gpsimd")

    def __init__(self, nc):
        self.nc = nc
        self.ops = {e: [] for e in ("tensor", "vector", "scalar", "gpsimd", "sync")}
        self.sem = {}
        self.count = {}
        self.seen = {e: {} for e in self.ops}
        self.sems = {}
        for e in self.COMPUTE:
            self.sem[e] = ("E", e)
            self.sems[("E", e)] = nc.alloc_semaphore(name=f"prog_{e}")
            self.count[e] = 0
        self.free_sems = []
        self.pending = {e: [] for e in self.ops}
        self.sem_total = {}
        self.nsem = 0
        self.nops = 0

    def _collect(self, E, reads, writes, partial=()):
        need = {}
        for b in reads:
            for k, (v, en) in b.writers.items():
                if en == E and E == "tensor":
                    continue
                if need.get(k, 0) < v:
                    need[k] = v
        for b in writes:
            for k, (v, en) in b.writers.items():
                if en == E:
                    continue
                if need.get(k, 0) < v:
                    need[k] = v
            for k, (v, en) in b.readers.items():
                if en == E:
                    continue
                if need.get(k, 0) < v:
                    need[k] = v
        for b in partial:
            for k, (v, en) in b.readers.items():
                if en == E:
                    continue
                if need.get(k, 0) < v:
                    need[k] = v
        seen = self.seen[E]
        out = []
        for k, v in need.items():
            if seen.get(k, 0) < v:
                seen[k] = v
                out.append((k, v))
        return out

    def _record(self, k, v, en, reads, writes, partial):
        for b in writes:
            b.writers = {k: (v, en)}
            b.readers = {}
        for b in partial:
            b.writers[k] = (v, en)
        for b in reads:
            b.readers[k] = (v, en)

    def barrier(self):
        for E in self.ops:
            seen = self.seen[E]
            for e2 in self.COMPUTE:
                if e2 != E and seen.get(self.sem[e2], 0) < self.count[e2]:
                    seen[self.sem[e2]] = self.count[e2]
                    self.pending[E].append((self.sem[e2], self.count[e2]))
            for k, v in self.sem_total.items():
                if seen.get(k, 0) < v:
                    seen[k] = v
                    self.pending[E].append((k, v))

    def op(self, E, fn, reads=(), writes=(), partial=()):
        waits = self.pending[E] + self._collect(E, reads, writes, partial)
        self.pending[E] = []
        self.count[E] += 1
        rec = _Rec()
        fn(rec)
        self.ops[E].append((waits, rec.call, self.sem[E], 1))
        self._record(self.sem[E], self.count[E], E, reads, writes, partial)
        self.nops += 1

    def dma(self, Q, fn, sb, reads=(), writes=(), partial=()):
        waits = self.pending[Q] + self._collect(Q, reads, writes, partial)
        self.pending[Q] = []
        if sb.dma_sem is None:
            if self.free_sems:
                key, total = self.free_sems.pop()
                sb.dma_sem = key
                sb.dma_total = total
                if self.seen[Q].get(key, 0) < total:
                    self.seen[Q][key] = total
                    waits.append((key, total))
            elif self.nsem < 88:
                key = ("D", self.nsem)
                self.sems[key] = self.nc.alloc_semaphore(name=f"dma_{self.nsem}")
                self.nsem += 1
                sb.dma_sem = key
            else:
                raise RuntimeError("too many dma semaphores")
        sb.dma_total += 16
        self.sem_total[sb.dma_sem] = sb.dma_total
        rec = _Rec()
        fn(rec)
        self.ops[Q].append((waits, rec.call, sb.dma_sem, 16))
        self._record(sb.dma_sem, sb.dma_total, None, reads, writes, partial)
        self.nops += 1

    def release(self, bufs):
        for b in bufs:
            if b.dma_sem is not None:
                self.free_sems.append((b.dma_sem, b.dma_total))
                b.dma_sem = None

    def finish(self, block):
        fin = list(self.sem_total.items())
        sems = self.sems
        ops = self.ops

        def replay(eng, name):
            for waits, fn, semkey, inc in ops[name]:
                for k, v in waits:
                    eng.wait_ge(sems[k], v)
                getattr(eng, fn[0])(*fn[1], **fn[2]).then_inc(sems[semkey], inc)
            if name == "sync":
                for k, v in fin:
                    eng.wait_ge(sems[k], v)

        block.sync(lambda e: replay(e, "sync"))
        block.tensor(lambda e: replay(e, "tensor"))
        block.vector(lambda e: replay(e, "vector"))
        block.scalar(lambda e: replay(e, "scalar"))
        block.gpsimd(lambda e: replay(e, "gpsimd"))


class TB:
    __slots__ = ("t", "b")

    def __init__(self, t, name):
        self.t = t
        self.b = Buf(name)

    def __getitem__(self, k):
        return self.t[k]


class Ring:
    def __init__(self, items):
        self.items = items
        self.i = 0

    def next(self):
        it = self.items[self.i % len(self.items)]
        self.i += 1
        return it


class Prog:
    def __init__(self, LS, NPR, debug=()):
        self.LS = LS
        self.NPR = NPR
        self.NT = LS + NPR * LP
        assert LS % 512 == 0 and self.NT % 512 == 0
        self.debug = set(debug)
        self.nc = bass.Bass("TRN2", target_bir_lowering=False)
        self.S = Sched(self.nc)
        self.inp = {}
        self.outp = {}
        self.scr = {}
        self.dbuf = {}
        self.seqs = [(0, LS, 0)] + [(LS + j * LP, LP, 1) for j in range(NPR)]

    def din(self, name, shape):
        self.inp[name] = self.nc.dram_tensor(name, list(shape), F32, kind="ExternalInput").ap()
        self.dbuf[name] = Buf(name)
        return self.inp[name]

    def dout(self, name, shape):
        self.outp[name] = self.nc.dram_tensor(name, list(shape), F32, kind="ExternalOutput").ap()
        self.dbuf[name] = Buf(name)
        return self.outp[name]

    def dscr(self, name, shape):
        kind = "ExternalOutput" if name in self.debug else "Internal"
        self.scr[name] = self.nc.dram_tensor(name, list(shape), F32, kind=kind).ap()
        self.dbuf[name] = Buf(name)
        return self.scr[name]

    def dreg(self, name, key):
        k = (name, key)
        if k not in self.dbuf:
            self.dbuf[k] = Buf(str(k))
        return self.dbuf[k]

    def sb(self, st, name, shape, dt=F32):
        return TB(st.enter_context(self.nc.sbuf_tensor(name, list(shape), dt)), name)

    def ring(self, st, name, shape, n, dt=F32):
        return Ring([self.sb(st, f"{name}{i}", shape, dt) for i in range(n)])

    def V(self, fn, reads=(), writes=(), partial=()):
        self.S.op("vector", fn, [r.b if isinstance(r, TB) else r for r in reads],
                  [w.b if isinstance(w, TB) else w for w in writes],
                  [w.b if isinstance(w, TB) else w for w in partial])

    def A(self, fn, reads=(), writes=(), partial=()):
        self.S.op("scalar", fn, [r.b if isinstance(r, TB) else r for r in reads],
                  [w.b if isinstance(w, TB) else w for w in writes],
                  [w.b if isinstance(w, TB) else w for w in partial])

    def G(self, fn, reads=(), writes=(), partial=()):
        self.S.op("gpsimd", fn, [r.b if isinstance(r, TB) else r for r in reads],
                  [w.b if isinstance(w, TB) else w for w in writes],
                  [w.b if isinstance(w, TB) else w for w in partial])

    def T(self, fn, reads=(), writes=(), partial=()):
        self.S.op("tensor", fn, [r.b if isinstance(r, TB) else r for r in reads],
                  [w.b if isinstance(w, TB) else w for w in writes],
                  [w.b if isinstance(w, TB) else w for w in partial])

    def load(self, dst, dst_ap, src_ap, src_bufs=(), partial=False):
        if partial:
            self.S.dma("sync", lambda e: e.dma_start(out=dst_ap, in_=src_ap), dst.b,
                       reads=list(src_bufs), partial=[dst.b])
        else:
            self.S.dma("sync", lambda e: e.dma_start(out=dst_ap, in_=src_ap), dst.b,
                       reads=list(src_bufs), writes=[dst.b])

    def store(self, src, dst_ap, src_ap, dst_bufs=(), partial_bufs=()):
        import os
        self.S.dma(os.environ.get("STQ", "gpsimd"), lambda e: e.dma_start(out=dst_ap, in_=src_ap), src.b,
                   reads=[src.b], writes=list(dst_bufs), partial=list(partial_bufs))

    def mm(self, ps, out_ap, lhsT_ap, rhs_ap, reads, start, stop, first=None):
        if first is None:
            first = start
        self.T(lambda e: e.matmul(out_ap, lhsT_ap, rhs_ap, start=start, stop=stop),
               reads=reads, writes=[ps] if first else (), partial=() if first else [ps])

    def tr(self, ps, out_ap, in_ap, reads):
        ident = self.ident
        self.T(lambda e: e.matmul(out_ap, in_ap, ident[:], start=True, stop=True, is_transpose=True), reads=list(reads) + [ident], partial=[ps])

    def ev(self, out_ap, in_ap, reads, writes=(), partial=()):
        self._evc = getattr(self, "_evc", 0) + 1
        if self._evc % 2:
            self.V(lambda e: e.tensor_copy(out_ap, in_ap), reads=reads, writes=writes, partial=partial)
        else:
            self.A(lambda e: e.copy(out_ap, in_ap), reads=reads, writes=writes, partial=partial)

    def setup(self, st):
        nc = self.nc
        self.ps = Ring([TB(st.enter_context(nc.psum_tensor(f"ps{i}", [128, 512], F32)), f"ps{i}") for i in range(8)])
        self.ident = self.sb(st, "ident", [128, 128])
        self.ones = self.sb(st, "ones", [128, 128])
        tmp = self.sb(st, "setup_tmp", [128, 128])
        self.G(lambda e: e.iota(tmp[:], [[1, 128]], base=0, channel_multiplier=-1,
                                allow_small_or_imprecise_dtypes=True), writes=[tmp])
        ident = self.ident
        self.V(lambda e: e.tensor_single_scalar(ident[:], tmp[:], 0.0, op=ALU.is_equal), reads=[tmp], writes=[ident])
        ones = self.ones
        self.V(lambda e: e.memset(ones[:], 1.0), writes=[ones])
        self.m_incl = [self.sb(st, f"m_incl{d}", [128, 128]) for d in range(2)]
        self.m_strict = [self.sb(st, f"m_strict{d}", [128, 128]) for d in range(2)]
        for d in range(2):
            mi, ms = self.m_incl[d], self.m_strict[d]
            op_i = ALU.is_ge if d == 0 else ALU.is_le
            op_s = ALU.is_gt if d == 0 else ALU.is_lt
            self.V(lambda e, mi=mi, op_i=op_i: e.tensor_single_scalar(mi[:], tmp[:], 0.0, op=op_i), reads=[tmp], writes=[mi])
            self.V(lambda e, ms=ms, op_s=op_s: e.tensor_single_scalar(ms[:], tmp[:], 0.0, op=op_s), reads=[tmp], writes=[ms])

    def adaln(self):
        nc = self.nc
        condT = self.din("condT", [128, KD, 2])
        ada_w = self.din("ada_w", [2, D, 3 * D])
        ada_bT = self.din("ada_bT", [2, 128, 48])
        norm_wT = self.din("norm_wT", [2, 128, KD])
        st0 = self.pst
        self.pad0 = self.sb(st0, "pad0", [128, 64])
        self.shiftT = [self.sb(st0, f"shiftT{l}", [128, KD, 2]) for l in range(2)]
        self.multT = [self.sb(st0, f"multT{l}", [128, KD, 2]) for l in range(2)]
        self.gateT = [self.sb(st0, f"gateT{l}", [128, KD, 2]) for l in range(2)]
        with ExitStack() as st:
            sc = self.sb(st, "ada_sc", [128, KD, 2])
            bt = self.sb(st, "ada_bt", [128, 48])
            nw = self.sb(st, "ada_nw", [128, KD])
            mt = self.sb(st, "ada_mt", [128, 48, 2])
            wr = self.ring(st, "ada_w", [128, KD, 512], 2)
            self.load(sc, sc[:], condT[:, :, :])
            self.A(lambda e: e.activation(out=sc[:], in_=sc[:], func=AF.Silu), reads=[sc], writes=[sc])
            for l in range(2):
                self.load(bt, bt[:], ada_bT[l, :, :])
                self.load(nw, nw[:], norm_wT[l, :, :])
                ps = self.ps.next()
                wv = ada_w[l].rearrange("(k p) n -> p k n", p=128)
                for cb in range(12):
                    wt = wr.next()
                    for kq in range(4):
                        self.load(wt, wt[:, kq * 4:(kq + 1) * 4, :], wv[:, kq * 4:(kq + 1) * 4, cb * 512:(cb + 1) * 512], partial=(kq > 0))
                    for m in range(4):
                        nb = cb * 4 + m
                        for k in range(KD):
                            self.mm(ps, ps[:, 2 * nb:2 * nb + 2], wt[:, k, m * 128:(m + 1) * 128], sc[:, k, :],
                                    reads=[wt, sc], start=(k == 0), stop=(k == KD - 1), first=(k == 0 and nb == 0))
                self.V(lambda e, ps=ps: e.tensor_tensor(out=mt[:], in0=ps[:, 0:96].rearrange("p (n c) -> p n c", c=2),
                                                        in1=bt[:].unsqueeze(2).to_broadcast([128, 48, 2]), op=ALU.add),
                       reads=[ps, bt], writes=[mt])
                sh, mu, ga = self.shiftT[l], self.multT[l], self.gateT[l]
                self.V(lambda e, sh=sh: e.tensor_copy(sh[:], mt[:, 0:16, :]), reads=[mt], writes=[sh])
                self.V(lambda e, ga=ga: e.tensor_copy(ga[:], mt[:, 32:48, :]), reads=[mt], writes=[ga])
                self.V(lambda e, mu=mu: e.scalar_tensor_tensor(out=mu[:], in0=mt[:, 16:32, :], scalar=1.0,
                                                                in1=nw[:].unsqueeze(2).to_broadcast([128, KD, 2]),
                                                                op0=ALU.add, op1=ALU.mult),
                       reads=[mt, nw], writes=[mu])
            self.S.barrier()
            self.S.release([sc.b, bt.b, nw.b] + [w.b for w in wr.items])

    def rr_sin(self, st, out_t, ang_t, shape, pfx):
        ni = self.sb(st, pfx + "_ni", shape, I32)
        nf = self.sb(st, pfx + "_nf", shape, F32)
        self.V(lambda e: e.tensor_single_scalar(nf[:], ang_t[:], 1.0 / TWO_PI, op=ALU.mult), reads=[ang_t], writes=[nf])
        self.V(lambda e: e.tensor_copy(ni[:], nf[:]), reads=[nf], writes=[ni])
        self.V(lambda e: e.tensor_copy(nf[:], ni[:]), reads=[ni], writes=[nf])
        self.V(lambda e: e.scalar_tensor_tensor(out=ang_t[:], in0=nf[:], scalar=-TWO_PI, in1=ang_t[:],
                                                op0=ALU.mult, op1=ALU.add), reads=[nf, ang_t], writes=[ang_t])
        self.V(lambda e: e.tensor_scalar(out=ang_t[:], in0=ang_t[:], scalar1=math.pi, scalar2=-math.pi,
                                         op0=ALU.min, op1=ALU.max), reads=[ang_t], writes=[ang_t])
        self.A(lambda e: e.activation(out=out_t[:], in_=ang_t[:], func=AF.Sin), reads=[ang_t], writes=[out_t])

    def posembed(self, st):
        LS = self.LS
        E = self.sb(st, "pe_E", [64, 1024])
        with ExitStack() as s2:
            om = self.sb(s2, "pe_om", [64, 512])
            kk = self.sb(s2, "pe_k", [64, 1])
            a1 = self.sb(s2, "pe_a1", [64, 512])
            a2 = self.sb(s2, "pe_a2", [64, 512])
            o1 = self.sb(s2, "pe_o1", [64, 512])
            self.G(lambda e: e.iota(om[:], [[1, 512]], base=0, channel_multiplier=0, allow_small_or_imprecise_dtypes=True), writes=[om])
            self.G(lambda e: e.iota(kk[:], [[1, 1]], base=0, channel_multiplier=1, allow_small_or_imprecise_dtypes=True), writes=[kk])
            self.A(lambda e: e.activation(out=om[:], in_=om[:], func=AF.Exp, scale=-math.log(10000.0) / 512.0), reads=[om], writes=[om])
            self.V(lambda e: e.tensor_scalar(out=a1[:], in0=om[:], scalar1=kk[:, 0:1], scalar2=None, op0=ALU.mult), reads=[om, kk], writes=[a1])
            self.V(lambda e: e.tensor_scalar(out=a2[:], in0=a1[:], scalar1=math.pi / 2, scalar2=None, op0=ALU.add), reads=[a1], writes=[a2])
            self.rr_sin(s2, o1, a1, [64, 512], "pe_r1")
            self.V(lambda e: e.tensor_copy(E[:, 0:512], o1[:]), reads=[o1], partial=[E])
            self.rr_sin(s2, o1, a2, [64, 512], "pe_r2")
            self.V(lambda e: e.tensor_copy(E[:, 512:1024], o1[:]), reads=[o1], partial=[E])
            self.S.barrier()
        self.pe_E = E
        selrow = self.sb(st, "pe_selrow", [64, LS])
        selcol = self.sb(st, "pe_selcol", [64, 128])
        with ExitStack() as s2:
            v1 = self.sb(s2, "pe_v1", [64, LS])
            m1 = self.sb(s2, "pe_m1", [64, LS])
            self.G(lambda e: e.iota(v1[:], [[1, LS]], base=0, channel_multiplier=-64, allow_small_or_imprecise_dtypes=True), writes=[v1])
            self.V(lambda e: e.tensor_single_scalar(m1[:], v1[:], 0.0, op=ALU.is_ge), reads=[v1], writes=[m1])
            self.V(lambda e: e.scalar_tensor_tensor(out=selrow[:], in0=v1[:], scalar=64.0, in1=m1[:], op0=ALU.is_lt, op1=ALU.mult),
                   reads=[v1, m1], writes=[selrow])
            self.G(lambda e: e.iota(v1[:, 0:128], [[1, 128]], base=0, channel_multiplier=-1, allow_small_or_imprecise_dtypes=True), writes=[v1])
            self.V(lambda e: e.tensor_single_scalar(m1[:, 0:128], v1[:, 0:128], 0.0, op=ALU.is_equal), reads=[v1], writes=[m1])
            self.V(lambda e: e.scalar_tensor_tensor(out=selcol[:], in0=v1[:, 0:128], scalar=64.0, in1=m1[:, 0:128], op0=ALU.is_equal, op1=ALU.add),
                   reads=[v1, m1], writes=[selcol])
            self.S.barrier()
        self.pe_selrow = selrow
        posc = self.sb(st, "pe_posc", [128, 1024])
        for hb in range(2):
            ps = self.ps.next()
            self.mm(ps, ps[:, :], selcol[:, :], E[:, hb * 512:(hb + 1) * 512], reads=[selcol, E], start=True, stop=True)
            self.ev(posc[:, hb * 512:(hb + 1) * 512], ps[:, :], reads=[ps], partial=[posc])
        self.pe_posc = posc

    def prologue(self, layer):
        NT, LS = self.NT, self.LS
        X, HT = self.scr["X"], self.scr["HT"]
        HTv = HT.rearrange("(k p) t -> p k t", p=128)
        with ExitStack() as st:
            if layer == 0:
                self.posembed(st)
            xr = self.ring(st, f"p{layer}_x", [128, D], 2)
            xn = self.sb(st, f"p{layer}_xn", [128, D])
            hr = self.ring(st, f"p{layer}_h", [128, KD, 128], 2)
            ss = self.sb(st, f"p{layer}_ss", [128, 1])
            rs = self.sb(st, f"p{layer}_rs", [128, 1])
            import os
            PS_ = int(os.environ.get("PRO_STOP", "9"))
            for ti in range(NT // 128 if PS_ > 0 else 0):
                t0 = ti * 128
                c = 0 if t0 < LS else 1
                xt = xr.next()
                if layer == 0:
                    src = self.inp["xs"][t0:t0 + 128, :] if c == 0 else self.inp["xp"][t0 - LS:t0 - LS + 128, :]
                    self.load(xt, xt[:], src)
                    if c == 0:
                        posc, E, selrow = self.pe_posc, self.pe_E, self.pe_selrow
                        self.V(lambda e, xt=xt: e.tensor_tensor(out=xt[:, 1024:2048], in0=xt[:, 1024:2048], in1=posc[:], op=ALU.add),
                               reads=[xt, posc], writes=[xt])
                        for hb in range(2):
                            ps = self.ps.next()
                            self.mm(ps, ps[:, :], selrow[:, t0:t0 + 128], E[:, hb * 512:(hb + 1) * 512], reads=[selrow, E], start=True, stop=True)
                            self.V(lambda e, xt=xt, ps=ps, hb=hb: e.tensor_tensor(out=xt[:, hb * 512:(hb + 1) * 512], in0=xt[:, hb * 512:(hb + 1) * 512], in1=ps[:, :], op=ALU.add),
                                   reads=[xt, ps], writes=[xt])
                    self.store(xt, X[t0:t0 + 128, :], xt[:], dst_bufs=[self.dreg("X", ti)])
                else:
                    self.load(xt, xt[:], X[t0:t0 + 128, :], src_bufs=[self.dreg("X", ti)])
                if PS_ < 2:
                    continue
                self.A(lambda e, xt=xt: e.activation(out=xn[:], in_=xt[:], func=AF.Square, accum_out=ss[:, 0:1]), reads=[xt], writes=[xn, ss])
                self.V(lambda e: e.tensor_scalar(out=rs[:], in0=ss[:], scalar1=1.0 / D, scalar2=EPS, op0=ALU.mult, op1=ALU.add), reads=[ss], writes=[rs])
                self.A(lambda e: e.activation(out=rs[:], in_=rs[:], func=AF.Sqrt), reads=[rs], writes=[rs])
                self.V(lambda e: e.reciprocal(rs[:], rs[:]), reads=[rs], writes=[rs])
                self.V(lambda e, xt=xt: e.tensor_scalar(out=xn[:], in0=xt[:], scalar1=rs[:, 0:1], scalar2=None, op0=ALU.mult), reads=[xt, rs], writes=[xn])
                if PS_ < 3:
                    continue
                ht = hr.next()
                mu, sh = self.multT[layer], self.shiftT[layer]
                for kb in range(4):
                    ps = self.ps.next()
                    for j in range(4):
                        k = kb * 4 + j
                        self.tr(ps, ps[:, j * 128:(j + 1) * 128], xn[:, k * 128:(k + 1) * 128], reads=[xn])
                    for j in range(4):
                        k = kb * 4 + j
                        fn = (lambda e, ht=ht, ps=ps, k=k, j=j, c=c: e.tensor_scalar(
                            out=ht[:, k, :], in0=ps[:, j * 128:(j + 1) * 128], scalar1=mu[:, k, c:c + 1], scalar2=sh[:, k, c:c + 1],
                            op0=ALU.mult, op1=ALU.add))
                        EVM = os.environ.get("EVMODE", "2")
                        if EVM == "0":
                            self.ev(ht[:, k, :], ps[:, j * 128:(j + 1) * 128], reads=[ps], partial=[ht])
                        elif j % 2 or EVM == "2":
                            self.V(fn, reads=[ps, mu, sh], partial=[ht])
                        else:
                            fn2 = (lambda e, ht=ht, ps=ps, k=k, j=j, c=c: e.activation(
                                out=ht[:, k, :], in_=ps[:, j * 128:(j + 1) * 128], func=AF.Identity,
                                scale=mu[:, k, c:c + 1], bias=sh[:, k, c:c + 1]))
                            self.A(fn2, reads=[ps, mu, sh], partial=[ht])
                for kq in range(0, KD, 4):
                    self.store(ht, HTv[:, kq:kq + 4, t0:t0 + 128], ht[:, kq:kq + 4, :], partial_bufs=[self.dreg("HT", "all")])
            self.S.barrier()
            self.S.release([x.b for x in xr.items] + [h.b for h in hr.items])

    def lin(self, name, K, W_ap, blocks, TS, CB, src_loader, src_reads_key=None):
        NT = self.NT
        KC = K // 128
        Wv = W_ap.rearrange("(k p) n -> p k n", p=128)
        with ExitStack() as st:
            sr = self.ring(st, f"{name}_src", [128, KC, TS], 2)
            wr = self.ring(st, f"{name}_w", [128, KC, CB], 2)
            self.lin_st = st
            for si in range(NT // TS):
                t0 = si * TS
                src = sr.next()
                src_loader(si, t0, TS, src)
                for (c0, c1, mode, epi) in blocks:
                    for cb0 in range(c0, c1, CB):
                        ncb = min(CB, c1 - cb0)
                        wt = wr.next()
                        for kq in range(0, KC, 4):
                            self.load(wt, wt[:, kq:kq + 4, 0:ncb], Wv[:, kq:kq + 4, cb0:cb0 + ncb], partial=(kq > 0))
                        if mode == "F":
                            for m0 in range(0, ncb, 128):
                                mc = min(128, ncb - m0)
                                ps = self.ps.next()
                                for k in range(KC):
                                    self.mm(ps, ps[0:mc, 0:TS], wt[:, k, m0:m0 + mc], src[:, k, :], reads=[wt, src],
                                            start=(k == 0), stop=(k == KC - 1))
                                epi(ps, mc, cb0 + m0, t0, TS)
                        else:
                            for tt in range(TS // 128):
                                ps = self.ps.next()
                                for k in range(KC):
                                    self.mm(ps, ps[:, 0:ncb], src[:, k, tt * 128:(tt + 1) * 128], wt[:, k, 0:ncb], reads=[wt, src],
                                            start=(k == 0), stop=(k == KC - 1))
                                epi(ps, ncb, cb0, t0 + tt * 128)
            self.S.barrier()
            self.S.release([x.b for x in sr.items] + [w.b for w in wr.items])

    def inproj0(self):
        NT = self.NT
        w = self.din("e_w_in", [D, EVEN_IN])
        UT = self.dscr("UT", [1024, NT]); SGT = self.dscr("SGT", [1024, NT])
        QT = self.dscr("QT", [GDKW, NT]); KT = self.dscr("KT", [GDKW, NT])
        DLT = self.dscr("DLT", [32, NT])
        Ktm = self.dscr("Ktm", [NT, GDKW]); Vtm = self.dscr("Vtm", [NT, GDVW]); GGtm = self.dscr("GGtm", [NT, GDVW])
        HTv = self.scr["HT"].rearrange("(k p) t -> p k t", p=128)

        def loader(si, t0, TS, dst):
            for kq in range(0, KD, 4):
                self.load(dst, dst[:, kq:kq + 4, :], HTv[:, kq:kq + 4, t0:t0 + TS], src_bufs=[self.dreg("HT", "all")], partial=(kq > 0))

        fmap = [(0, 1024, UT, "UT"), (1024, 2048, SGT, "SGT"), (2048, 3584, QT, "QT"), (3584, 5120, KT, "KT"), (11264, 11296, DLT, "DLT")]
        tmap = [(3584, 5120, Ktm, "Ktm"), (5120, 8192, Vtm, "Vtm"), (8192, 11264, GGtm, "GGtm")]

        def epiF(ps, mc, col0, t0, TS):
            o = self.ip_or.next()
            self.ev(o[0:mc, 0:TS], ps[0:mc, 0:TS], reads=[ps], writes=[o])
            for (a, b, ten, nm) in fmap:
                if a <= col0 < b:
                    self.store(o, ten[col0 - a:col0 - a + mc, t0:t0 + TS], o[0:mc, 0:TS], partial_bufs=[self.dreg(nm, "all")])

        def epiT(ps, ncb, col0, tok0):
            o = self.ip_or.next()
            self.ev(o[:, 0:ncb], ps[:, 0:ncb], reads=[ps], writes=[o])
            for (a, b, ten, nm) in tmap:
                if a <= col0 < b:
                    self.store(o, ten[tok0:tok0 + 128, col0 - a:col0 - a + ncb], o[:, 0:ncb], partial_bufs=[self.dreg(nm, "all")])

        with ExitStack() as st:
            self.ip_or = self.ring(st, "ip0_o", [128, 512], 4)
            blocks = [(0, 5120, "F", epiF), (11264, 11296, "F", epiF), (3584, 11264, "T", epiT)]
            self.lin("ip0", D, w, blocks, 512, 512, loader)
            self.S.barrier()
            self.S.release([o.b for o in self.ip_or.items])


    def cplx_lambda_bar(self, st, lam, pfx):
        sh = [128, 32]
        dt = self.sb(st, pfx + "dt", sh); mag = self.sb(st, pfx + "mag", sh)
        a1 = self.sb(st, pfx + "a1", sh); a2 = self.sb(st, pfx + "a2", sh)
        sn = self.sb(st, pfx + "sn", sh); cs = self.sb(st, pfx + "cs", sh)
        abre = self.sb(st, pfx + "abre", sh); abim = self.sb(st, pfx + "abim", sh)
        self.A(lambda e: e.activation(out=dt[:], in_=lam[:, 2, :], func=AF.Exp), reads=[lam], writes=[dt])
        self.V(lambda e: e.tensor_tensor(out=mag[:], in0=lam[:, 0, :], in1=dt[:], op=ALU.mult), reads=[lam, dt], writes=[mag])
        self.A(lambda e: e.activation(out=mag[:], in_=mag[:], func=AF.Exp), reads=[mag], writes=[mag])
        self.V(lambda e: e.tensor_tensor(out=a1[:], in0=lam[:, 1, :], in1=dt[:], op=ALU.mult), reads=[lam, dt], writes=[a1])
        self.V(lambda e: e.tensor_scalar(out=a2[:], in0=a1[:], scalar1=math.pi / 2, scalar2=None, op0=ALU.add), reads=[a1], writes=[a2])
        self.rr_sin(st, sn, a1, sh, pfx + "r1")
        self.rr_sin(st, cs, a2, sh, pfx + "r2")
        self.V(lambda e: e.tensor_tensor(out=abre[:], in0=mag[:], in1=cs[:], op=ALU.mult), reads=[mag, cs], writes=[abre])
        self.V(lambda e: e.tensor_tensor(out=abim[:], in0=mag[:], in1=sn[:], op=ALU.mult), reads=[mag, sn], writes=[abim])
        return abre, abim

    def cmul(self, st, ar, ai, br, bi, pfx, shape):
        orr = self.sb(st, pfx + "re", shape); oi = self.sb(st, pfx + "im", shape); tm = self.sb(st, pfx + "tm", shape)
        self.V(lambda e: e.tensor_tensor(out=orr[:], in0=ar[:], in1=br[:], op=ALU.mult), reads=[ar, br], writes=[orr])
        self.V(lambda e: e.tensor_tensor(out=tm[:], in0=ai[:], in1=bi[:], op=ALU.mult), reads=[ai, bi], writes=[tm])
        self.V(lambda e: e.tensor_tensor(out=orr[:], in0=orr[:], in1=tm[:], op=ALU.subtract), reads=[orr, tm], writes=[orr])
        self.V(lambda e: e.tensor_tensor(out=oi[:], in0=ar[:], in1=bi[:], op=ALU.mult), reads=[ar, bi], writes=[oi])
        self.V(lambda e: e.tensor_tensor(out=tm[:], in0=ai[:], in1=br[:], op=ALU.mult), reads=[ai, br], writes=[tm])
        self.V(lambda e: e.tensor_tensor(out=oi[:], in0=oi[:], in1=tm[:], op=ALU.add), reads=[oi, tm], writes=[oi])
        return orr, oi

    def s5(self):
        NT, LS, NPR = self.NT, self.LS, self.NPR
        lamC = self.din("s5_lamC", [2, 128, 3, 32])
        bT = self.din("s5_bT", [2, 2, 32, 32, 128])
        cT = self.din("s5_cT", [2, 2, 128, 32, 32])
        h0C = self.din("s5_h0C", [2, 2, 128, 32])
        dTd = self.din("s5_dT", [32, 32])
        o_re = self.dout("new_s5_re", [NPR, 2, G_S5, P_S5]); o_im = self.dout("new_s5_im", [NPR, 2, G_S5, P_S5])
        UT = self.scr["UT"]; GYT = self.dscr("GYT", [1024, NT])
        nlev_s = int(math.log2(LS)); nlev_p = 8
        nlev = max(nlev_s, nlev_p)
        with ExitStack() as st:
            pw = []
            fco = []
            h0a = []
            cts = []
            for d in range(2):
                lam = self.sb(st, f"s5lam{d}", [128, 3, 32])
                self.load(lam, lam[:], lamC[d])
                abre, abim = self.cplx_lambda_bar(st, lam, f"s5lb{d}")
                sh = [128, 32]
                den = self.sb(st, f"s5den{d}", sh); t1 = self.sb(st, f"s5t1{d}", sh); t2 = self.sb(st, f"s5t2{d}", sh)
                fre = self.sb(st, f"s5fre{d}", sh); fim = self.sb(st, f"s5fim{d}", sh); nfim = self.sb(st, f"s5nfim{d}", sh)
                self.V(lambda e, lam=lam, den=den: e.tensor_tensor(out=den[:], in0=lam[:, 0, :], in1=lam[:, 0, :], op=ALU.mult), reads=[lam], writes=[den])
                self.V(lambda e, lam=lam, t2=t2: e.tensor_tensor(out=t2[:], in0=lam[:, 1, :], in1=lam[:, 1, :], op=ALU.mult), reads=[lam], writes=[t2])
                self.V(lambda e, den=den, t2=t2: e.tensor_tensor(out=den[:], in0=den[:], in1=t2[:], op=ALU.add), reads=[den, t2], writes=[den])
                self.V(lambda e, den=den: e.reciprocal(den[:], den[:]), reads=[den], writes=[den])
                self.V(lambda e, t1=t1, abre=abre: e.tensor_scalar(out=t1[:], in0=abre[:], scalar1=-1.0, scalar2=None, op0=ALU.add), reads=[abre], writes=[t1])
                self.V(lambda e, fre=fre, t1=t1, lam=lam: e.tensor_tensor(out=fre[:], in0=t1[:], in1=lam[:, 0, :], op=ALU.mult), reads=[t1, lam], writes=[fre])
                self.V(lambda e, t2=t2, abim=abim, lam=lam: e.tensor_tensor(out=t2[:], in0=abim[:], in1=lam[:, 1, :], op=ALU.mult), reads=[abim, lam], writes=[t2])
                self.V(lambda e, fre=fre, t2=t2: e.tensor_tensor(out=fre[:], in0=fre[:], in1=t2[:], op=ALU.add), reads=[fre, t2], writes=[fre])
                self.V(lambda e, fre=fre, den=den: e.tensor_tensor(out=fre[:], in0=fre[:], in1=den[:], op=ALU.mult), reads=[fre, den], writes=[fre])
                self.V(lambda e, fim=fim, abim=abim, lam=lam: e.tensor_tensor(out=fim[:], in0=abim[:], in1=lam[:, 0, :], op=ALU.mult), reads=[abim, lam], writes=[fim])
                self.V(lambda e, t2=t2, t1=t1, lam=lam: e.tensor_tensor(out=t2[:], in0=t1[:], in1=lam[:, 1, :], op=ALU.mult), reads=[t1, lam], writes=[t2])
                self.V(lambda e, fim=fim, t2=t2: e.tensor_tensor(out=fim[:], in0=fim[:], in1=t2[:], op=ALU.subtract), reads=[fim, t2], writes=[fim])
                self.V(lambda e, fim=fim, den=den: e.tensor_tensor(out=fim[:], in0=fim[:], in1=den[:], op=ALU.mult), reads=[fim, den], writes=[fim])
                self.V(lambda e, fim=fim, nfim=nfim: e.tensor_scalar(out=nfim[:], in0=fim[:], scalar1=-1.0, scalar2=None, op0=ALU.mult), reads=[fim], writes=[nfim])
                fco.append((fre, fim, nfim))
                h0 = self.sb(st, f"s5h0{d}", [128, 2, 32])
                self.load(h0, h0[:, 0, :], h0C[d, 0]); self.load(h0, h0[:, 1, :], h0C[d, 1], partial=True)
                h0r = self.sb(st, f"s5h0r{d}", sh); h0i = self.sb(st, f"s5h0i{d}", sh)
                self.V(lambda e, h0=h0, h0r=h0r: e.tensor_copy(h0r[:], h0[:, 0, :]), reads=[h0], writes=[h0r])
                self.V(lambda e, h0=h0, h0i=h0i: e.tensor_copy(h0i[:], h0[:, 1, :]), reads=[h0], writes=[h0i])
                h0a.append(self.cmul(st, abre, abim, h0r, h0i, f"s5h0a{d}", sh))
                lv = []
                cr, ci = abre, abim
                for l in range(nlev):
                    ni = self.sb(st, f"s5pwn{d}_{l}", sh)
                    self.V(lambda e, ni=ni, ci=ci: e.tensor_scalar(out=ni[:], in0=ci[:], scalar1=-1.0, scalar2=None, op0=ALU.mult), reads=[ci], writes=[ni])
                    lv.append((cr, ci, ni))
                    if l < nlev - 1:
                        cr, ci = self.cmul(st, cr, ci, cr, ci, f"s5pw{d}_{l}", sh)
                pw.append(lv)
                ctr = self.sb(st, f"s5ctr{d}", [128, 32, 32]); cti = self.sb(st, f"s5cti{d}", [128, 32, 32])
                self.load(ctr, ctr[:], cT[d, 0]); self.load(cti, cti[:], cT[d, 1])
                self.V(lambda e, cti=cti: e.tensor_scalar(out=cti[:], in0=cti[:], scalar1=-1.0, scalar2=None, op0=ALU.mult), reads=[cti], writes=[cti])
                cts.append((ctr, cti))
            dT = self.sb(st, "s5dT", [32, 32])
            self.load(dT, dT[:], dTd[:, :])
            LB = max(LS, NPR * LP)
            bufs = [[self.sb(st, f"s5b{a}{c}", [128, LB]) for c in range(2)] for a in range(2)]
            uT = self.sb(st, "s5uT", [32, NT]); yacc = self.sb(st, "s5yacc", [32, NT]); tg = self.sb(st, "s5tg", [32, NT])
            btr = self.ring(st, "s5bt", [32, 2, 2, 128], 2)
            fin = self.sb(st, "s5fin", [128, 32 * 2 * NPR * 2])
            groups = [(0, LS, 1, LS, nlev_s, True)] + ([(LS, NPR * LP, NPR, LP, nlev_p, False)] if NPR else [])
            for i in range(32):
                self.load(uT, uT[:], UT[32 * i:32 * i + 32, :], src_bufs=[self.dreg("UT", "all")])
                bt = btr.next()
                for d in range(2):
                    for c in range(2):
                        self.load(bt, bt[:, d, c, :], bT[d, c, :, i, :], partial=(d + c > 0))
                for (g0, glen, nseq, L, nl, is_sample) in groups:
                    for d in range(2):
                        fre, fim, nfim = fco[d]
                        cur = 0
                        A_re, A_im = bufs[cur]
                        for b0 in range(0, glen, 512):
                            bl = min(512, glen - b0)
                            p1 = self.ps.next(); p2 = self.ps.next()
                            self.mm(p1, p1[:, 0:bl], bt[:, d, 0, :], uT[:, g0 + b0:g0 + b0 + bl], reads=[bt, uT], start=True, stop=True)
                            self.mm(p2, p2[:, 0:bl], bt[:, d, 1, :], uT[:, g0 + b0:g0 + b0 + bl], reads=[bt, uT], start=True, stop=True)
                            self.V(lambda e, A_re=A_re, p1=p1, b0=b0, bl=bl, fre=fre, i=i: e.tensor_scalar(out=A_re[:, b0:b0 + bl], in0=p1[:, 0:bl], scalar1=fre[:, i:i + 1], scalar2=None, op0=ALU.mult),
                                   reads=[p1, fre], partial=[A_re])
                            self.V(lambda e, A_re=A_re, p2=p2, b0=b0, bl=bl, nfim=nfim, i=i: e.scalar_tensor_tensor(out=A_re[:, b0:b0 + bl], in0=p2[:, 0:bl], scalar=nfim[:, i:i + 1], in1=A_re[:, b0:b0 + bl], op0=ALU.mult, op1=ALU.add),
                                   reads=[p2, nfim, A_re], partial=[A_re])
                            self.V(lambda e, A_im=A_im, p2=p2, b0=b0, bl=bl, fre=fre, i=i: e.tensor_scalar(out=A_im[:, b0:b0 + bl], in0=p2[:, 0:bl], scalar1=fre[:, i:i + 1], scalar2=None, op0=ALU.mult),
                                   reads=[p2, fre], partial=[A_im])
                            self.V(lambda e, A_im=A_im, p1=p1, b0=b0, bl=bl, fim=fim, i=i: e.scalar_tensor_tensor(out=A_im[:, b0:b0 + bl], in0=p1[:, 0:bl], scalar=fim[:, i:i + 1], in1=A_im[:, b0:b0 + bl], op0=ALU.mult, op1=ALU.add),
                                   reads=[p1, fim, A_im], partial=[A_im])
                        if is_sample:
                            col = 0 if d == 0 else L - 1
                            for c in range(2):
                                tgt = (A_re, A_im)[c]; add = h0a[d][c]
                                self.V(lambda e, tgt=tgt, add=add, col=col, i=i: e.tensor_tensor(out=tgt[:, col:col + 1], in0=tgt[:, col:col + 1], in1=add[:, i:i + 1], op=ALU.add),
                                       reads=[tgt, add], writes=[tgt])
                        for l in range(nl):
                            s = 1 << l
                            cr, ci, nci = pw[d][l]
                            src_re, src_im = bufs[cur]; dst_re, dst_im = bufs[1 - cur]
                            def v3(t, a, b):
                                return t[:, 0:glen].rearrange("p (n l) -> p n l", l=L)[:, :, a:b]
                            if d == 0:
                                o_sl, sh_sl, keep = (s, L), (0, L - s), (0, s)
                            else:
                                o_sl, sh_sl, keep = (0, L - s), (s, L), (L - s, L)
                            self.A(lambda e, dst_re=dst_re, src_re=src_re, keep=keep: e.copy(v3(dst_re, *keep), v3(src_re, *keep)), reads=[src_re], partial=[dst_re])
                            self.A(lambda e, dst_im=dst_im, src_im=src_im, keep=keep: e.copy(v3(dst_im, *keep), v3(src_im, *keep)), reads=[src_im], partial=[dst_im])
                            self.V(lambda e, dst_re=dst_re, src_re=src_re, o_sl=o_sl, sh_sl=sh_sl, cr=cr, i=i: e.scalar_tensor_tensor(
                                out=v3(dst_re, *o_sl), in0=v3(src_re, *sh_sl), scalar=cr[:, i:i + 1], in1=v3(src_re, *o_sl), op0=ALU.mult, op1=ALU.add),
                                reads=[src_re, cr], partial=[dst_re])
                            self.V(lambda e, dst_re=dst_re, src_im=src_im, o_sl=o_sl, sh_sl=sh_sl, nci=nci, i=i: e.scalar_tensor_tensor(
                                out=v3(dst_re, *o_sl), in0=v3(src_im, *sh_sl), scalar=nci[:, i:i + 1], in1=v3(dst_re, *o_sl), op0=ALU.mult, op1=ALU.add),
                                reads=[src_im, nci, dst_re], partial=[dst_re])
                            self.V(lambda e, dst_im=dst_im, src_re=src_re, src_im=src_im, o_sl=o_sl, sh_sl=sh_sl, ci=ci, i=i: e.scalar_tensor_tensor(
                                out=v3(dst_im, *o_sl), in0=v3(src_re, *sh_sl), scalar=ci[:, i:i + 1], in1=v3(src_im, *o_sl), op0=ALU.mult, op1=ALU.add),
                                reads=[src_re, src_im, ci], partial=[dst_im])
                            self.V(lambda e, dst_im=dst_im, src_im=src_im, o_sl=o_sl, sh_sl=sh_sl, cr=cr, i=i: e.scalar_tensor_tensor(
                                out=v3(dst_im, *o_sl), in0=v3(src_im, *sh_sl), scalar=cr[:, i:i + 1], in1=v3(dst_im, *o_sl), op0=ALU.mult, op1=ALU.add),
                                reads=[src_im, cr, dst_im], partial=[dst_im])
                            cur = 1 - cur
                        H_re, H_im = bufs[cur]
                        if not is_sample:
                            for j in range(nseq):
                                col = j * L + (L - 1 if d == 0 else 0)
                                for c in range(2):
                                    idx = ((i * 2 + d) * NPR + j) * 2 + c
                                    src = (H_re, H_im)[c]
                                    self.V(lambda e, src=src, col=col, idx=idx: e.tensor_copy(fin[:, idx:idx + 1], src[:, col:col + 1]), reads=[src], partial=[fin])
                        ctr, ncti = cts[d]
                        for b0 in range(0, glen, 512):
                            bl = min(512, glen - b0)
                            p1 = self.ps.next()
                            self.mm(p1, p1[0:32, 0:bl], ctr[:, i, :], H_re[:, b0:b0 + bl], reads=[ctr, H_re], start=True, stop=False)
                            self.mm(p1, p1[0:32, 0:bl], ncti[:, i, :], H_im[:, b0:b0 + bl], reads=[ncti, H_im], start=False, stop=True)
                            if d == 0:
                                self.ev(yacc[:, g0 + b0:g0 + b0 + bl], p1[0:32, 0:bl], reads=[p1], partial=[yacc])
                            else:
                                self.V(lambda e, p1=p1, b0=b0, bl=bl, g0=g0: e.tensor_tensor(out=yacc[:, g0 + b0:g0 + b0 + bl], in0=yacc[:, g0 + b0:g0 + b0 + bl], in1=p1[0:32, 0:bl], op=ALU.add),
                                       reads=[p1, yacc], partial=[yacc])
                self.V(lambda e, i=i: e.scalar_tensor_tensor(out=yacc[:], in0=uT[:], scalar=dT[:, i:i + 1], in1=yacc[:], op0=ALU.mult, op1=ALU.add), reads=[uT, dT, yacc], writes=[yacc])
                self.A(lambda e: e.activation(out=tg[:], in_=yacc[:], func=AF.Square), reads=[yacc], writes=[tg])
                self.V(lambda e: e.tensor_scalar(out=tg[:], in0=tg[:], scalar1=0.044715, scalar2=1.0, op0=ALU.mult, op1=ALU.add), reads=[tg], writes=[tg])
                self.V(lambda e: e.tensor_tensor(out=tg[:], in0=tg[:], in1=yacc[:], op=ALU.mult), reads=[tg, yacc], writes=[tg])
                self.A(lambda e: e.activation(out=tg[:], in_=tg[:], func=AF.Sigmoid, scale=2.0 * math.sqrt(2.0 / math.pi)), reads=[tg], writes=[tg])
                self.V(lambda e: e.tensor_tensor(out=tg[:], in0=tg[:], in1=yacc[:], op=ALU.mult), reads=[tg, yacc], writes=[tg])
                self.store(tg, GYT[32 * i:32 * i + 32, :], tg[:], partial_bufs=[self.dreg("GYT", "all")])
            for i in range(32):
                for d in range(2):
                    for j in range(NPR):
                        for c in range(2):
                            idx = ((i * 2 + d) * NPR + j) * 2 + c
                            o = (o_re, o_im)[c]
                            self.store(fin, o[j, d, 2 * i:2 * i + 2, :].rearrange("g (p o) -> (g p) o", o=1), fin[:, idx:idx + 1])
            self.S.barrier()
            self.S.release([uT.b, tg.b, fin.b, dT.b] + [b.b for b in btr.items] + [x.b for pr in cts for x in pr])

    def glu(self):
        NT = self.NT
        w = self.din("s5_glu_w", [1024, 1024]); gb = self.din("s5_glu_bT", [128, 8])
        GYT, SGT = self.scr["GYT"], self.scr["SGT"]
        MIXT = self.dscr("MIXT", [4096, NT])
        GYv = GYT.rearrange("(k p) t -> p k t", p=128)

        def loader(si, t0, TS, dst):
            for kq in range(0, 8, 4):
                self.load(dst, dst[:, kq:kq + 4, :], GYv[:, kq:kq + 4, t0:t0 + TS], src_bufs=[self.dreg("GYT", "all")], partial=(kq > 0))

        with ExitStack() as st:
            gbt = self.sb(st, "glu_b", [128, 8])
            self.load(gbt, gbt[:], gb[:, :])
            orr = self.ring(st, "glu_o", [128, 512], 2); g1r = self.ring(st, "glu_g1", [128, 512], 2); g2r = self.ring(st, "glu_g2", [128, 512], 2)

            def epi(ps, mc, col0, t0, TS):
                o = orr.next(); g1 = g1r.next(); g2 = g2r.next()
                cb = col0 // 128
                self.A(lambda e: e.activation(out=o[:, 0:TS], in_=ps[:, 0:TS], func=AF.Sigmoid, bias=gbt[:, cb:cb + 1]), reads=[ps, gbt], writes=[o])
                self.load(g1, g1[:, 0:TS], GYT[col0:col0 + 128, t0:t0 + TS], src_bufs=[self.dreg("GYT", "all")])
                self.load(g2, g2[:, 0:TS], SGT[col0:col0 + 128, t0:t0 + TS], src_bufs=[self.dreg("SGT", "all")])
                self.A(lambda e: e.activation(out=g2[:, 0:TS], in_=g2[:, 0:TS], func=AF.Silu), reads=[g2], writes=[g2])
                self.V(lambda e: e.tensor_tensor(out=o[:, 0:TS], in0=o[:, 0:TS], in1=g1[:, 0:TS], op=ALU.mult), reads=[o, g1], writes=[o])
                self.V(lambda e: e.tensor_tensor(out=o[:, 0:TS], in0=o[:, 0:TS], in1=g2[:, 0:TS], op=ALU.mult), reads=[o, g2], writes=[o])
                self.store(o, MIXT[col0:col0 + 128, t0:t0 + TS], o[:, 0:TS], partial_bufs=[self.dreg("MIXT", "all")])

            self.lin("glu", 1024, w, [(0, 1024, "F", epi)], 512, 512, loader)
            self.S.barrier()
            self.S.release([gbt.b] + [x.b for r in (orr, g1r, g2r) for x in r.items])


    def final_norm(self):
        NT = self.NT
        fw = self.din("final_norm_w", [D])
        y = self.dout("y", [NT, D])
        X = self.scr["X"]
        with ExitStack() as st:
            fwb = self.sb(st, "fn_w", [128, D])
            self.load(fwb, fwb[:], fw.partition_broadcast(128))
            xr = self.ring(st, "fn_x", [128, D], 2)
            xn = self.sb(st, "fn_xn", [128, D])
            ss = self.sb(st, "fn_ss", [128, 1]); rs = self.sb(st, "fn_rs", [128, 1])
            for ti in range(NT // 128):
                t0 = ti * 128
                xt = xr.next()
                self.load(xt, xt[:], X[t0:t0 + 128, :], src_bufs=[self.dreg("X", ti)])
                self.A(lambda e: e.activation(out=xn[:], in_=xt[:], func=AF.Square, accum_out=ss[:, 0:1]), reads=[xt], writes=[xn, ss])
                self.V(lambda e: e.tensor_scalar(out=rs[:], in0=ss[:], scalar1=1.0 / D, scalar2=EPS, op0=ALU.mult, op1=ALU.add), reads=[ss], writes=[rs])
                self.A(lambda e: e.activation(out=rs[:], in_=rs[:], func=AF.Sqrt), reads=[rs], writes=[rs])
                self.V(lambda e: e.reciprocal(rs[:], rs[:]), reads=[rs], writes=[rs])
                self.V(lambda e: e.scalar_tensor_tensor(out=xt[:], in0=xt[:], scalar=rs[:, 0:1], in1=fwb[:], op0=ALU.mult, op1=ALU.mult), reads=[xt, rs, fwb], writes=[xt])
                self.store(xt, y[t0:t0 + 128, :], xt[:])
            self.S.barrier()
            self.S.release([fwb.b] + [x.b for x in xr.items])


    def gla(self):
        NT, LS, NPR = self.NT, self.LS, self.NPR
        dup_d = self.din("gla_dup", [2, 64, GDKW])
        nw_d = self.din("gla_nw", [GDVW])
        s0_d = self.din("gla_s0", [2, GH, GDK, GDV])
        o_gla = self.dout("new_gla", [NPR, 2, GH, GDK, GDV])
        QT, KT, DLT = self.scr["QT"], self.scr["KT"], self.scr["DLT"]
        Ktm, Vtm, GGtm = self.scr["Ktm"], self.scr["Vtm"], self.scr["GGtm"]
        OACC = self.dscr("OACC", [NT, GDVW])
        MIXT = self.scr["MIXT"]
        QTv = QT.rearrange("(k p) t -> p k t", p=128); KTv = KT.rearrange("(k p) t -> p k t", p=128)
        MXv = MIXT[1024:4096, :].rearrange("(k p) t -> p k t", p=128)
        with ExitStack() as st:
            dup = self.sb(st, "gl_dup", [64, 2, GDKW])
            for d in range(2):
                self.load(dup, dup[:, d, :], dup_d[d], partial=(d > 0))
            nw = self.sb(st, "gl_nw", [128, GDVW])
            self.load(nw, nw[:], nw_d.partition_broadcast(128))
            blk = self.sb(st, "gl_blk", [128, 128])
            self.V(lambda e: e.memset(blk[:], 0.0), writes=[blk])
            self.V(lambda e: e.memset(blk[0:64, 0:64], 1.0), writes=[blk])
            self.V(lambda e: e.memset(blk[64:128, 64:128], 1.0), writes=[blk])
            tri2 = [self.sb(st, f"gl_tri{d}", [128, 128]) for d in range(2)]
            for d in range(2):
                self.V(lambda e, d=d: e.tensor_tensor(out=tri2[d][:], in0=self.m_incl[d][:], in1=blk[:], op=ALU.mult), reads=[self.m_incl[d], blk], writes=[tri2[d]])
            dla_r = self.ring(st, "gl_dla", [64, 128], 2)
            for dla in dla_r.items:
                self.V(lambda e, dla=dla: e.memset(dla[:], 0.0), writes=[dla])
                self.V(lambda e, dla=dla: e.memset(dla[32:33, :], 1.0), writes=[dla])
            qT_r = self.ring(st, "gl_qT", [128, 12, 128], 2); kT_r = self.ring(st, "gl_kT", [128, 12, 128], 2)
            ktm_r = self.ring(st, "gl_ktm", [128, GDKW], 2)
            vtm = self.sb(st, "gl_vtm", [128, GDVW]); gg = self.sb(st, "gl_gg", [128, GDVW]); oacc = self.sb(st, "gl_oacc", [128, GDVW])
            la = self.sb(st, "gl_la", [128, GDKW]); bsb = self.sb(st, "gl_b", [128, GDKW]); kend = self.sb(st, "gl_kend", [128, GDKW])
            eb = self.sb(st, "gl_eb", [128, 12, 128]); qdec = self.sb(st, "gl_qdec", [128, 12, 128]); kinv = self.sb(st, "gl_kinv", [128, 12, 128])
            att = self.ring(st, "gl_att", [128, 64], 2)
            S = [self.sb(st, f"gl_S{h}", [128, 2, GDV]) for h in range(GH)]
            ssq = self.sb(st, "gl_ssq", [128, GH]); oT_r = self.ring(st, "gl_oT", [128, 4, 128], 2)
            for d in range(2):
                for (q0, L, kind) in self.seqs:
                    for h in range(GH):
                        if kind == 0:
                            self.load(S[h], S[h][:], s0_d[d, h].rearrange("(k p) v -> p k v", p=128))
                        else:
                            self.V(lambda e, h=h: e.memset(S[h][:], 0.0), writes=[S[h]])
                    nb = L // 128
                    order = range(nb) if d == 0 else range(nb - 1, -1, -1)
                    for bi in order:
                        t0 = q0 + bi * 128
                        dla = dla_r.next(); qT = qT_r.next(); kT = kT_r.next(); ktm = ktm_r.next()
                        self.load(dla, dla[0:32, :], DLT[:, t0:t0 + 128], src_bufs=[self.dreg("DLT", "all")], partial=True)
                        for kq in range(0, 12, 4):
                            self.load(qT, qT[:, kq:kq + 4, :], QTv[:, kq:kq + 4, t0:t0 + 128], src_bufs=[self.dreg("QT", "all")], partial=(kq > 0))
                            self.load(kT, kT[:, kq:kq + 4, :], KTv[:, kq:kq + 4, t0:t0 + 128], src_bufs=[self.dreg("KT", "all")], partial=(kq > 0))
                        self.load(ktm, ktm[:], Ktm[t0:t0 + 128, :], src_bufs=[self.dreg("Ktm", "all")])
                        self.load(vtm, vtm[:], Vtm[t0:t0 + 128, :], src_bufs=[self.dreg("Vtm", "all")])
                        if d == 1:
                            self.load(gg, gg[:], GGtm[t0:t0 + 128, :], src_bufs=[self.dreg("GGtm", "all")])
                            self.load(oacc, oacc[:], OACC[t0:t0 + 128, :], src_bufs=[self.dreg("OACC", t0 // 128)])
                        for c3 in range(3):
                            ps = self.ps.next()
                            self.mm(ps, ps[:, :], dla[:, :], dup[:, d, c3 * 512:(c3 + 1) * 512], reads=[dla, dup], start=True, stop=True)
                            sl = slice(c3 * 512, (c3 + 1) * 512)
                            self.A(lambda e, ps=ps, sl=sl: e.activation(out=la[:, sl], in_=ps[:, :], func=AF.Exp, scale=-1.0), reads=[ps], partial=[la])
                        self.A(lambda e: e.activation(out=la[:], in_=la[:], func=AF.Ln, bias=1.0), reads=[la], writes=[la])
                        self.V(lambda e: e.tensor_scalar(out=la[:], in0=la[:], scalar1=-1.0 / 16.0, scalar2=-1.0, op0=ALU.mult, op1=ALU.max), reads=[la], writes=[la])
                        for c3 in range(3):
                            sl = slice(c3 * 512, (c3 + 1) * 512)
                            p1 = self.ps.next(); p2 = self.ps.next()
                            self.mm(p1, p1[:, :], tri2[d][:, :], la[:, sl], reads=[tri2[d], la], start=True, stop=True)
                            self.mm(p2, p2[:, :], blk[:, :], la[:, sl], reads=[blk, la], start=True, stop=True)
                            self.A(lambda e, p1=p1, sl=sl: e.copy(bsb[:, sl], p1[:, :]), reads=[p1], partial=[bsb])
                            self.V(lambda e, p2=p2, sl=sl: e.tensor_tensor(out=kend[:, sl], in0=p2[:, :], in1=bsb[:, sl], op=ALU.subtract), reads=[p2, bsb], partial=[kend])
                        self.A(lambda e: e.activation(out=kend[:], in_=kend[:], func=AF.Exp), reads=[kend], writes=[kend])
                        self.V(lambda e, ktm=ktm: e.tensor_tensor(out=kend[:], in0=kend[:], in1=ktm[:], op=ALU.mult), reads=[kend, ktm], writes=[kend])
                        for k4 in range(3):
                            ps = self.ps.next()
                            for j in range(4):
                                kb = k4 * 4 + j
                                self.mm(ps, ps[:, j * 128:(j + 1) * 128], la[:, kb * 128:(kb + 1) * 128], tri2[d][:, :], reads=[la, tri2[d]], start=True, stop=True, first=(j == 0))
                            self.A(lambda e, ps=ps, k4=k4: e.activation(out=eb[:, k4 * 4:(k4 + 1) * 4, :], in_=ps[:, :].rearrange("p (k t) -> p k t", t=128), func=AF.Exp), reads=[ps], partial=[eb])
                            self.A(lambda e, ps=ps, k4=k4: e.activation(out=kinv[:, k4 * 4:(k4 + 1) * 4, :], in_=ps[:, :].rearrange("p (k t) -> p k t", t=128), func=AF.Exp, scale=-1.0), reads=[ps], partial=[kinv])
                        self.V(lambda e, qT=qT: e.scalar_tensor_tensor(out=qdec[:], in0=qT[:], scalar=1.0 / 16.0, in1=eb[:], op0=ALU.mult, op1=ALU.mult), reads=[qT, eb], writes=[qdec])
                        self.V(lambda e, kT=kT: e.tensor_tensor(out=kinv[:], in0=kinv[:], in1=kT[:], op=ALU.mult), reads=[kinv, kT], writes=[kinv])
                        for h in range(GH):
                            ps_a = self.ps.next(); ps_o = self.ps.next()
                            at = att.next()
                            for ci, c in enumerate((0, 1) if d == 0 else (1, 0)):
                                cs = slice(64 * c, 64 * c + 64)
                                col = 64 * c + (63 if d == 0 else 0)
                                for j in range(2):
                                    kb = 2 * h + j
                                    self.mm(ps_a, ps_a[cs, 0:64], kinv[:, kb, cs], qdec[:, kb, cs], reads=[kinv, qdec], start=(j == 0), stop=(j == 1), first=(j == 0 and ci == 0))
                                self.V(lambda e, ps_a=ps_a, at=at, cs=cs, d=d: e.tensor_tensor(out=at[cs, :], in0=ps_a[cs, 0:64], in1=self.m_incl[d][cs, cs], op=ALU.mult),
                                       reads=[ps_a, self.m_incl[d]], partial=[at])
                                self.mm(ps_o, ps_o[cs, :], at[cs, :], vtm[cs, h * GDV:(h + 1) * GDV], reads=[at, vtm], start=True, stop=False, first=(ci == 0))
                                for j in range(2):
                                    kb = 2 * h + j
                                    self.mm(ps_o, ps_o[cs, :], qdec[:, kb, cs], S[h][:, j, :], reads=[qdec, S[h]], start=False, stop=(j == 1))
                                for j in range(2):
                                    kb = 2 * h + j
                                    ps_s = self.ps.next()
                                    self.mm(ps_s, ps_s[:, :], kend[cs, kb * 128:(kb + 1) * 128], vtm[cs, h * GDV:(h + 1) * GDV], reads=[kend, vtm], start=True, stop=True)
                                    self.V(lambda e, ps_s=ps_s, h=h, j=j, kb=kb, col=col: e.scalar_tensor_tensor(out=S[h][:, j, :], in0=S[h][:, j, :], scalar=eb[:, kb, col:col + 1], in1=ps_s[:, :], op0=ALU.mult, op1=ALU.add),
                                           reads=[S[h], eb, ps_s], partial=[S[h]])
                            hs = slice(h * GDV, (h + 1) * GDV)
                            if d == 0:
                                self.ev(oacc[:, hs], ps_o[:, :], reads=[ps_o], partial=[oacc])
                            else:
                                self.V(lambda e, ps_o=ps_o, hs=hs: e.tensor_tensor(out=oacc[:, hs], in0=oacc[:, hs], in1=ps_o[:, :], op=ALU.add), reads=[ps_o, oacc], partial=[oacc])
                        if d == 0:
                            self.store(oacc, OACC[t0:t0 + 128, :], oacc[:], dst_bufs=[self.dreg("OACC", t0 // 128)])
                        else:
                            for h in range(GH):
                                hs = slice(h * GDV, (h + 1) * GDV)
                                self.A(lambda e, h=h, hs=hs: e.activation(out=la[:, 0:GDV], in_=oacc[:, hs], func=AF.Square, accum_out=ssq[:, h:h + 1]), reads=[oacc], partial=[la, ssq])
                            self.V(lambda e: e.tensor_scalar(out=ssq[:], in0=ssq[:], scalar1=1.0 / GDV, scalar2=EPS, op0=ALU.mult, op1=ALU.add), reads=[ssq, la], writes=[ssq])
                            self.A(lambda e: e.activation(out=ssq[:], in_=ssq[:], func=AF.Sqrt), reads=[ssq], writes=[ssq])
                            self.V(lambda e: e.reciprocal(ssq[:], ssq[:]), reads=[ssq], writes=[ssq])
                            self.A(lambda e: e.activation(out=gg[:], in_=gg[:], func=AF.Silu), reads=[gg], writes=[gg])
                            for h in range(GH):
                                hs = slice(h * GDV, (h + 1) * GDV)
                                self.V(lambda e, h=h, hs=hs: e.scalar_tensor_tensor(out=oacc[:, hs], in0=oacc[:, hs], scalar=ssq[:, h:h + 1], in1=nw[:, hs], op0=ALU.mult, op1=ALU.mult),
                                       reads=[oacc, ssq, nw], partial=[oacc])
                            self.V(lambda e: e.tensor_tensor(out=oacc[:], in0=oacc[:], in1=gg[:], op=ALU.mult), reads=[oacc, gg], writes=[oacc])
                            for k4 in range(6):
                                ps = self.ps.next(); oT = oT_r.next()
                                for j in range(4):
                                    kb = k4 * 4 + j
                                    self.tr(ps, ps[:, j * 128:(j + 1) * 128], oacc[:, kb * 128:(kb + 1) * 128], reads=[oacc])
                                self.ev(oT[:], ps[:, :].rearrange("p (k t) -> p k t", t=128), reads=[ps], writes=[oT])
                                self.store(oT, MXv[:, k4 * 4:(k4 + 1) * 4, t0:t0 + 128], oT[:], partial_bufs=[self.dreg("MIXT", "all")])
                    if kind == 1:
                        j = (q0 - LS) // LP
                        for h in range(GH):
                            self.store(S[h], o_gla[j, d, h].rearrange("(k p) v -> p k v", p=128), S[h][:])
            self.S.barrier()
            self.S.release([x.b for x in [dup, nw, vtm, gg, oacc] + S + dla_r.items + qT_r.items + kT_r.items + ktm_r.items + oT_r.items])

    def outproj(self, layer, K, w_name, srcname):
        NT, LS = self.NT, self.LS
        w = self.din(w_name, [K, D])
        X = self.scr["X"]
        SRCv = self.scr[srcname].rearrange("(k p) t -> p k t", p=128)
        KC = K // 128

        def loader(si, t0, TS, dst):
            for kq in range(0, KC, 4):
                self.load(dst, dst[:, kq:kq + 4, :], SRCv[:, kq:kq + 4, t0:t0 + TS], src_bufs=[self.dreg(srcname, "all")], partial=(kq > 0))

        with ExitStack() as st:
            gate = [self.sb(st, f"op{layer}_gate{c}", [128, D]) for c in range(2)]
            gT = self.gateT[layer]
            for c in range(2):
                for k4 in range(4):
                    ps = self.ps.next()
                    for j in range(4):
                        k = k4 * 4 + j
                        self.mm(ps, ps[:, j * 128:(j + 1) * 128], gT[:, k, c:c + 1].to_broadcast([128, 128]), self.ident[:, :], reads=[gT, self.ident], start=True, stop=True, first=(j == 0))
                    self.ev(gate[c][:, k4 * 512:(k4 + 1) * 512], ps[:, :], reads=[ps], partial=[gate[c]])
            xr = self.ring(st, f"op{layer}_x", [128, 512], 3)

            def epi(ps, ncb, col0, tok0):
                c = 0 if tok0 < LS else 1
                xt = xr.next()
                reg = self.dreg("X", (tok0 // 128, col0))
                self.load(xt, xt[:, 0:ncb], X[tok0:tok0 + 128, col0:col0 + ncb], src_bufs=[self.dreg("X", tok0 // 128), reg])
                self.V(lambda e: e.tensor_tensor(out=ps[:, 0:ncb], in0=ps[:, 0:ncb], in1=gate[c][:, col0:col0 + ncb], op=ALU.mult), reads=[gate[c]], writes=[ps])
                self.V(lambda e: e.tensor_tensor(out=xt[:, 0:ncb], in0=xt[:, 0:ncb], in1=ps[:, 0:ncb], op=ALU.add), reads=[ps, xt], writes=[xt])
                self.store(xt, X[tok0:tok0 + 128, col0:col0 + ncb], xt[:, 0:ncb], dst_bufs=[reg], partial_bufs=[self.dreg("X", tok0 // 128)])

            TS, CB = (256, 256) if K > 2048 else (512, 512)
            self.lin(f"op{layer}", K, w, [(0, D, "T", epi)], TS, CB, loader)
            self.S.barrier()
            self.S.release([x.b for x in xr.items])


    def inproj1(self):
        NT = self.NT
        w = self.din("o_w_in", [D, ODD_IN])
        muT = self.din("rwkv_muT", [2, 128, KD])
        HTv = self.scr["HT"].rearrange("(k p) t -> p k t", p=128)
        names = ["Rtm", "K1tm", "V1tm", "Gtm"]
        for nm in names:
            self.dscr(nm, [NT, D])
        WAT = self.dscr("WAT", [384, NT])
        starts = set(q0 for (q0, L, k) in self.seqs)
        ends = set(q0 + L for (q0, L, k) in self.seqs)
        with ExitStack() as st:
            mu = self.sb(st, "ip1_mu", [128, 2, KD]); c0 = self.sb(st, "ip1_c0", [128, KD])
            self.load(mu, mu[:, 0, :], muT[0]); self.load(mu, mu[:, 1, :], muT[1], partial=True)
            self.V(lambda e: e.tensor_tensor(out=c0[:], in0=mu[:, 0, :], in1=mu[:, 1, :], op=ALU.add), reads=[mu], writes=[c0])
            self.V(lambda e: e.tensor_scalar(out=c0[:], in0=c0[:], scalar1=-1.0, scalar2=1.0, op0=ALU.mult, op1=ALU.add), reads=[c0], writes=[c0])
            TS = 256
            hp = self.sb(st, "ip1_hp", [128, KD, TS]); hn = self.sb(st, "ip1_hn", [128, KD, TS])
            src_all = [self.dreg("HT", "all")]

            def loader(si, t0, TS, dst):
                for kq in range(0, KD, 4):
                    ks = slice(kq, kq + 4)
                    self.load(dst, dst[:, ks, :], HTv[:, ks, t0:t0 + TS], src_bufs=src_all, partial=(kq > 0))
                    if t0 > 0:
                        self.load(hp, hp[:, ks, :], HTv[:, ks, t0 - 1:t0 + TS - 1], src_bufs=src_all, partial=(kq > 0))
                    else:
                        self.load(hp, hp[:, ks, 1:TS], HTv[:, ks, 0:TS - 1], src_bufs=src_all, partial=(kq > 0))
                    if t0 + TS < NT:
                        self.load(hn, hn[:, ks, :], HTv[:, ks, t0 + 1:t0 + TS + 1], src_bufs=src_all, partial=(kq > 0))
                    else:
                        self.load(hn, hn[:, ks, 0:TS - 1], HTv[:, ks, t0 + 1:t0 + TS], src_bufs=src_all, partial=(kq > 0))
                for j in range(TS):
                    if (t0 + j) in starts:
                        self.V(lambda e, j=j: e.memset(hp[:, :, j:j + 1], 0.0), reads=[hp], writes=[hp])
                    if (t0 + j + 1) in ends:
                        self.V(lambda e, j=j: e.memset(hn[:, :, j:j + 1], 0.0), reads=[hn], writes=[hn])
                for k in range(KD):
                    self.V(lambda e, k=k: e.tensor_scalar(out=dst[:, k, :], in0=dst[:, k, :], scalar1=c0[:, k:k + 1], scalar2=None, op0=ALU.mult), reads=[dst, c0], writes=[dst])
                    self.V(lambda e, k=k: e.scalar_tensor_tensor(out=dst[:, k, :], in0=hp[:, k, :], scalar=mu[:, 0, k:k + 1], in1=dst[:, k, :], op0=ALU.mult, op1=ALU.add), reads=[hp, mu, dst], writes=[dst])
                    self.V(lambda e, k=k: e.scalar_tensor_tensor(out=dst[:, k, :], in0=hn[:, k, :], scalar=mu[:, 1, k:k + 1], in1=dst[:, k, :], op0=ALU.mult, op1=ALU.add), reads=[hn, mu, dst], writes=[dst])

            orr = self.ring(st, "ip1_o", [128, 512], 4)

            def epiT(ps, ncb, col0, tok0):
                o = orr.next()
                self.ev(o[:, 0:ncb], ps[:, 0:ncb], reads=[ps], writes=[o])
                nm = names[col0 // D]
                cc = col0 % D
                self.store(o, self.scr[nm][tok0:tok0 + 128, cc:cc + ncb], o[:, 0:ncb], partial_bufs=[self.dreg(nm, "all")])

            def epiF(ps, mc, col0, t0, TS):
                o = orr.next()
                self.ev(o[0:mc, 0:TS], ps[0:mc, 0:TS], reads=[ps], writes=[o])
                r0 = col0 - 4 * D
                self.store(o, WAT[r0:r0 + mc, t0:t0 + TS], o[0:mc, 0:TS], partial_bufs=[self.dreg("WAT", "all")])

            self.lin("ip1", D, w, [(0, 4 * D, "T", epiT), (4 * D, ODD_IN, "F", epiF)], TS, 512, loader)
            self.S.barrier()
            self.S.release([mu.b, hp.b, hn.b] + [o.b for o in orr.items])

    def rwkv_prep(self):
        NT = self.NT
        w2a = self.din("rwkv_w2a", [2, 97, D]); a2a = self.din("rwkv_a2a", [2, 97, D])
        kk_d = self.din("rwkv_k_k", [D]); ka_d = self.din("rwkv_k_a", [D]); rk_d = self.din("rwkv_r_k", [D])
        for nm in ["LW0", "LW1", "KD0", "KD1", "BV0", "BV1", "AV", "BONUS"]:
            self.dscr(nm, [NT, D])
        WAT = self.scr["WAT"]
        with ExitStack() as st:
            w2 = self.sb(st, "rp_w2", [97, 2, D]); a2 = self.sb(st, "rp_a2", [97, 2, D])
            for d in range(2):
                self.load(w2, w2[:, d, :], w2a[d], partial=(d > 0)); self.load(a2, a2[:, d, :], a2a[d], partial=(d > 0))
            kkb = self.sb(st, "rp_kk", [128, D]); kab = self.sb(st, "rp_ka", [128, D]); rkb = self.sb(st, "rp_rk", [128, D])
            self.load(kkb, kkb[:], kk_d.partition_broadcast(128)); self.load(kab, kab[:], ka_d.partition_broadcast(128)); self.load(rkb, rkb[:], rk_d.partition_broadcast(128))
            wl = [self.sb(st, f"rp_wl{d}", [97, 128]) for d in range(2)]; al = [self.sb(st, f"rp_al{d}", [97, 128]) for d in range(2)]
            for t in wl + al:
                self.V(lambda e, t=t: e.memset(t[96:97, :], 1.0), writes=[t])
            r = self.sb(st, "rp_r", [128, D]); k = self.sb(st, "rp_k", [128, D]); v = self.sb(st, "rp_v", [128, D])
            kkn = self.sb(st, "rp_kkn", [128, D]); t1 = self.sb(st, "rp_t1", [128, D]); t2 = self.sb(st, "rp_t2", [128, D]); t3 = self.sb(st, "rp_t3", [128, D])
            ssq = self.sb(st, "rp_ssq", [128, RH]); bs = self.sb(st, "rp_bs", [128, 2, RH])

            def h3(t):
                return t[:].rearrange("p (h j) -> p h j", j=RN)

            def bc(t):
                return t.unsqueeze(2).to_broadcast([128, RH, RN])

            for ti in range(NT // 128):
                t0 = ti * 128
                ts = slice(t0, t0 + 128)
                self.load(r, r[:], self.scr["Rtm"][ts, :], src_bufs=[self.dreg("Rtm", "all")])
                self.load(k, k[:], self.scr["K1tm"][ts, :], src_bufs=[self.dreg("K1tm", "all")])
                self.load(v, v[:], self.scr["V1tm"][ts, :], src_bufs=[self.dreg("V1tm", "all")])
                for d in range(2):
                    self.load(wl[d], wl[d][0:96, :], WAT[96 * d:96 * d + 96, ts], src_bufs=[self.dreg("WAT", "all")], partial=True)
                    self.load(al[d], al[d][0:96, :], WAT[192 + 96 * d:192 + 96 * d + 96, ts], src_bufs=[self.dreg("WAT", "all")], partial=True)
                    self.A(lambda e, d=d: e.activation(out=wl[d][0:96, :], in_=wl[d][0:96, :], func=AF.Tanh), reads=[wl[d]], partial=[wl[d]])
                self.V(lambda e: e.tensor_tensor(out=kkn[:], in0=k[:], in1=kkb[:], op=ALU.mult), reads=[k, kkb], writes=[kkn])
                self.A(lambda e: e.activation(out=t1[:], in_=kkn[:], func=AF.Square), reads=[kkn], writes=[t1])
                self.V(lambda e: e.tensor_reduce(out=ssq[:], in_=h3(t1), axis=AX.X, op=ALU.add), reads=[t1], writes=[ssq])
                self.A(lambda e: e.activation(out=ssq[:], in_=ssq[:], func=AF.Sqrt), reads=[ssq], writes=[ssq])
                self.V(lambda e: e.tensor_scalar(out=ssq[:], in0=ssq[:], scalar1=1e-12, scalar2=None, op0=ALU.max), reads=[ssq], writes=[ssq])
                self.V(lambda e: e.reciprocal(ssq[:], ssq[:]), reads=[ssq], writes=[ssq])
                self.V(lambda e: e.tensor_tensor(out=h3(kkn), in0=h3(kkn), in1=bc(ssq[:]), op=ALU.mult), reads=[kkn, ssq], writes=[kkn])
                self.V(lambda e: e.tensor_scalar(out=t1[:], in0=kkn[:], scalar1=-1.0, scalar2=None, op0=ALU.mult), reads=[kkn], writes=[t1])
                self.store(t1, self.scr["AV"][ts, :], t1[:], partial_bufs=[self.dreg("AV", "all")])
                self.V(lambda e: e.tensor_tensor(out=r[:], in0=r[:], in1=rkb[:], op=ALU.mult), reads=[r, rkb], writes=[r])
                for d in range(2):
                    pw_ = [self.ps.next() for _ in range(4)]
                    for q in range(4):
                        self.mm(pw_[q], pw_[q][:, :], wl[d][:, :], w2[:, d, q * 512:(q + 1) * 512], reads=[wl[d], w2], start=True, stop=True)
                        qs = slice(q * 512, (q + 1) * 512)
                        self.A(lambda e, q=q, qs=qs: e.activation(out=t2[:, qs], in_=pw_[q][:, :], func=AF.Exp, scale=-1.0), reads=[pw_[q]], partial=[t2])
                    self.A(lambda e: e.activation(out=t2[:], in_=t2[:], func=AF.Ln, bias=1.0), reads=[t2], writes=[t2])
                    self.A(lambda e: e.activation(out=t2[:], in_=t2[:], func=AF.Exp, scale=-1.0, bias=-0.5), reads=[t2], writes=[t2])
                    self.V(lambda e: e.tensor_scalar(out=t2[:], in0=t2[:], scalar1=-1.0, scalar2=None, op0=ALU.mult), reads=[t2], writes=[t2])
                    self.store(t2, self.scr[f"LW{d}"][ts, :], t2[:], partial_bufs=[self.dreg(f"LW{d}", "all")])
                    pa_ = [self.ps.next() for _ in range(4)]
                    for q in range(4):
                        self.mm(pa_[q], pa_[q][:, :], al[d][:, :], a2[:, d, q * 512:(q + 1) * 512], reads=[al[d], a2], start=True, stop=True)
                        qs = slice(q * 512, (q + 1) * 512)
                        self.A(lambda e, q=q, qs=qs: e.activation(out=t3[:, qs], in_=pa_[q][:, :], func=AF.Sigmoid), reads=[pa_[q]], partial=[t3])
                    self.V(lambda e: e.tensor_tensor(out=t1[:], in0=kkn[:], in1=t3[:], op=ALU.mult), reads=[kkn, t3], writes=[t1])
                    self.store(t1, self.scr[f"BV{d}"][ts, :], t1[:], partial_bufs=[self.dreg(f"BV{d}", "all")])
                    self.V(lambda e: e.scalar_tensor_tensor(out=t3[:], in0=t3[:], scalar=-1.0, in1=kab[:], op0=ALU.add, op1=ALU.mult), reads=[t3, kab], writes=[t3])
                    self.V(lambda e: e.scalar_tensor_tensor(out=t3[:], in0=t3[:], scalar=1.0, in1=k[:], op0=ALU.add, op1=ALU.mult), reads=[t3, k], writes=[t3])
                    self.store(t3, self.scr[f"KD{d}"][ts, :], t3[:], partial_bufs=[self.dreg(f"KD{d}", "all")])
                    self.V(lambda e: e.tensor_tensor(out=t2[:], in0=t3[:], in1=r[:], op=ALU.mult), reads=[t3, r], writes=[t2])
                    self.V(lambda e, d=d: e.tensor_reduce(out=bs[:, d, :], in_=h3(t2), axis=AX.X, op=ALU.add), reads=[t2], partial=[bs])
                self.V(lambda e: e.tensor_tensor(out=bs[:, 0, :], in0=bs[:, 0, :], in1=bs[:, 1, :], op=ALU.add), reads=[bs], writes=[bs])
                self.V(lambda e: e.tensor_tensor(out=h3(t1), in0=h3(v), in1=bc(bs[:, 0, :]), op=ALU.mult), reads=[v, bs], writes=[t1])
                self.store(t1, self.scr["BONUS"][ts, :], t1[:], partial_bufs=[self.dreg("BONUS", "all")])
            self.S.barrier()
            self.S.release([x.b for x in [w2, a2, kkb, kab, rkb, r, k, v, t1, t2, t3] + wl + al])


    def rwkv(self):
        NT, LS, NPR = self.NT, self.LS, self.NPR
        s0T = self.din("rwkv_s0T", [2, 16, 128, RN])
        lw_d = self.din("rwkv_lnx_w", [D]); lb_d = self.din("rwkv_lnx_b", [D])
        import os
        RW = int(os.environ.get("RW_STOP", "99"))
        o_st = self.dout("new_rwkv", [NPR, 2, RH, RN, RN])
        WACC = self.dscr("WACC", [NT, D]); MIX1T = self.dscr("MIX1T", [D, NT])
        M1v = MIX1T.rearrange("(k p) t -> p k t", p=128)
        sc = self.scr
        with ExitStack() as st:
            lnw = self.sb(st, "rw_lnw", [128, D]); lnb = self.sb(st, "rw_lnb", [128, D])
            self.load(lnw, lnw[:], lw_d.partition_broadcast(128)); self.load(lnb, lnb[:], lb_d.partition_broadcast(128))
            MK = [self.sb(st, f"rw_mk{d}", [128, 512]) for d in range(2)]
            for d in range(2):
                for q in range(4):
                    m = self.m_strict[d] if q % 2 == 0 else self.m_incl[d]
                    self.V(lambda e, d=d, q=q, m=m: e.tensor_copy(MK[d][:, q * 128:(q + 1) * 128], m[:]), reads=[m], partial=[MK[d]])
            lw = self.sb(st, "rw_lw", [128, D]); kd = self.sb(st, "rw_kd", [128, D]); bv = self.sb(st, "rw_bv", [128, D])
            r = self.sb(st, "rw_r", [128, D]); v = self.sb(st, "rw_v", [128, D]); av = self.sb(st, "rw_av", [128, D])
            bt = self.sb(st, "rw_bt", [128, D]); kt = self.sb(st, "rw_kt", [128, D])
            x1 = self.sb(st, "rw_x1", [128, D]); x2 = self.sb(st, "rw_x2", [128, D])
            AR = self.sb(st, "rw_AR", [128, 16, 2, 128]); BK = self.sb(st, "rw_BK", [128, 16, 2, 128])
            gCT = self.sb(st, "rw_gCT", [128, 16])
            ST = [self.sb(st, f"rw_ST{p}", [128, RN]) for p in range(16)]
            AMr = self.ring(st, "rw_AM", [128, 512], 2)
            Xr = self.ring(st, "rw_X", [128, 256], 6)
            Pr = self.ring(st, "rw_P", [128, 128], 2)
            PPr = self.ring(st, "rw_PP", [128, 256], 3)
            XRr = self.ring(st, "rw_XR", [128, 256], 2)
            BD32 = self.sb(st, "rw_bd32", [128, 128]); BD64 = self.sb(st, "rw_bd64", [128, 128])
            OFF64 = self.sb(st, "rw_off64", [128, 128]); OFF128 = self.sb(st, "rw_off128", [128, 128])
            self.V(lambda e: e.memset(BD32[:], 0.0), writes=[BD32]); self.V(lambda e: e.memset(BD64[:], 0.0), writes=[BD64])
            for i_ in range(4):
                self.V(lambda e, i_=i_: e.memset(BD32[32 * i_:32 * i_ + 32, 32 * i_:32 * i_ + 32], 1.0), writes=[BD32])
            for i_ in range(2):
                self.V(lambda e, i_=i_: e.memset(BD64[64 * i_:64 * i_ + 64, 64 * i_:64 * i_ + 64], 1.0), writes=[BD64])
            self.V(lambda e: e.tensor_tensor(out=OFF64[:], in0=BD64[:], in1=BD32[:], op=ALU.subtract), reads=[BD64, BD32], writes=[OFF64])
            self.V(lambda e: e.tensor_scalar(out=OFF128[:], in0=BD64[:], scalar1=-1.0, scalar2=1.0, op0=ALU.mult, op1=ALU.add), reads=[BD64], writes=[OFF128])
            WUr = self.ring(st, "rw_WU", [128, 128], 2)
            wkv = self.sb(st, "rw_wkv", [128, D])
            sq = self.sb(st, "rw_sq", [128, RH]); mean = self.sb(st, "rw_mean", [128, RH])
            oTr = self.ring(st, "rw_oT", [128, 4, 128], 2)
            stT = self.ring(st, "rw_stT", [64, 128], 2)

            def h3(t):
                return t[:].rearrange("p (h j) -> p h j", j=RN)

            def bc(a):
                return a.unsqueeze(2).to_broadcast([128, RH, RN])

            for d in range(2):
                for (q0, L, kind) in self.seqs:
                    for p in range(16):
                        if kind == 0:
                            self.load(ST[p], ST[p][:], s0T[d, p])
                        else:
                            self.V(lambda e, p=p: e.memset(ST[p][:], 0.0), writes=[ST[p]])
                    nb = L // 128
                    for bi in ((range(nb) if d == 0 else range(nb - 1, -1, -1)) if RW >= 0 else []):
                        t0 = q0 + bi * 128
                        ts = slice(t0, t0 + 128)
                        for (tb, nm) in [(lw, f"LW{d}"), (kd, f"KD{d}"), (bv, f"BV{d}"), (r, "Rtm"), (v, "V1tm"), (av, "AV")]:
                            self.load(tb, tb[:], sc[nm][ts, :], src_bufs=[self.dreg(nm, "all")])
                        if RW < 1:
                            continue
                        cps = [self.ps.next() for _ in range(4)]
                        tps = [self.ps.next() for _ in range(4)]
                        for q in range(4):
                            qs = slice(q * 512, (q + 1) * 512)
                            self.mm(cps[q], cps[q][:, :], self.m_incl[d][:, :], lw[:, qs], reads=[self.m_incl[d], lw], start=True, stop=True)
                            self.mm(tps[q], tps[q][:, :], self.ones[:, :], lw[:, qs], reads=[self.ones, lw], start=True, stop=True)
                        RS = int(os.environ.get("RW_SUB", "9"))
                        if RS < 1:
                            continue
                        for q in range(4):
                            qs = slice(q * 512, (q + 1) * 512)
                            self.A(lambda e, q=q, qs=qs: e.activation(out=x1[:, qs], in_=cps[q][:, :], func=AF.Exp), reads=[cps[q]], partial=[x1])
                            self.V(lambda e, q=q, qs=qs: e.tensor_tensor(out=x2[:, qs], in0=cps[q][:, :], in1=lw[:, qs], op=ALU.subtract), reads=[cps[q], lw, x1], partial=[x2])
                        if RS < 2:
                            continue
                        self.V(lambda e: e.tensor_tensor(out=r[:], in0=r[:], in1=x1[:], op=ALU.mult), reads=[r, x1], writes=[r])
                        self.A(lambda e: e.activation(out=x2[:], in_=x2[:], func=AF.Exp), reads=[x2], writes=[x2])
                        self.V(lambda e: e.tensor_tensor(out=av[:], in0=av[:], in1=x2[:], op=ALU.mult), reads=[av, x2], writes=[av])
                        for q in range(4):
                            qs = slice(q * 512, (q + 1) * 512)
                            self.A(lambda e, q=q, qs=qs: e.activation(out=x1[:, qs], in_=cps[q][:, :], func=AF.Exp, scale=-1.0), reads=[cps[q], r], partial=[x1])
                            self.V(lambda e, q=q, qs=qs: e.tensor_copy(x2[:, qs], cps[q][:, :]), reads=[cps[q], av, x1], partial=[x2])
                        for q in range(4):
                            qs = slice(q * 512, (q + 1) * 512)
                            self.V(lambda e, q=q, qs=qs: e.tensor_tensor(out=x2[:, qs], in0=tps[q][:, :], in1=x2[:, qs], op=ALU.subtract), reads=[tps[q], x2], partial=[x2])
                        if RS < 3:
                            continue
                        self.V(lambda e: e.tensor_tensor(out=bt[:], in0=bv[:], in1=x1[:], op=ALU.mult), reads=[bv, x1], writes=[bt])
                        self.V(lambda e: e.tensor_tensor(out=kt[:], in0=kd[:], in1=x1[:], op=ALU.mult), reads=[kd, x1], writes=[kt])
                        self.A(lambda e: e.activation(out=x2[:], in_=x2[:], func=AF.Exp), reads=[x2], writes=[x2])
                        self.V(lambda e: e.tensor_tensor(out=bv[:], in0=bv[:], in1=x2[:], op=ALU.mult), reads=[bv, x2, bt], writes=[bv])
                        self.V(lambda e: e.tensor_tensor(out=kd[:], in0=kd[:], in1=x2[:], op=ALU.mult), reads=[kd, x2, kt], writes=[kd])
                        if RW == 1:
                            continue
                        gps = self.ps.next()
                        for p in range(16):
                            self.mm(gps, gps[:, p:p + 1], lw[:, p * 128:(p + 1) * 128], self.ones[:, 0:1], reads=[lw, self.ones], start=True, stop=True, first=(p == 0))
                        self.A(lambda e, gps=gps: e.activation(out=gCT[:], in_=gps[:, 0:16], func=AF.Exp), reads=[gps], writes=[gCT])
                        if RW <= 2:
                            continue
                        for (src, dst, idx) in [(av, AR, 0), (r, AR, 1), (bt, BK, 0), (kt, BK, 1)]:
                            for p4 in range(4):
                                ps = self.ps.next()
                                for j in range(4):
                                    p = p4 * 4 + j
                                    self.tr(ps, ps[:, j * 128:(j + 1) * 128], src[:, p * 128:(p + 1) * 128], reads=[src])
                                self.ev(dst[:, p4 * 4:(p4 + 1) * 4, idx, :], ps[:, :].rearrange("p (k t) -> p k t", t=128), reads=[ps], partial=[dst])
                        psY = None
                        full_ring = self.ps
                        self.ps = Ring(full_ring.items[:6])
                        yring = Ring(full_ring.items[6:8])
                        for h in range(RH if RW >= 4 else 0):
                            p, hf = h // 2, h % 2
                            rows = slice(64 * hf, 64 * hf + 64)
                            hc = slice(h * RN, (h + 1) * RN)
                            ps1 = self.ps.next(); ps2 = self.ps.next()
                            arv = AR[rows, p, :, :].rearrange("p a t -> p (a t)")
                            self.mm(ps1, ps1[:, 0:256], BK[rows, p, 0, :], arv, reads=[BK, AR], start=True, stop=True)
                            self.mm(ps1, ps1[:, 256:512], BK[rows, p, 1, :], arv, reads=[BK, AR], start=True, stop=True, first=False)
                            self.mm(ps2, ps2[:, 0:128], AR[rows, p, 0, :], BK[rows, p, 0, :], reads=[BK, AR], start=True, stop=True)
                            AM = AMr.next(); XR = XRr.next(); XX = Xr.next(); PP = PPr.next()
                            self.V(lambda e, ps1=ps1, AM=AM, d=d: e.tensor_tensor(out=AM[:], in0=ps1[:, :], in1=MK[d][:], op=ALU.mult), reads=[ps1, MK[d]], writes=[AM])
                            self.V(lambda e, XR=XR, AM=AM: e.tensor_copy(XR[:, 0:128], AM[:, 0:128]), reads=[AM], partial=[XR])
                            self.V(lambda e, ps2=ps2, XR=XR, d=d: e.tensor_tensor(out=XR[:, 128:256], in0=ps2[:, 0:128], in1=self.m_strict[1 - d][:], op=ALU.mult), reads=[ps2, self.m_strict[1 - d]], partial=[XR])
                            self.V(lambda e, XX=XX, XR=XR: e.tensor_tensor(out=XX[:].rearrange("p (a t) -> p a t", a=2), in0=XR[:].rearrange("p (a t) -> p a t", a=2),
                                                                              in1=BD32[:].unsqueeze(1).to_broadcast([128, 2, 128]), op=ALU.mult), reads=[XR, BD32], writes=[XX])
                            self.V(lambda e, XX=XX, PP=PP: e.tensor_tensor(out=PP[:].rearrange("p (a t) -> p a t", a=2), in0=XX[:].rearrange("p (a t) -> p a t", a=2),
                                                                              in1=self.ident[:].unsqueeze(1).to_broadcast([128, 2, 128]), op=ALU.add), reads=[XX, self.ident], writes=[PP])
                            for l in range(1, 5):
                                ps3 = self.ps.next()
                                XN = Xr.next()
                                if l < 4:
                                    self.mm(ps3, ps3[:, 0:128], XX[:, 128:256], XX[:, 0:128], reads=[XX], start=True, stop=True)
                                    self.mm(ps3, ps3[:, 128:256], XX[:, 0:128], XX[:, 128:256], reads=[XX], start=True, stop=True, first=False)
                                    self.A(lambda e, ps3=ps3, XN=XN: e.copy(XN[:], ps3[:, 0:256]), reads=[ps3], writes=[XN])
                                else:
                                    self.mm(ps3, ps3[:, 128:256], XX[:, 0:128], XX[:, 128:256], reads=[XX], start=True, stop=True)
                                    self.A(lambda e, ps3=ps3, XN=XN: e.copy(XN[:, 128:256], ps3[:, 128:256]), reads=[ps3], writes=[XN])
                                self.mm(ps3, ps3[:, 256:384], XN[:, 128:256], PP[:, 0:128], reads=[XN, PP], start=True, stop=True, first=False)
                                self.mm(ps3, ps3[:, 384:512], PP[:, 0:128], XN[:, 128:256], reads=[XN, PP], start=True, stop=True, first=False)
                                PN = PPr.next()
                                self.V(lambda e, ps3=ps3, PP=PP, PN=PN: e.tensor_tensor(out=PN[:], in0=PP[:], in1=ps3[:, 256:512], op=ALU.add), reads=[ps3, PP], writes=[PN])
                                XX, PP = XN, PN
                            NO = Xr.next(); TT = Xr.next()
                            self.V(lambda e, NO=NO, XR=XR: e.tensor_tensor(out=NO[:].rearrange("p (a t) -> p a t", a=2), in0=XR[:].rearrange("p (a t) -> p a t", a=2),
                                                                              in1=OFF64[:].unsqueeze(1).to_broadcast([128, 2, 128]), op=ALU.mult), reads=[XR, OFF64], writes=[NO])
                            ps3 = self.ps.next()
                            self.mm(ps3, ps3[:, 0:128], NO[:, 128:256], PP[:, 0:128], reads=[NO, PP], start=True, stop=True)
                            self.mm(ps3, ps3[:, 128:256], NO[:, 0:128], PP[:, 128:256], reads=[NO, PP], start=True, stop=True, first=False)
                            self.A(lambda e, ps3=ps3, TT=TT: e.copy(TT[:], ps3[:, 0:256]), reads=[ps3], writes=[TT])
                            self.mm(ps3, ps3[:, 256:384], PP[:, 128:256], TT[:, 0:128], reads=[TT, PP], start=True, stop=True, first=False)
                            self.mm(ps3, ps3[:, 384:512], PP[:, 0:128], TT[:, 128:256], reads=[TT, PP], start=True, stop=True, first=False)
                            PN = PPr.next()
                            self.V(lambda e, ps3=ps3, PP=PP, PN=PN: e.tensor_tensor(out=PN[:], in0=PP[:], in1=ps3[:, 256:512], op=ALU.add), reads=[ps3, PP], writes=[PN])
                            PP = PN
                            NO = Xr.next(); TT = Xr.next()
                            self.V(lambda e, NO=NO, XR=XR: e.tensor_tensor(out=NO[:, 128:256], in0=XR[:, 128:256], in1=OFF128[:], op=ALU.mult), reads=[XR, OFF128], writes=[NO])
                            ps3 = self.ps.next()
                            self.mm(ps3, ps3[:, 0:128], NO[:, 128:256], PP[:, 0:128], reads=[NO, PP], start=True, stop=True)
                            self.A(lambda e, ps3=ps3, TT=TT: e.copy(TT[:, 0:128], ps3[:, 0:128]), reads=[ps3], writes=[TT])
                            self.mm(ps3, ps3[:, 128:256], PP[:, 128:256], TT[:, 0:128], reads=[TT, PP], start=True, stop=True, first=False)
                            P = Pr.next()
                            self.V(lambda e, ps3=ps3, PP=PP, P=P: e.tensor_tensor(out=P[:], in0=PP[:, 0:128], in1=ps3[:, 128:256], op=ALU.add), reads=[ps3, PP], writes=[P])
                            ps5 = self.ps.next(); WU = WUr.next()
                            self.mm(ps5, ps5[:, 0:64], AR[rows, p, 0, :], ST[p][rows, :], reads=[AR, ST[p]], start=True, stop=False)
                            self.mm(ps5, ps5[:, 0:64], AM[:, 256:384], v[:, hc], reads=[AM, v], start=False, stop=True)
                            self.A(lambda e, ps5=ps5, WU=WU: e.copy(WU[:, 0:64], ps5[:, 0:64]), reads=[ps5], partial=[WU])
                            self.mm(ps5, ps5[:, 64:128], P[:, :], WU[:, 0:64], reads=[P, WU], start=True, stop=True, first=False)
                            self.A(lambda e, ps5=ps5, WU=WU: e.copy(WU[:, 64:128], ps5[:, 64:128]), reads=[ps5], partial=[WU])
                            if h % 8 == 0:
                                psY = yring.next()
                            yc = slice((h % 8) * 64, (h % 8) * 64 + 64)
                            self.mm(psY, psY[:, yc], AR[rows, p, 1, :], ST[p][rows, :], reads=[AR, ST[p]], start=True, stop=False, first=(h % 8 == 0))
                            self.mm(psY, psY[:, yc], AM[:, 128:256], WU[:, 64:128], reads=[AM, WU], start=False, stop=False)
                            self.mm(psY, psY[:, yc], AM[:, 384:512], v[:, hc], reads=[AM, v], start=False, stop=True)
                            ps6 = self.ps.next()
                            self.mm(ps6, ps6[rows, 0:64], bv[:, hc], WU[:, 64:128], reads=[bv, WU], start=True, stop=False)
                            self.mm(ps6, ps6[rows, 0:64], kd[:, hc], v[:, hc], reads=[kd, v], start=False, stop=True)
                            self.V(lambda e, ps6=ps6, p=p, rows=rows: e.scalar_tensor_tensor(out=ST[p][rows, :], in0=ST[p][rows, :], scalar=gCT[rows, p:p + 1], in1=ps6[rows, 0:64], op0=ALU.mult, op1=ALU.add),
                                   reads=[ST[p], gCT, ps6], partial=[ST[p]])
                            if h % 8 == 7:
                                q = h // 8
                                qs = slice(q * 512, (q + 1) * 512)
                                self.ev(wkv[:, qs], psY[:, :], reads=[psY], partial=[wkv])
                        self.ps = full_ring
                        if d == 0:
                            self.store(wkv, WACC[ts, :], wkv[:], dst_bufs=[self.dreg("WACC", t0 // 128)])
                        else:
                            self.load(x1, x1[:], WACC[ts, :], src_bufs=[self.dreg("WACC", t0 // 128)])
                            self.load(x2, x2[:], sc["BONUS"][ts, :], src_bufs=[self.dreg("BONUS", "all")])
                            self.load(lw, lw[:], sc["Gtm"][ts, :], src_bufs=[self.dreg("Gtm", "all")])
                            self.V(lambda e: e.tensor_tensor(out=wkv[:], in0=wkv[:], in1=x1[:], op=ALU.add), reads=[wkv, x1], writes=[wkv])
                            self.V(lambda e: e.tensor_reduce(out=mean[:], in_=h3(wkv), axis=AX.X, op=ALU.add), reads=[wkv], writes=[mean])
                            self.V(lambda e: e.tensor_scalar(out=mean[:], in0=mean[:], scalar1=1.0 / RN, scalar2=None, op0=ALU.mult), reads=[mean], writes=[mean])
                            self.V(lambda e: e.tensor_tensor(out=h3(wkv), in0=h3(wkv), in1=bc(mean[:]), op=ALU.subtract), reads=[wkv, mean], writes=[wkv])
                            self.A(lambda e: e.activation(out=x1[:], in_=wkv[:], func=AF.Square), reads=[wkv], writes=[x1])
                            self.V(lambda e: e.tensor_reduce(out=sq[:], in_=h3(x1), axis=AX.X, op=ALU.add), reads=[x1], writes=[sq])
                            self.V(lambda e: e.tensor_scalar(out=sq[:], in0=sq[:], scalar1=1.0 / RN, scalar2=LNX_EPS, op0=ALU.mult, op1=ALU.add), reads=[sq], writes=[sq])
                            self.A(lambda e: e.activation(out=sq[:], in_=sq[:], func=AF.Sqrt), reads=[sq], writes=[sq])
                            self.V(lambda e: e.reciprocal(sq[:], sq[:]), reads=[sq], writes=[sq])
                            self.V(lambda e: e.tensor_tensor(out=h3(wkv), in0=h3(wkv), in1=bc(sq[:]), op=ALU.mult), reads=[wkv, sq], writes=[wkv])
                            self.V(lambda e: e.tensor_tensor(out=wkv[:], in0=wkv[:], in1=lnw[:], op=ALU.mult), reads=[wkv, lnw], writes=[wkv])
                            self.V(lambda e: e.tensor_tensor(out=wkv[:], in0=wkv[:], in1=lnb[:], op=ALU.add), reads=[wkv, lnb], writes=[wkv])
                            self.V(lambda e: e.tensor_tensor(out=wkv[:], in0=wkv[:], in1=x2[:], op=ALU.add), reads=[wkv, x2], writes=[wkv])
                            self.A(lambda e: e.activation(out=lw[:], in_=lw[:], func=AF.Silu), reads=[lw], writes=[lw])
                            self.V(lambda e: e.tensor_tensor(out=wkv[:], in0=wkv[:], in1=lw[:], op=ALU.mult), reads=[wkv, lw], writes=[wkv])
                            for k4 in range(4):
                                ps = self.ps.next(); oT = oTr.next()
                                for j in range(4):
                                    kb = k4 * 4 + j
                                    self.tr(ps, ps[:, j * 128:(j + 1) * 128], wkv[:, kb * 128:(kb + 1) * 128], reads=[wkv])
                                self.ev(oT[:], ps[:, :].rearrange("p (k t) -> p k t", t=128), reads=[ps], writes=[oT])
                                self.store(oT, M1v[:, k4 * 4:(k4 + 1) * 4, ts], oT[:], partial_bufs=[self.dreg("MIX1T", "all")])
                    if kind == 1 and RW != -2:
                        jq = (q0 - LS) // LP
                        for p in range(16):
                            ps = self.ps.next(); sT = stT.next()
                            self.tr(ps, ps[0:64, 0:128], ST[p][:, :], reads=[ST[p]])
                            self.ev(sT[:, :], ps[0:64, 0:128], reads=[ps], writes=[sT])
                            for hf in range(2):
                                self.store(sT, o_st[jq, d, 2 * p + hf], sT[:, hf * 64:(hf + 1) * 64])
            self.S.barrier()
            self.S.release([x.b for x in [lnw, lnb, lw, kd, bv, r, v, av, wkv, x1, x2] + ST + oTr.items + stT.items])


def build_program(LS, NPR, upto="all", debug=()):
    P = Prog(LS, NPR, debug)
    nc = P.nc
    NT = P.NT
    P.din("xs", [LS, D]); P.din("xp", [NPR * LP, D])
    P.dscr("X", [NT, D]); P.dscr("HT", [D, NT])
    if upto == "rwonly":
        with ExitStack() as pst:
            P.pst = pst
            P.setup(pst)
            for nm in ["LW0", "LW1", "KD0", "KD1", "BV0", "BV1", "AV", "BONUS", "Rtm", "V1tm", "Gtm"]:
                P.scr[nm] = P.din(nm, [NT, D])
            P.rwkv()
            with nc.Block() as block:
                P.S.finish(block)
        return P
    with ExitStack() as pst:
        P.pst = pst
        P.setup(pst)
        P.adaln()
        if upto != "ada":
            P.prologue(0)
        if upto not in ("p0", "ada"):
            P.inproj0()
        if upto not in ("p0", "ada", "ip0"):
            P.s5()
            P.glu()
        if upto not in ("p0", "ada", "ip0", "s5"):
            P.gla()
            P.outproj(0, 4096, "e_w_out", "MIXT")
        if upto not in ("p0", "ada", "ip0", "s5", "l0"):
            P.prologue(1)
            P.inproj1()
            P.rwkv_prep()
        if upto not in ("p0", "ada", "ip0", "s5", "l0", "prep"):
            P.rwkv()
            P.outproj(1, 2048, "o_w_out", "MIX1T")
        if upto == "all":
            P.final_norm()
        if upto in ("ada", "p0"):
            dbg = P.dout("dbg_ada", [128, 6 * KD * 2])
            for i, t in enumerate([P.shiftT[0], P.multT[0], P.gateT[0], P.shiftT[1], P.multT[1], P.gateT[1]]):
                P.store(t, dbg[:, i * 32:(i + 1) * 32], t[:].rearrange("p k c -> p (k c)"))
        with nc.Block() as block:
            P.S.finish(block)
    return P


def _f32(a):
    return np.ascontiguousarray(np.asarray(a, dtype=np.float32))


def prep_core_inputs(inp, b, prompt_ids, names):
    out = {}

    def fm(v, nchunk):
        return _f32(np.asarray(v).reshape(nchunk, 128).T)

    for n in names:
        if n == "xs":
            out[n] = _f32(inp["x_sample"][b])
        elif n == "xp":
            out[n] = _f32(np.concatenate([inp["x_prompt"][j] for j in prompt_ids], axis=0))
        elif n == "condT":
            c2 = np.stack([np.asarray(inp["c"][b]), np.asarray(inp["c_ctx"])], axis=-1)
            out[n] = _f32(c2.reshape(KD, 128, 2).transpose(1, 0, 2))
        elif n == "ada_w":
            out[n] = _f32(inp["ada_w"])
        elif n == "ada_bT":
            out[n] = _f32(np.stack([fm(inp["ada_b"][l], 48) for l in range(2)]))
        elif n == "norm_wT":
            out[n] = _f32(np.stack([fm(inp["norm_w"][l], KD) for l in range(2)]))
        elif n == "e_w_in":
            out[n] = _f32(inp["e_w_in"][0])
        elif n == "s5_lamC":
            o = np.zeros((2, 128, 3, 32), np.float32)
            for d in range(2):
                for ci, key in enumerate(["s5_lambda_re", "s5_lambda_im"]):
                    a = np.asarray(inp[key][0, d])
                    o[d, :, ci, :] = a.reshape(32, 128).T
                ls = np.repeat(np.asarray(inp["s5_log_step"][0, d])[:, None], P_S5, axis=1)
                o[d, :, 2, :] = ls.reshape(32, 128).T
            out[n] = o
        elif n == "s5_bT":
            o = np.zeros((2, 2, 32, 32, 128), np.float32)
            for d in range(2):
                for ci, key in enumerate(["s5_b_re", "s5_b_im"]):
                    a = np.asarray(inp[key][0, d])
                    for gl in range(2):
                        blk = a[gl::2]
                        o[d, ci, gl * 16:(gl + 1) * 16, :, gl * 64:(gl + 1) * 64] = blk.transpose(2, 0, 1)
            out[n] = o
        elif n == "s5_cT":
            o = np.zeros((2, 2, 128, 32, 32), np.float32)
            for d in range(2):
                for ci, key in enumerate(["s5_c_re", "s5_c_im"]):
                    a = np.asarray(inp[key][0, d])
                    for gl in range(2):
                        blk = a[gl::2]
                        o[d, ci, gl * 64:(gl + 1) * 64, :, gl * 16:(gl + 1) * 16] = blk.transpose(2, 0, 1)
            out[n] = o
        elif n == "s5_h0C":
            o = np.zeros((2, 2, 128, 32), np.float32)
            for d in range(2):
                for ci, key in enumerate(["state_s5_re", "state_s5_im"]):
                    a = np.asarray(inp[key][b, 0, d])
                    o[d, ci] = a.reshape(32, 128).T
            out[n] = o
        elif n == "s5_dT":
            out[n] = _f32(np.asarray(inp["s5_d"][0]).reshape(32, 32).T)
        elif n == "gla_dup":
            o = np.zeros((2, 64, GDKW), np.float32)
            for d in range(2):
                o[d, 16 * d:16 * d + 16] = np.asarray(inp["gla_decay_up"][0, d])
                o[d, 32] = np.asarray(inp["gla_decay_b"][0, d])
            out[n] = o
        elif n == "gla_nw":
            out[n] = _f32(inp["gla_norm_w"][0])
        elif n == "gla_s0":
            out[n] = _f32(inp["state_gla"][b, 0])
        elif n == "e_w_out":
            out[n] = _f32(inp["e_w_out"][0])
        elif n == "o_w_in":
            out[n] = _f32(inp["o_w_in"][0])
        elif n == "o_w_out":
            out[n] = _f32(inp["o_w_out"][0])
        elif n == "rwkv_muT":
            out[n] = _f32(np.stack([fm(inp["rwkv_mu"][0, i], KD) for i in range(2)]))
        elif n == "rwkv_w2a":
            out[n] = _f32(np.stack([np.concatenate([inp["rwkv_w2"][0, d], np.asarray(inp["rwkv_w0"][0, d])[None]], 0) for d in range(2)]))
        elif n == "rwkv_a2a":
            out[n] = _f32(np.stack([np.concatenate([inp["rwkv_a2"][0, d], np.asarray(inp["rwkv_a0"][0, d])[None]], 0) for d in range(2)]))
        elif n in ("rwkv_k_k", "rwkv_k_a", "rwkv_lnx_w", "rwkv_lnx_b"):
            out[n] = _f32(inp[n][0])
        elif n == "rwkv_r_k":
            out[n] = _f32(np.asarray(inp[n][0]).reshape(-1))
        elif n == "rwkv_s0T":
            a = np.asarray(inp["state_rwkv"][b, 0])
            out[n] = _f32(a.reshape(2, 16, 2, 64, 64).transpose(0, 1, 2, 4, 3).reshape(2, 16, 128, 64))
        elif n == "final_norm_w":
            out[n] = _f32(inp["final_norm_w"])
        elif n == "s5_glu_w":
            out[n] = _f32(inp["s5_glu_w"][0])
        elif n == "s5_glu_bT":
            out[n] = fm(inp["s5_glu_b"][0], 8)
        else:
            raise KeyError(n)
    return out


_PROG = {}


def kernel(**inputs):
    LS, NPR = 4096, 4
    inp = {k: np.asarray(v) for k, v in inputs.items()}
    if "prog" not in _PROG:
        _PROG["prog"] = build_program(LS, NPR, upto="all")
    P = _PROG["prog"]
    names = list(P.inp.keys())
    per_core = []
    for core in range(8):
        b = core % 4
        per_core.append(prep_core_inputs(inp, b, [4 * b + j for j in range(NPR)], names))
    res = run_bass_kernel_spmd(P.nc, per_core, core_ids=list(range(8)))
    B, Bd = 16, 4
    y_prompt = np.zeros((B, LP, D), np.float32)
    y_sample = np.zeros((Bd, LS, D), np.float32)
    new_s5_re = np.zeros((B, 1, 2, G_S5, P_S5), np.float32)
    new_s5_im = np.zeros((B, 1, 2, G_S5, P_S5), np.float32)
    new_gla = np.zeros((B, 1, 2, GH, GDK, GDV), np.float32)
    new_rwkv = np.zeros((B, 1, 2, RH, RN, RN), np.float32)
    for b in range(4):
        r = res.results[b]
        y_sample[b] = r["y"][:LS]
        for j in range(NPR):
            y_prompt[4 * b + j] = r["y"][LS + j * LP:LS + (j + 1) * LP]
            new_s5_re[4 * b + j, 0] = r["new_s5_re"][j]
            new_s5_im[4 * b + j, 0] = r["new_s5_im"][j]
            new_gla[4 * b + j, 0] = r["new_gla"][j]
            new_rwkv[4 * b + j, 0] = r["new_rwkv"][j]
    return (y_prompt, y_sample, new_s5_re, new_s5_im, new_gla, new_rwkv)
```

```python
import math
from contextlib import ExitStack
import numpy as np
import concourse.bass as bass
import concourse.mybir as mybir
from concourse.bass_utils import run_bass_kernel_spmd

F32 = mybir.dt.float32
I32 = mybir.dt.int32
AF = mybir.ActivationFunctionType
ALU = mybir.AluOpType
AX = mybir.AxisListType

D = 2048
KD = D // 128
LP = 256
EPS = 1e-6
S5_W = 1024
G_S5 = 64
P_S5 = 64
H_S5 = 16
GH = 6
GDK = 256
GDV = 512
GDKW = GH * GDK
GDVW = GH * GDV
EVEN_IN = 11296
RH = 32
RN = 64
ODD_IN = 8576
LNX_EPS = 64e-5
TWO_PI = 2.0 * math.pi


class Buf:
    __slots__ = ("name", "writers", "readers", "dma_sem", "dma_total")

    def __init__(self, name):
        self.name = name
        self.writers = {}
        self.readers = {}
        self.dma_sem = None
        self.dma_total = 0


class _Rec:
    def __init__(self):
        self.call = None

    def __getattr__(self, name):
        def f(*a, **kw):
            self.call = (name, a, kw)
            return self
        return f


class Sched:
    COMPUTE = ("tensor", "vector", "scalar", "gpsimd")

    def __init__(self, nc):
        self.nc = nc
        self.ops = {e: [] for e in ("tensor", "vector", "scalar", "gpsimd", "sync")}
        self.sem = {}
        self.count = {}
        self.seen = {e: {} for e in self.ops}
        self.sems = {}
        for e in self.COMPUTE:
            self.sem[e] = ("E", e)
            self.sems[("E", e)] = nc.alloc_semaphore(name=f"prog_{e}")
            self.count[e] = 0
        self.free_sems = []
        self.pending = {e: [] for e in self.ops}
        self.sem_total = {}
        self.nsem = 0
        self.nops = 0

    def _collect(self, E, reads, writes, partial=()):
        need = {}
        for b in reads:
            for k, (v, en) in b.writers.items():
                if en == E and E == "tensor":
                    continue
                if need.get(k, 0) < v:
                    need[k] = v
        for b in writes:
            for k, (v, en) in b.writers.items():
                if en == E:
                    continue
                if need.get(k, 0) < v:
                    need[k] = v
            for k, (v, en) in b.readers.items():
                if en == E:
                    continue
                if need.get(k, 0) < v:
                    need[k] = v
        for b in partial:
            for k, (v, en) in b.readers.items():
                if en == E:
                    continue
                if need.get(k, 0) < v:
                    need[k] = v
        seen = self.seen[E]
        out = []
        for k, v in need.items():
            if seen.get(k, 0) < v:
                seen[k] = v
                out.append((k, v))
        return out

    def _record(self, k, v, en, reads, writes, partial):
        for b in writes:
            b.writers = {k: (v, en)}
            b.readers = {}
        for b in partial:
            b.writers[k] = (v, en)
        for b in reads:
            b.readers[k] = (v, en)

    def barrier(self):
        for E in self.ops:
            seen = self.seen[E]
            for e2 in self.COMPUTE:
                if e2 != E and seen.get(self.sem[e2], 0) < self.count[e2]:
                    seen[self.sem[e2]] = self.count[e2]
                    self.pending[E].append((self.sem[e2], self.count[e2]))
            for k, v in self.sem_total.items():
                if seen.get(k, 0) < v:
                    seen[k] = v
                    self.pending[E].append((k, v))

    def op(self, E, fn, reads=(), writes=(), partial=()):
        waits = self.pending[E] + self._collect(E, reads, writes, partial)
        self.pending[E] = []
        self.count[E] += 1
        rec = _Rec()
        fn(rec)
        self.ops[E].append((waits, rec.call, self.sem[E], 1))
        self._record(self.sem[E], self.count[E], E, reads, writes, partial)
        self.nops += 1

    def dma(self, Q, fn, sb, reads=(), writes=(), partial=()):
        waits = self.pending[Q] + self._collect(Q, reads, writes, partial)
        self.pending[Q] = []
        if sb.dma_sem is None:
            if self.free_sems:
                key, total = self.free_sems.pop()
                sb.dma_sem = key
                sb.dma_total = total
                if self.seen[Q].get(key, 0) < total:
                    self.seen[Q][key] = total
                    waits.append((key, total))
            elif self.nsem < 88:
                key = ("D", self.nsem)
                self.sems[key] = self.nc.alloc_semaphore(name=f"dma_{self.nsem}")
                self.nsem += 1
                sb.dma_sem = key
            else:
                raise RuntimeError("too many dma semaphores")
        sb.dma_total += 16
        self.sem_total[sb.dma_sem] = sb.dma_total
        rec = _Rec()
        fn(rec)
        self.ops[Q].append((waits, rec.call, sb.dma_sem, 16))
        self._record(sb.dma_sem, sb.dma_total, None, reads, writes, partial)
        self.nops += 1

    def release(self, bufs):
        for b in bufs:
            if b.dma_sem is not None:
                self.free_sems.append((b.dma_sem, b.dma_total))
                b.dma_sem = None

    def finish(self, block):
        fin = list(self.sem_total.items())
        sems = self.sems
        ops = self.ops

        def replay(eng, name):
            for waits, fn, semkey, inc in ops[name]:
                for k, v in waits:
                    eng.wait_ge(sems[k], v)
                getattr(eng, fn[0])(*fn[1], **fn[2]).then_inc(sems[semkey], inc)
            if name == "sync":
                for k, v in fin:
                    eng.wait_ge(sems[k], v)

        block.sync(lambda e: replay(e, "sync"))
        block.tensor(lambda e: replay(e, "tensor"))
        block.vector(lambda e: replay(e, "vector"))
        block.scalar(lambda e: replay(e, "scalar"))
        block.gpsimd(lambda e: replay(e, "gpsimd"))


class TB:
    __slots__ = ("t", "b")

    def __init__(self, t, name):
        self.t = t
        self.b = Buf(name)

    def __getitem__(self, k):
        return self.t[k]


class Ring:
    def __init__(self, items):
        self.items = items
        self.i = 0

    def next(self):
        it = self.items[self.i % len(self.items)]
        self.i += 1
        return it


class Prog:
    def __init__(self, LS, NPR, debug=()):
        self.LS = LS
        self.NPR = NPR
        self.NT = LS + NPR * LP
        assert LS % 512 == 0 and self.NT % 512 == 0
        self.debug = set(debug)
        self.nc = bass.Bass("TRN2", target_bir_lowering=False)
        self.S = Sched(self.nc)
        self.inp = {}
        self.outp = {}
        self.scr = {}
        self.dbuf = {}
        self.seqs = [(0, LS, 0)] + [(LS + j * LP, LP, 1) for j in range(NPR)]

    def din(self, name, shape):
        self.inp[name] = self.nc.dram_tensor(name, list(shape), F32, kind="ExternalInput").ap()
        self.dbuf[name] = Buf(name)
        return self.inp[name]

    def dout(self, name, shape):
        self.outp[name] = self.nc.dram_tensor(name, list(shape), F32, kind="ExternalOutput").ap()
        self.dbuf[name] = Buf(name)
        return self.outp[name]

    def dscr(self, name, shape):
        kind = "ExternalOutput" if name in self.debug else "Internal"
        self.scr[name] = self.nc.dram_tensor(name, list(shape), F32, kind=kind).ap()
        self.dbuf[name] = Buf(name)
        return self.scr[name]

    def dreg(self, name, key):
        k = (name, key)
        if k not in self.dbuf:
            self.dbuf[k] = Buf(str(k))
        return self.dbuf[k]

    def sb(self, st, name, shape, dt=F32):
        return TB(st.enter_context(self.nc.sbuf_tensor(name, list(shape), dt)), name)

    def ring(self, st, name, shape, n, dt=F32):
        return Ring([self.sb(st, f"{name}{i}", shape, dt) for i in range(n)])

    def V(self, fn, reads=(), writes=(), partial=()):
        self.S.op("vector", fn, [r.b if isinstance(r, TB) else r for r in reads],
                  [w.b if isinstance(w, TB) else w for w in writes],
                  [w.b if isinstance(w, TB) else w for w in partial])

    def A(self, fn, reads=(), writes=(), partial=()):
        self.S.op("scalar", fn, [r.b if isinstance(r, TB) else r for r in reads],
                  [w.b if isinstance(w, TB) else w for w in writes],
                  [w.b if isinstance(w, TB) else w for w in partial])

    def G(self, fn, reads=(), writes=(), partial=()):
        self.S.op("gpsimd", fn, [r.b if isinstance(r, TB) else r for r in reads],
                  [w.b if isinstance(w, TB) else w for w in writes],
                  [w.b if isinstance(w, TB) else w for w in partial])

    def T(self, fn, reads=(), writes=(), partial=()):
        self.S.op("tensor", fn, [r.b if isinstance(r, TB) else r for r in reads],
                  [w.b if isinstance(w, TB) else w for w in writes],
                  [w.b if isinstance(w, TB) else w for w in partial])

    def load(self, dst, dst_ap, src_ap, src_bufs=(), partial=False):
        if partial:
            self.S.dma("sync", lambda e: e.dma_start(out=dst_ap, in_=src_ap), dst.b,
                       reads=list(src_bufs), partial=[dst.b])
        else:
            self.S.dma("sync", lambda e: e.dma_start(out=dst_ap, in_=src_ap), dst.b,
                       reads=list(src_bufs), writes=[dst.b])

    def store(self, src, dst_ap, src_ap, dst_bufs=(), partial_bufs=()):
        import os
        self.S.dma(os.environ.get("STQ", "gpsimd"), lambda e: e.dma_start(out=dst_ap, in_=src_ap), src.b,
                   reads=[src.b], writes=list(dst_bufs), partial=list(partial_bufs))

    def mm(self, ps, out_ap, lhsT_ap, rhs_ap, reads, start, stop, first=None):
        if first is None:
            first = start
        self.T(lambda e: e.matmul(out_ap, lhsT_ap, rhs_ap, start=start, stop=stop),
               reads=reads, writes=[ps] if first else (), partial=() if first else [ps])

    def tr(self, ps, out_ap, in_ap, reads):
        ident = self.ident
        self.T(lambda e: e.matmul(out_ap, in_ap, ident[:], start=True, stop=True, is_transpose=True), reads=list(reads) + [ident], partial=[ps])

    def ev(self, out_ap, in_ap, reads, writes=(), partial=()):
        self._evc = getattr(self, "_evc", 0) + 1
        if self._evc % 2:
            self.V(lambda e: e.tensor_copy(out_ap, in_ap), reads=reads, writes=writes, partial=partial)
        else:
            self.A(lambda e: e.copy(out_ap, in_ap), reads=reads, writes=writes, partial=partial)

    def setup(self, st):
        nc = self.nc
        self.ps = Ring([TB(st.enter_context(nc.psum_tensor(f"ps{i}", [128, 512], F32)), f"ps{i}") for i in range(8)])
        self.ident = self.sb(st, "ident", [128, 128])
        self.ones = self.sb(st, "ones", [128, 128])
        tmp = self.sb(st, "setup_tmp", [128, 128])
        self.G(lambda e: e.iota(tmp[:], [[1, 128]], base=0, channel_multiplier=-1,
                                allow_small_or_imprecise_dtypes=True), writes=[tmp])
        ident = self.ident
        self.V(lambda e: e.tensor_single_scalar(ident[:], tmp[:], 0.0, op=ALU.is_equal), reads=[tmp], writes=[ident])
        ones = self.ones
        self.V(lambda e: e.memset(ones[:], 1.0), writes=[ones])
        self.m_incl = [self.sb(st, f"m_incl{d}", [128, 128]) for d in range(2)]
        self.m_strict = [self.sb(st, f"m_strict{d}", [128, 128]) for d in range(2)]
        for d in range(2):
            mi, ms = self.m_incl[d], self.m_strict[d]
            op_i = ALU.is_ge if d == 0 else ALU.is_le
            op_s = ALU.is_gt if d == 0 else ALU.is_lt
            self.V(lambda e, mi=mi, op_i=op_i: e.tensor_single_scalar(mi[:], tmp[:], 0.0, op=op_i), reads=[tmp], writes=[mi])
            self.V(lambda e, ms=ms, op_s=op_s: e.tensor_single_scalar(ms[:], tmp[:], 0.0, op=op_s), reads=[tmp], writes=[ms])

    def adaln(self):
        nc = self.nc
        condT = self.din("condT", [128, KD, 2])
        ada_w = self.din("ada_w", [2, D, 3 * D])
        ada_bT = self.din("ada_bT", [2, 128, 48])
        norm_wT = self.din("norm_wT", [2, 128, KD])
        st0 = self.pst
        self.pad0 = self.sb(st0, "pad0", [128, 64])
        self.shiftT = [self.sb(st0, f"shiftT{l}", [128, KD, 2]) for l in range(2)]
        self.multT = [self.sb(st0, f"multT{l}", [128, KD, 2]) for l in range(2)]
        self.gateT = [self.sb(st0, f"gateT{l}", [128, KD, 2]) for l in range(2)]
        with ExitStack() as st:
            sc = self.sb(st, "ada_sc", [128, KD, 2])
            bt = self.sb(st, "ada_bt", [128, 48])
            nw = self.sb(st, "ada_nw", [128, KD])
            mt = self.sb(st, "ada_mt", [128, 48, 2])
            wr = self.ring(st, "ada_w", [128, KD, 512], 2)
            self.load(sc, sc[:], condT[:, :, :])
            self.A(lambda e: e.activation(out=sc[:], in_=sc[:], func=AF.Silu), reads=[sc], writes=[sc])
            for l in range(2):
                self.load(bt, bt[:], ada_bT[l, :, :])
                self.load(nw, nw[:], norm_wT[l, :, :])
                ps = self.ps.next()
                wv = ada_w[l].rearrange("(k p) n -> p k n", p=128)
                for cb in range(12):
                    wt = wr.next()
                    for kq in range(4):
                        self.load(wt, wt[:, kq * 4:(kq + 1) * 4, :], wv[:, kq * 4:(kq + 1) * 4, cb * 512:(cb + 1) * 512], partial=(kq > 0))
                    for m in range(4):
                        nb = cb * 4 + m
                        for k in range(KD):
                            self.mm(ps, ps[:, 2 * nb:2 * nb + 2], wt[:, k, m * 128:(m + 1) * 128], sc[:, k, :],
                                    reads=[wt, sc], start=(k == 0), stop=(k == KD - 1), first=(k == 0 and nb == 0))
                self.V(lambda e, ps=ps: e.tensor_tensor(out=mt[:], in0=ps[:, 0:96].rearrange("p (n c) -> p n c", c=2),
                                                        in1=bt[:].unsqueeze(2).to_broadcast([128, 48, 2]), op=ALU.add),
                       reads=[ps, bt], writes=[mt])
                sh, mu, ga = self.shiftT[l], self.multT[l], self.gateT[l]
                self.V(lambda e, sh=sh: e.tensor_copy(sh[:], mt[:, 0:16, :]), reads=[mt], writes=[sh])
                self.V(lambda e, ga=ga: e.tensor_copy(ga[:], mt[:, 32:48, :]), reads=[mt], writes=[ga])
                self.V(lambda e, mu=mu: e.scalar_tensor_tensor(out=mu[:], in0=mt[:, 16:32, :], scalar=1.0,
                                                                in1=nw[:].unsqueeze(2).to_broadcast([128, KD, 2]),
                                                                op0=ALU.add, op1=ALU.mult),
                       reads=[mt, nw], writes=[mu])
            self.S.barrier()
            self.S.release([sc.b, bt.b, nw.b] + [w.b for w in wr.items])

    def rr_sin(self, st, out_t, ang_t, shape, pfx):
        ni = self.sb(st, pfx + "_ni", shape, I32)
        nf = self.sb(st, pfx + "_nf", shape, F32)
        self.V(lambda e: e.tensor_single_scalar(nf[:], ang_t[:], 1.0 / TWO_PI, op=ALU.mult), reads=[ang_t], writes=[nf])
        self.V(lambda e: e.tensor_copy(ni[:], nf[:]), reads=[nf], writes=[ni])
        self.V(lambda e: e.tensor_copy(nf[:], ni[:]), reads=[ni], writes=[nf])
        self.V(lambda e: e.scalar_tensor_tensor(out=ang_t[:], in0=nf[:], scalar=-TWO_PI, in1=ang_t[:],
                                                op0=ALU.mult, op1=ALU.add), reads=[nf, ang_t], writes=[ang_t])
        self.V(lambda e: e.tensor_scalar(out=ang_t[:], in0=ang_t[:], scalar1=math.pi, scalar2=-math.pi,
                                         op0=ALU.min, op1=ALU.max), reads=[ang_t], writes=[ang_t])
        self.A(lambda e: e.activation(out=out_t[:], in_=ang_t[:], func=AF.Sin), reads=[ang_t], writes=[out_t])

    def posembed(self, st):
        LS = self.LS
        E = self.sb(st, "pe_E", [64, 1024])
        with ExitStack() as s2:
            om = self.sb(s2, "pe_om", [64, 512])
            kk = self.sb(s2, "pe_k", [64, 1])
            a1 = self.sb(s2, "pe_a1", [64, 512])
            a2 = self.sb(s2, "pe_a2", [64, 512])
            o1 = self.sb(s2, "pe_o1", [64, 512])
            self.G(lambda e: e.iota(om[:], [[1, 512]], base=0, channel_multiplier=0, allow_small_or_imprecise_dtypes=True), writes=[om])
            self.G(lambda e: e.iota(kk[:], [[1, 1]], base=0, channel_multiplier=1, allow_small_or_imprecise_dtypes=True), writes=[kk])
            self.A(lambda e: e.activation(out=om[:], in_=om[:], func=AF.Exp, scale=-math.log(10000.0) / 512.0), reads=[om], writes=[om])
            self.V(lambda e: e.tensor_scalar(out=a1[:], in0=om[:], scalar1=kk[:, 0:1], scalar2=None, op0=ALU.mult), reads=[om, kk], writes=[a1])
            self.V(lambda e: e.tensor_scalar(out=a2[:], in0=a1[:], scalar1=math.pi / 2, scalar2=None, op0=ALU.add), reads=[a1], writes=[a2])
            self.rr_sin(s2, o1, a1, [64, 512], "pe_r1")
            self.V(lambda e: e.tensor_copy(E[:, 0:512], o1[:]), reads=[o1], partial=[E])
            self.rr_sin(s2, o1, a2, [64, 512], "pe_r2")
            self.V(lambda e: e.tensor_copy(E[:, 512:1024], o1[:]), reads=[o1], partial=[E])
            self.S.barrier()
        self.pe_E = E
        selrow = self.sb(st, "pe_selrow", [64, LS])
        selcol = self.sb(st, "pe_selcol", [64, 128])
        with ExitStack() as s2:
            v1 = self.sb(s2, "pe_v1", [64, LS])
            m1 = self.sb(s2, "pe_m1", [64, LS])
            self.G(lambda e: e.iota(v1[:], [[1, LS]], base=0, channel_multiplier=-64, allow_small_or_imprecise_dtypes=True), writes=[v1])
            self.V(lambda e: e.tensor_single_scalar(m1[:], v1[:], 0.0, op=ALU.is_ge), reads=[v1], writes=[m1])
            self.V(lambda e: e.scalar_tensor_tensor(out=selrow[:], in0=v1[:], scalar=64.0, in1=m1[:], op0=ALU.is_lt, op1=ALU.mult),
                   reads=[v1, m1], writes=[selrow])
            self.G(lambda e: e.iota(v1[:, 0:128], [[1, 128]], base=0, channel_multiplier=-1, allow_small_or_imprecise_dtypes=True), writes=[v1])
            self.V(lambda e: e.tensor_single_scalar(m1[:, 0:128], v1[:, 0:128], 0.0, op=ALU.is_equal), reads=[v1], writes=[m1])
            self.V(lambda e: e.scalar_tensor_tensor(out=selcol[:], in0=v1[:, 0:128], scalar=64.0, in1=m1[:, 0:128], op0=ALU.is_equal, op1=ALU.add),
                   reads=[v1, m1], writes=[selcol])
            self.S.barrier()
        self.pe_selrow = selrow
        posc = self.sb(st, "pe_posc", [128, 1024])
        for hb in range(2):
            ps = self.ps.next()
            self.mm(ps, ps[:, :], selcol[:, :], E[:, hb * 512:(hb + 1) * 512], reads=[selcol, E], start=True, stop=True)
            self.ev(posc[:, hb * 512:(hb + 1) * 512], ps[:, :], reads=[ps], partial=[posc])
        self.pe_posc = posc

    def prologue(self, layer):
        NT, LS = self.NT, self.LS
        X, HT = self.scr["X"], self.scr["HT"]
        HTv = HT.rearrange("(k p) t -> p k t", p=128)
        with ExitStack() as st:
            if layer == 0:
                self.posembed(st)
            xr = self.ring(st, f"p{layer}_x", [128, D], 2)
            xn = self.sb(st, f"p{layer}_xn", [128, D])
            hr = self.ring(st, f"p{layer}_h", [128, KD, 128], 2)
            ss = self.sb(st, f"p{layer}_ss", [128, 1])
            rs = self.sb(st, f"p{layer}_rs", [128, 1])
            import os
            PS_ = int(os.environ.get("PRO_STOP", "9"))
            for ti in range(NT // 128 if PS_ > 0 else 0):
                t0 = ti * 128
                c = 0 if t0 < LS else 1
                xt = xr.next()
                if layer == 0:
                    src = self.inp["xs"][t0:t0 + 128, :] if c == 0 else self.inp["xp"][t0 - LS:t0 - LS + 128, :]
                    self.load(xt, xt[:], src)
                    if c == 0:
                        posc, E, selrow = self.pe_posc, self.pe_E, self.pe_selrow
                        self.V(lambda e, xt=xt: e.tensor_tensor(out=xt[:, 1024:2048], in0=xt[:, 1024:2048], in1=posc[:], op=ALU.add),
                               reads=[xt, posc], writes=[xt])
                        for hb in range(2):
                            ps = self.ps.next()
                            self.mm(ps, ps[:, :], selrow[:, t0:t0 + 128], E[:, hb * 512:(hb + 1) * 512], reads=[selrow, E], start=True, stop=True)
                            self.V(lambda e, xt=xt, ps=ps, hb=hb: e.tensor_tensor(out=xt[:, hb * 512:(hb + 1) * 512], in0=xt[:, hb * 512:(hb + 1) * 512], in1=ps[:, :], op=ALU.add),
                                   reads=[xt, ps], writes=[xt])
                    self.store(xt, X[t0:t0 + 128, :], xt[:], dst_bufs=[self.dreg("X", ti)])
                else:
                    self.load(xt, xt[:], X[t0:t0 + 128, :], src_bufs=[self.dreg("X", ti)])
                if PS_ < 2:
                    continue
                self.A(lambda e, xt=xt: e.activation(out=xn[:], in_=xt[:], func=AF.Square, accum_out=ss[:, 0:1]), reads=[xt], writes=[xn, ss])
                self.V(lambda e: e.tensor_scalar(out=rs[:], in0=ss[:], scalar1=1.0 / D, scalar2=EPS, op0=ALU.mult, op1=ALU.add), reads=[ss], writes=[rs])
                self.A(lambda e: e.activation(out=rs[:], in_=rs[:], func=AF.Sqrt), reads=[rs], writes=[rs])
                self.V(lambda e: e.reciprocal(rs[:], rs[:]), reads=[rs], writes=[rs])
                self.V(lambda e, xt=xt: e.tensor_scalar(out=xn[:], in0=xt[:], scalar1=rs[:, 0:1], scalar2=None, op0=ALU.mult), reads=[xt, rs], writes=[xn])
                if PS_ < 3:
                    continue
                ht = hr.next()
                mu, sh = self.multT[layer], self.shiftT[layer]
                for kb in range(4):
                    ps = self.ps.next()
                    for j in range(4):
                        k = kb * 4 + j
                        self.tr(ps, ps[:, j * 128:(j + 1) * 128], xn[:, k * 128:(k + 1) * 128], reads=[xn])
                    for j in range(4):
                        k = kb * 4 + j
                        fn = (lambda e, ht=ht, ps=ps, k=k, j=j, c=c: e.tensor_scalar(
                            out=ht[:, k, :], in0=ps[:, j * 128:(j + 1) * 128], scalar1=mu[:, k, c:c + 1], scalar2=sh[:, k, c:c + 1],
                            op0=ALU.mult, op1=ALU.add))
                        EVM = os.environ.get("EVMODE", "2")
                        if EVM == "0":
                            self.ev(ht[:, k, :], ps[:, j * 128:(j + 1) * 128], reads=[ps], partial=[ht])
                        elif j % 2 or EVM == "2":
                            self.V(fn, reads=[ps, mu, sh], partial=[ht])
                        else:
                            fn2 = (lambda e, ht=ht, ps=ps, k=k, j=j, c=c: e.activation(
                                out=ht[:, k, :], in_=ps[:, j * 128:(j + 1) * 128], func=AF.Identity,
                                scale=mu[:, k, c:c + 1], bias=sh[:, k, c:c + 1]))
                            self.A(fn2, reads=[ps, mu, sh], partial=[ht])
                for kq in range(0, KD, 4):
                    self.store(ht, HTv[:, kq:kq + 4, t0:t0 + 128], ht[:, kq:kq + 4, :], partial_bufs=[self.dreg("HT", "all")])
            self.S.barrier()
            self.S.release([x.b for x in xr.items] + [h.b for h in hr.items])

    def lin(self, name, K, W_ap, blocks, TS, CB, src_loader, src_reads_key=None):
        NT = self.NT
        KC = K // 128
        Wv = W_ap.rearrange("(k p) n -> p k n", p=128)
        with ExitStack() as st:
            sr = self.ring(st, f"{name}_src", [128, KC, TS], 2)
            wr = self.ring(st, f"{name}_w", [128, KC, CB], 2)
            self.lin_st = st
            for si in range(NT // TS):
                t0 = si * TS
                src = sr.next()
                src_loader(si, t0, TS, src)
                for (c0, c1, mode, epi) in blocks:
                    for cb0 in range(c0, c1, CB):
                        ncb = min(CB, c1 - cb0)
                        wt = wr.next()
                        for kq in range(0, KC, 4):
                            self.load(wt, wt[:, kq:kq + 4, 0:ncb], Wv[:, kq:kq + 4, cb0:cb0 + ncb], partial=(kq > 0))
                        if mode == "F":
                            for m0 in range(0, ncb, 128):
                                mc = min(128, ncb - m0)
                                ps = self.ps.next()
                                for k in range(KC):
                                    self.mm(ps, ps[0:mc, 0:TS], wt[:, k, m0:m0 + mc], src[:, k, :], reads=[wt, src],
                                            start=(k == 0), stop=(k == KC - 1))
                                epi(ps, mc, cb0 + m0, t0, TS)
                        else:
                            for tt in range(TS // 128):
                                ps = self.ps.next()
                                for k in range(KC):
                                    self.mm(ps, ps[:, 0:ncb], src[:, k, tt * 128:(tt + 1) * 128], wt[:, k, 0:ncb], reads=[wt, src],
                                            start=(k == 0), stop=(k == KC - 1))
                                epi(ps, ncb, cb0, t0 + tt * 128)
            self.S.barrier()
            self.S.release([x.b for x in sr.items] + [w.b for w in wr.items])

    def inproj0(self):
        NT = self.NT
        w = self.din("e_w_in", [D, EVEN_IN])
        UT = self.dscr("UT", [1024, NT]); SGT = self.dscr("SGT", [1024, NT])
        QT = self.dscr("QT", [GDKW, NT]); KT = self.dscr("KT", [GDKW, NT])
        DLT = self.dscr("DLT", [32, NT])
        Ktm = self.dscr("Ktm", [NT, GDKW]); Vtm = self.dscr("Vtm", [NT, GDVW]); GGtm = self.dscr("GGtm", [NT, GDVW])
        HTv = self.scr["HT"].rearrange("(k p) t -> p k t", p=128)

        def loader(si, t0, TS, dst):
            for kq in range(0, KD, 4):
                self.load(dst, dst[:, kq:kq + 4, :], HTv[:, kq:kq + 4, t0:t0 + TS], src_bufs=[self.dreg("HT", "all")], partial=(kq > 0))

        fmap = [(0, 1024, UT, "UT"), (1024, 2048, SGT, "SGT"), (2048, 3584, QT, "QT"), (3584, 5120, KT, "KT"), (11264, 11296, DLT, "DLT")]
        tmap = [(3584, 5120, Ktm, "Ktm"), (5120, 8192, Vtm, "Vtm"), (8192, 11264, GGtm, "GGtm")]

        def epiF(ps, mc, col0, t0, TS):
            o = self.ip_or.next()
            self.ev(o[0:mc, 0:TS], ps[0:mc, 0:TS], reads=[ps], writes=[o])
            for (a, b, ten, nm) in fmap:
                if a <= col0 < b:
                    self.store(o, ten[col0 - a:col0 - a + mc, t0:t0 + TS], o[0:mc, 0:TS], partial_bufs=[self.dreg(nm, "all")])

        def epiT(ps, ncb, col0, tok0):
            o = self.ip_or.next()
            self.ev(o[:, 0:ncb], ps[:, 0:ncb], reads=[ps], writes=[o])
            for (a, b, ten, nm) in tmap:
                if a <= col0 < b:
                    self.store(o, ten[tok0:tok0 + 128, col0 - a:col0 - a + ncb], o[:, 0:ncb], partial_bufs=[self.dreg(nm, "all")])

        with ExitStack() as st:
            self.ip_or = self.ring(st, "ip0_o", [128, 512], 4)
            blocks = [(0, 5120, "F", epiF), (11264, 11296, "F", epiF), (3584, 11264, "T", epiT)]
            self.lin("ip0", D, w, blocks, 512, 512, loader)
            self.S.barrier()
            self.S.release([o.b for o in self.ip_or.items])


    def cplx_lambda_bar(self, st, lam, pfx):
        sh = [128, 32]
        dt = self.sb(st, pfx + "dt", sh); mag = self.sb(st, pfx + "mag", sh)
        a1 = self.sb(st, pfx + "a1", sh); a2 = self.sb(st, pfx + "a2", sh)
        sn = self.sb(st, pfx + "sn", sh); cs = self.sb(st, pfx + "cs", sh)
        abre = self.sb(st, pfx + "abre", sh); abim = self.sb(st, pfx + "abim", sh)
        self.A(lambda e: e.activation(out=dt[:], in_=lam[:, 2, :], func=AF.Exp), reads=[lam], writes=[dt])
        self.V(lambda e: e.tensor_tensor(out=mag[:], in0=lam[:, 0, :], in1=dt[:], op=ALU.mult), reads=[lam, dt], writes=[mag])
        self.A(lambda e: e.activation(out=mag[:], in_=mag[:], func=AF.Exp), reads=[mag], writes=[mag])
        self.V(lambda e: e.tensor_tensor(out=a1[:], in0=lam[:, 1, :], in1=dt[:], op=ALU.mult), reads=[lam, dt], writes=[a1])
        self.V(lambda e: e.tensor_scalar(out=a2[:], in0=a1[:], scalar1=math.pi / 2, scalar2=None, op0=ALU.add), reads=[a1], writes=[a2])
        self.rr_sin(st, sn, a1, sh, pfx + "r1")
        self.rr_sin(st, cs, a2, sh, pfx + "r2")
        self.V(lambda e: e.tensor_tensor(out=abre[:], in0=mag[:], in1=cs[:], op=ALU.mult), reads=[mag, cs], writes=[abre])
        self.V(lambda e: e.tensor_tensor(out=abim[:], in0=mag[:], in1=sn[:], op=ALU.mult), reads=[mag, sn], writes=[abim])
        return abre, abim

    def cmul(self, st, ar, ai, br, bi, pfx, shape):
        orr = self.sb(st, pfx + "re", shape); oi = self.sb(st, pfx + "im", shape); tm = self.sb(st, pfx + "tm", shape)
        self.V(lambda e: e.tensor_tensor(out=orr[:], in0=ar[:], in1=br[:], op=ALU.mult), reads=[ar, br], writes=[orr])
        self.V(lambda e: e.tensor_tensor(out=tm[:], in0=ai[:], in1=bi[:], op=ALU.mult), reads=[ai, bi], writes=[tm])
        self.V(lambda e: e.tensor_tensor(out=orr[:], in0=orr[:], in1=tm[:], op=ALU.subtract), reads=[orr, tm], writes=[orr])
        self.V(lambda e: e.tensor_tensor(out=oi[:], in0=ar[:], in1=bi[:], op=ALU.mult), reads=[ar, bi], writes=[oi])
        self.V(lambda e: e.tensor_tensor(out=tm[:], in0=ai[:], in1=br[:], op=ALU.mult), reads=[ai, br], writes=[tm])
        self.V(lambda e: e.tensor_tensor(out=oi[:], in0=oi[:], in1=tm[:], op=ALU.add), reads=[oi, tm], writes=[oi])
        return orr, oi

    def s5(self):
        NT, LS, NPR = self.NT, self.LS, self.NPR
        lamC = self.din("s5_lamC", [2, 128, 3, 32])
        bT = self.din("s5_bT", [2, 2, 32, 32, 128])
        cT = self.din("s5_cT", [2, 2, 128, 32, 32])
        h0C = self.din("s5_h0C", [2, 2, 128, 32])
        dTd = self.din("s5_dT", [32, 32])
        o_re = self.dout("new_s5_re", [NPR, 2, G_S5, P_S5]); o_im = self.dout("new_s5_im", [NPR, 2, G_S5, P_S5])
        UT = self.scr["UT"]; GYT = self.dscr("GYT", [1024, NT])
        nlev_s = int(math.log2(LS)); nlev_p = 8
        nlev = max(nlev_s, nlev_p)
        with ExitStack() as st:
            pw = []
            fco = []
            h0a = []
            cts = []
            for d in range(2):
                lam = self.sb(st, f"s5lam{d}", [128, 3, 32])
                self.load(lam, lam[:], lamC[d])
                abre, abim = self.cplx_lambda_bar(st, lam, f"s5lb{d}")
                sh = [128, 32]
                den = self.sb(st, f"s5den{d}", sh); t1 = self.sb(st, f"s5t1{d}", sh); t2 = self.sb(st, f"s5t2{d}", sh)
                fre = self.sb(st, f"s5fre{d}", sh); fim = self.sb(st, f"s5fim{d}", sh); nfim = self.sb(st, f"s5nfim{d}", sh)
                self.V(lambda e, lam=lam, den=den: e.tensor_tensor(out=den[:], in0=lam[:, 0, :], in1=lam[:, 0, :], op=ALU.mult), reads=[lam], writes=[den])
                self.V(lambda e, lam=lam, t2=t2: e.tensor_tensor(out=t2[:], in0=lam[:, 1, :], in1=lam[:, 1, :], op=ALU.mult), reads=[lam], writes=[t2])
                self.V(lambda e, den=den, t2=t2: e.tensor_tensor(out=den[:], in0=den[:], in1=t2[:], op=ALU.add), reads=[den, t2], writes=[den])
                self.V(lambda e, den=den: e.reciprocal(den[:], den[:]), reads=[den], writes=[den])
                self.V(lambda e, t1=t1, abre=abre: e.tensor_scalar(out=t1[:], in0=abre[:], scalar1=-1.0, scalar2=None, op0=ALU.add), reads=[abre], writes=[t1])
                self.V(lambda e, fre=fre, t1=t1, lam=lam: e.tensor_tensor(out=fre[:], in0=t1[:], in1=lam[:, 0, :], op=ALU.mult), reads=[t1, lam], writes=[fre])
                self.V(lambda e, t2=t2, abim=abim, lam=lam: e.tensor_tensor(out=t2[:], in0=abim[:], in1=lam[:, 1, :], op=ALU.mult), reads=[abim, lam], writes=[t2])
                self.V(lambda e, fre=fre, t2=t2: e.tensor_tensor(out=fre[:], in0=fre[:], in1=t2[:], op=ALU.add), reads=[fre, t2], writes=[fre])
                self.V(lambda e, fre=fre, den=den: e.tensor_tensor(out=fre[:], in0=fre[:], in1=den[:], op=ALU.mult), reads=[fre, den], writes=[fre])
                self.V(lambda e, fim=fim, abim=abim, lam=lam: e.tensor_tensor(out=fim[:], in0=abim[:], in1=lam[:, 0, :], op=ALU.mult), reads=[abim, lam], writes=[fim])
                self.V(lambda e, t2=t2, t1=t1, lam=lam: e.tensor_tensor(out=t2[:], in0=t1[:], in1=lam[:, 1, :], op=ALU.mult), reads=[t1, lam], writes=[t2])
                self.V(lambda e, fim=fim, t2=t2: e.tensor_tensor(out=fim[:], in0=fim[:], in1=t2[:], op=ALU.subtract), reads=[fim, t2], writes=[fim])
                self.V(lambda e, fim=fim, den=den: e.tensor_tensor(out=fim[:], in0=fim[:], in1=den[:], op=ALU.mult), reads=[fim, den], writes=[fim])
                self.V(lambda e, fim=fim, nfim=nfim: e.tensor_scalar(out=nfim[:], in0=fim[:], scalar1=-1.0, scalar2=None, op0=ALU.mult), reads=[fim], writes=[nfim])
                fco.append((fre, fim, nfim))
                h0 = self.sb(st, f"s5h0{d}", [128, 2, 32])
                self.load(h0, h0[:, 0, :], h0C[d, 0]); self.load(h0, h0[:, 1, :], h0C[d, 1], partial=True)
                h0r = self.sb(st, f"s5h0r{d}", sh); h0i = self.sb(st, f"s5h0i{d}", sh)
                self.V(lambda e, h0=h0, h0r=h0r: e.tensor_copy(h0r[:], h0[:, 0, :]), reads=[h0], writes=[h0r])
                self.V(lambda e, h0=h0, h0i=h0i: e.tensor_copy(h0i[:], h0[:, 1, :]), reads=[h0], writes=[h0i])
                h0a.append(self.cmul(st, abre, abim, h0r, h0i, f"s5h0a{d}", sh))
                lv = []
                cr, ci = abre, abim
                for l in range(nlev):
                    ni = self.sb(st, f"s5pwn{d}_{l}", sh)
                    self.V(lambda e, ni=ni, ci=ci: e.tensor_scalar(out=ni[:], in0=ci[:], scalar1=-1.0, scalar2=None, op0=ALU.mult), reads=[ci], writes=[ni])
                    lv.append((cr, ci, ni))
                    if l < nlev - 1:
                        cr, ci = self.cmul(st, cr, ci, cr, ci, f"s5pw{d}_{l}", sh)
                pw.append(lv)
                ctr = self.sb(st, f"s5ctr{d}", [128, 32, 32]); cti = self.sb(st, f"s5cti{d}", [128, 32, 32])
                self.load(ctr, ctr[:], cT[d, 0]); self.load(cti, cti[:], cT[d, 1])
                self.V(lambda e, cti=cti: e.tensor_scalar(out=cti[:], in0=cti[:], scalar1=-1.0, scalar2=None, op0=ALU.mult), reads=[cti], writes=[cti])
                cts.append((ctr, cti))
            dT = self.sb(st, "s5dT", [32, 32])
            self.load(dT, dT[:], dTd[:, :])
            LB = max(LS, NPR * LP)
            bufs = [[self.sb(st, f"s5b{a}{c}", [128, LB]) for c in range(2)] for a in range(2)]
            uT = self.sb(st, "s5uT", [32, NT]); yacc = self.sb(st, "s5yacc", [32, NT]); tg = self.sb(st, "s5tg", [32, NT])
            btr = self.ring(st, "s5bt", [32, 2, 2, 128], 2)
            fin = self.sb(st, "s5fin", [128, 32 * 2 * NPR * 2])
            groups = [(0, LS, 1, LS, nlev_s, True)] + ([(LS, NPR * LP, NPR, LP, nlev_p, False)] if NPR else [])
            for i in range(32):
                self.load(uT, uT[:], UT[32 * i:32 * i + 32, :], src_bufs=[self.dreg("UT", "all")])
                bt = btr.next()
                for d in range(2):
                    for c in range(2):
                        self.load(bt, bt[:, d, c, :], bT[d, c, :, i, :], partial=(d + c > 0))
                for (g0, glen, nseq, L, nl, is_sample) in groups:
                    for d in range(2):
                        fre, fim, nfim = fco[d]
                        cur = 0
                        A_re, A_im = bufs[cur]
                        for b0 in range(0, glen, 512):
                            bl = min(512, glen - b0)
                            p1 = self.ps.next(); p2 = self.ps.next()
                            self.mm(p1, p1[:, 0:bl], bt[:, d, 0, :], uT[:, g0 + b0:g0 + b0 + bl], reads=[bt, uT], start=True, stop=True)
                            self.mm(p2, p2[:, 0:bl], bt[:, d, 1, :], uT[:, g0 + b0:g0 + b0 + bl], reads=[bt, uT], start=True, stop=True)
                            self.V(lambda e, A_re=A_re, p1=p1, b0=b0, bl=bl, fre=fre, i=i: e.tensor_scalar(out=A_re[:, b0:b0 + bl], in0=p1[:, 0:bl], scalar1=fre[:, i:i + 1], scalar2=None, op0=ALU.mult),
                                   reads=[p1, fre], partial=[A_re])
                            self.V(lambda e, A_re=A_re, p2=p2, b0=b0, bl=bl, nfim=nfim, i=i: e.scalar_tensor_tensor(out=A_re[:, b0:b0 + bl], in0=p2[:, 0:bl], scalar=nfim[:, i:i + 1], in1=A_re[:, b0:b0 + bl], op0=ALU.mult, op1=ALU.add),
                                   reads=[p2, nfim, A_re], partial=[A_re])
                            self.V(lambda e, A_im=A_im, p2=p2, b0=b0, bl=bl, fre=fre, i=i: e.tensor_scalar(out=A_im[:, b0:b0 + bl], in0=p2[:, 0:bl], scalar1=fre[:, i:i + 1], scalar2=None, op0=ALU.mult),
                                   reads=[p2, fre], partial=[A_im])
                            self.V(lambda e, A_im=A_im, p1=p1, b0=b0, bl=bl, fim=fim, i=i: e.scalar_tensor_tensor(out=A_im[:, b0:b0 + bl], in0=p1[:, 0:bl], scalar=fim[:, i:i + 1], in1=A_im[:, b0:b0 + bl], op0=ALU.mult, op1=ALU.add),
                                   reads=[p1, fim, A_im], partial=[A_im])
                        if is_sample:
                            col = 0 if d == 0 else L - 1
                            for c in range(2):
                                tgt = (A_re, A_im)[c]; add = h0a[d][c]
                                self.V(lambda e, tgt=tgt, add=add, col=col, i=i: e.tensor_tensor(out=tgt[:, col:col + 1], in0=tgt[:, col:col + 1], in1=add[:, i:i + 1], op=ALU.add),
                                       reads=[tgt, add], writes=[tgt])
                        for l in range(nl):
                            s = 1 << l
                            cr, ci, nci = pw[d][l]
                            src_re, src_im = bufs[cur]; dst_re, dst_im = bufs[1 - cur]
                            def v3(t, a, b):
                                return t[:, 0:glen].rearrange("p (n l) -> p n l", l=L)[:, :, a:b]
                            if d == 0:
                                o_sl, sh_sl, keep = (s, L), (0, L - s), (0, s)
                            else:
                                o_sl, sh_sl, keep = (0, L - s), (s, L), (L - s, L)
                            self.A(lambda e, dst_re=dst_re, src_re=src_re, keep=keep: e.copy(v3(dst_re, *keep), v3(src_re, *keep)), reads=[src_re], partial=[dst_re])
                            self.A(lambda e, dst_im=dst_im, src_im=src_im, keep=keep: e.copy(v3(dst_im, *keep), v3(src_im, *keep)), reads=[src_im], partial=[dst_im])
                            self.V(lambda e, dst_re=dst_re, src_re=src_re, o_sl=o_sl, sh_sl=sh_sl, cr=cr, i=i: e.scalar_tensor_tensor(
                                out=v3(dst_re, *o_sl), in0=v3(src_re, *sh_sl), scalar=cr[:, i:i + 1], in1=v3(src_re, *o_sl), op0=ALU.mult, op1=ALU.add),
                                reads=[src_re, cr], partial=[dst_re])
                            self.V(lambda e, dst_re=dst_re, src_im=src_im, o_sl=o_sl, sh_sl=sh_sl, nci=nci, i=i: e.scalar_tensor_tensor(
                                out=v3(dst_re, *o_sl), in0=v3(src_im, *sh_sl), scalar=nci[:, i:i + 1], in1=v3(dst_re, *o_sl), op0=ALU.mult, op1=ALU.add),
                                reads=[src_im, nci, dst_re], partial=[dst_re])
                            self.V(lambda e, dst_im=dst_im, src_re=src_re, src_im=src_im, o_sl=o_sl, sh_sl=sh_sl, ci=ci, i=i: e.scalar_tensor_tensor(
                                out=v3(dst_im, *o_sl), in0=v3(src_re, *sh_sl), scalar=ci[:, i:i + 1], in1=v3(src_im, *o_sl), op0=ALU.mult, op1=ALU.add),
                                reads=[src_re, src_im, ci], partial=[dst_im])
                            self.V(lambda e, dst_im=dst_im, src_im=src_im, o_sl=o_sl, sh_sl=sh_sl, cr=cr, i=i: e.scalar_tensor_tensor(
                                out=v3(dst_im, *o_sl), in0=v3(src_im, *sh_sl), scalar=cr[:, i:i + 1], in1=v3(dst_im, *o_sl), op0=ALU.mult, op1=ALU.add),
                                reads=[src_im, cr, dst_im], partial=[dst_im])
                            cur = 1 - cur
                        H_re, H_im = bufs[cur]
                        if not is_sample:
                            for j in range(nseq):
                                col = j * L + (L - 1 if d == 0 else 0)
                                for c in range(2):
                                    idx = ((i * 2 + d) * NPR + j) * 2 + c
                                    src = (H_re, H_im)[c]
                                    self.V(lambda e, src=src, col=col, idx=idx: e.tensor_copy(fin[:, idx:idx + 1], src[:, col:col + 1]), reads=[src], partial=[fin])
                        ctr, ncti = cts[d]
                        for b0 in range(0, glen, 512):
                            bl = min(512, glen - b0)
                            p1 = self.ps.next()
                            self.mm(p1, p1[0:32, 0:bl], ctr[:, i, :], H_re[:, b0:b0 + bl], reads=[ctr, H_re], start=True, stop=False)
                            self.mm(p1, p1[0:32, 0:bl], ncti[:, i, :], H_im[:, b0:b0 + bl], reads=[ncti, H_im], start=False, stop=True)
                            if d == 0:
                                self.ev(yacc[:, g0 + b0:g0 + b0 + bl], p1[0:32, 0:bl], reads=[p1], partial=[yacc])
                            else:
                                self.V(lambda e, p1=p1, b0=b0, bl=bl, g0=g0: e.tensor_tensor(out=yacc[:, g0 + b0:g0 + b0 + bl], in0=yacc[:, g0 + b0:g0 + b0 + bl], in1=p1[0:32, 0:bl], op=ALU.add),
                                       reads=[p1, yacc], partial=[yacc])
                self.V(lambda e, i=i: e.scalar_tensor_tensor(out=yacc[:], in0=uT[:], scalar=dT[:, i:i + 1], in1=yacc[:], op0=ALU.mult, op1=ALU.add), reads=[uT, dT, yacc], writes=[yacc])
                self.A(lambda e: e.activation(out=tg[:], in_=yacc[:], func=AF.Square), reads=[yacc], writes=[tg])
                self.V(lambda e: e.tensor_scalar(out=tg[:], in0=tg[:], scalar1=0.044715, scalar2=1.0, op0=ALU.mult, op1=ALU.add), reads=[tg], writes=[tg])
                self.V(lambda e: e.tensor_tensor(out=tg[:], in0=tg[:], in1=yacc[:], op=ALU.mult), reads=[tg, yacc], writes=[tg])
                self.A(lambda e: e.activation(out=tg[:], in_=tg[:], func=AF.Sigmoid, scale=2.0 * math.sqrt(2.0 / math.pi)), reads=[tg], writes=[tg])
                self.V(lambda e: e.tensor_tensor(out=tg[:], in0=tg[:], in1=yacc[:], op=ALU.mult), reads=[tg, yacc], writes=[tg])
                self.store(tg, GYT[32 * i:32 * i + 32, :], tg[:], partial_bufs=[self.dreg("GYT", "all")])
            for i in range(32):
                for d in range(2):
                    for j in range(NPR):
                        for c in range(2):
                            idx = ((i * 2 + d) * NPR + j) * 2 + c
                            o = (o_re, o_im)[c]
                            self.store(fin, o[j, d, 2 * i:2 * i + 2, :].rearrange("g (p o) -> (g p) o", o=1), fin[:, idx:idx + 1])
            self.S.barrier()
            self.S.release([uT.b, tg.b, fin.b, dT.b] + [b.b for b in btr.items] + [x.b for pr in cts for x in pr])

    def glu(self):
        NT = self.NT
        w = self.din("s5_glu_w", [1024, 1024]); gb = self.din("s5_glu_bT", [128, 8])
        GYT, SGT = self.scr["GYT"], self.scr["SGT"]
        MIXT = self.dscr("MIXT", [4096, NT])
        GYv = GYT.rearrange("(k p) t -> p k t", p=128)

        def loader(si, t0, TS, dst):
            for kq in range(0, 8, 4):
                self.load(dst, dst[:, kq:kq + 4, :], GYv[:, kq:kq + 4, t0:t0 + TS], src_bufs=[self.dreg("GYT", "all")], partial=(kq > 0))

        with ExitStack() as st:
            gbt = self.sb(st, "glu_b", [128, 8])
            self.load(gbt, gbt[:], gb[:, :])
            orr = self.ring(st, "glu_o", [128, 512], 2); g1r = self.ring(st, "glu_g1", [128, 512], 2); g2r = self.ring(st, "glu_g2", [128, 512], 2)

            def epi(ps, mc, col0, t0, TS):
                o = orr.next(); g1 = g1r.next(); g2 = g2r.next()
                cb = col0 // 128
                self.A(lambda e: e.activation(out=o[:, 0:TS], in_=ps[:, 0:TS], func=AF.Sigmoid, bias=gbt[:, cb:cb + 1]), reads=[ps, gbt], writes=[o])
                self.load(g1, g1[:, 0:TS], GYT[col0:col0 + 128, t0:t0 + TS], src_bufs=[self.dreg("GYT", "all")])
                self.load(g2, g2[:, 0:TS], SGT[col0:col0 + 128, t0:t0 + TS], src_bufs=[self.dreg("SGT", "all")])
                self.A(lambda e: e.activation(out=g2[:, 0:TS], in_=g2[:, 0:TS], func=AF.Silu), reads=[g2], writes=[g2])
                self.V(lambda e: e.tensor_tensor(out=o[:, 0:TS], in0=o[:, 0:TS], in1=g1[:, 0:TS], op=ALU.mult), reads=[o, g1], writes=[o])
                self.V(lambda e: e.tensor_tensor(out=o[:, 0:TS], in0=o[:, 0:TS], in1=g2[:, 0:TS], op=ALU.mult), reads=[o, g2], writes=[o])
                self.store(o, MIXT[col0:col0 + 128, t0:t0 + TS], o[:, 0:TS], partial_bufs=[self.dreg("MIXT", "all")])

            self.lin("glu", 1024, w, [(0, 1024, "F", epi)], 512, 512, loader)
            self.S.barrier()
            self.S.release([gbt.b] + [x.b for r in (orr, g1r, g2r) for x in r.items])


    def final_norm(self):
        NT = self.NT
        fw = self.din("final_norm_w", [D])
        y = self.dout("y", [NT, D])
        X = self.scr["X"]
        with ExitStack() as st:
            fwb = self.sb(st, "fn_w", [128, D])
            self.load(fwb, fwb[:], fw.partition_broadcast(128))
            xr = self.ring(st, "fn_x", [128, D], 2)
            xn = self.sb(st, "fn_xn", [128, D])
            ss = self.sb(st, "fn_ss", [128, 1]); rs = self.sb(st, "fn_rs", [128, 1])
            for ti in range(NT // 128):
                t0 = ti * 128
                xt = xr.next()
                self.load(xt, xt[:], X[t0:t0 + 128, :], src_bufs=[self.dreg("X", ti)])
                self.A(lambda e: e.activation(out=xn[:], in_=xt[:], func=AF.Square, accum_out=ss[:, 0:1]), reads=[xt], writes=[xn, ss])
                self.V(lambda e: e.tensor_scalar(out=rs[:], in0=ss[:], scalar1=1.0 / D, scalar2=EPS, op0=ALU.mult, op1=ALU.add), reads=[ss], writes=[rs])
                self.A(lambda e: e.activation(out=rs[:], in_=rs[:], func=AF.Sqrt), reads=[rs], writes=[rs])
                self.V(lambda e: e.reciprocal(rs[:], rs[:]), reads=[rs], writes=[rs])
                self.V(lambda e: e.scalar_tensor_tensor(out=xt[:], in0=xt[:], scalar=rs[:, 0:1], in1=fwb[:], op0=ALU.mult, op1=ALU.mult), reads=[xt, rs, fwb], writes=[xt])
                self.store(xt, y[t0:t0 + 128, :], xt[:])
            self.S.barrier()
            self.S.release([fwb.b] + [x.b for x in xr.items])


    def gla(self):
        NT, LS, NPR = self.NT, self.LS, self.NPR
        dup_d = self.din("gla_dup", [2, 64, GDKW])
        nw_d = self.din("gla_nw", [GDVW])
        s0_d = self.din("gla_s0", [2, GH, GDK, GDV])
        o_gla = self.dout("new_gla", [NPR, 2, GH, GDK, GDV])
        QT, KT, DLT = self.scr["QT"], self.scr["KT"], self.scr["DLT"]
        Ktm, Vtm, GGtm = self.scr["Ktm"], self.scr["Vtm"], self.scr["GGtm"]
        OACC = self.dscr("OACC", [NT, GDVW])
        MIXT = self.scr["MIXT"]
        QTv = QT.rearrange("(k p) t -> p k t", p=128); KTv = KT.rearrange("(k p) t -> p k t", p=128)
        MXv = MIXT[1024:4096, :].rearrange("(k p) t -> p k t", p=128)
        with ExitStack() as st:
            dup = self.sb(st, "gl_dup", [64, 2, GDKW])
            for d in range(2):
                self.load(dup, dup[:, d, :], dup_d[d], partial=(d > 0))
            nw = self.sb(st, "gl_nw", [128, GDVW])
            self.load(nw, nw[:], nw_d.partition_broadcast(128))
            blk = self.sb(st, "gl_blk", [128, 128])
            self.V(lambda e: e.memset(blk[:], 0.0), writes=[blk])
            self.V(lambda e: e.memset(blk[0:64, 0:64], 1.0), writes=[blk])
            self.V(lambda e: e.memset(blk[64:128, 64:128], 1.0), writes=[blk])
            tri2 = [self.sb(st, f"gl_tri{d}", [128, 128]) for d in range(2)]
            for d in range(2):
                self.V(lambda e, d=d: e.tensor_tensor(out=tri2[d][:], in0=self.m_incl[d][:], in1=blk[:], op=ALU.mult), reads=[self.m_incl[d], blk], writes=[tri2[d]])
            dla_r = self.ring(st, "gl_dla", [64, 128], 2)
            for dla in dla_r.items:
                self.V(lambda e, dla=dla: e.memset(dla[:], 0.0), writes=[dla])
                self.V(lambda e, dla=dla: e.memset(dla[32:33, :], 1.0), writes=[dla])
            qT_r = self.ring(st, "gl_qT", [128, 12, 128], 2); kT_r = self.ring(st, "gl_kT", [128, 12, 128], 2)
            ktm_r = self.ring(st, "gl_ktm", [128, GDKW], 2)
            vtm = self.sb(st, "gl_vtm", [128, GDVW]); gg = self.sb(st, "gl_gg", [128, GDVW]); oacc = self.sb(st, "gl_oacc", [128, GDVW])
            la = self.sb(st, "gl_la", [128, GDKW]); bsb = self.sb(st, "gl_b", [128, GDKW]); kend = self.sb(st, "gl_kend", [128, GDKW])
            eb = self.sb(st, "gl_eb", [128, 12, 128]); qdec = self.sb(st, "gl_qdec", [128, 12, 128]); kinv = self.sb(st, "gl_kinv", [128, 12, 128])
            att = self.ring(st, "gl_att", [128, 64], 2)
            S = [self.sb(st, f"gl_S{h}", [128, 2, GDV]) for h in range(GH)]
            ssq = self.sb(st, "gl_ssq", [128, GH]); oT_r = self.ring(st, "gl_oT", [128, 4, 128], 2)
            for d in range(2):
                for (q0, L, kind) in self.seqs:
                    for h in range(GH):
                        if kind == 0:
                            self.load(S[h], S[h][:], s0_d[d, h].rearrange("(k p) v -> p k v", p=128))
                        else:
                            self.V(lambda e, h=h: e.memset(S[h][:], 0.0), writes=[S[h]])
                    nb = L // 128
                    order = range(nb) if d == 0 else range(nb - 1, -1, -1)
                    for bi in order:
                        t0 = q0 + bi * 128
                        dla = dla_r.next(); qT = qT_r.next(); kT = kT_r.next(); ktm = ktm_r.next()
                        self.load(dla, dla[0:32, :], DLT[:, t0:t0 + 128], src_bufs=[self.dreg("DLT", "all")], partial=True)
                        for kq in range(0, 12, 4):
                            self.load(qT, qT[:, kq:kq + 4, :], QTv[:, kq:kq + 4, t0:t0 + 128], src_bufs=[self.dreg("QT", "all")], partial=(kq > 0))
                            self.load(kT, kT[:, kq:kq + 4, :], KTv[:, kq:kq + 4, t0:t0 + 128], src_bufs=[self.dreg("KT", "all")], partial=(kq > 0))
                        self.load(ktm, ktm[:], Ktm[t0:t0 + 128, :], src_bufs=[self.dreg("Ktm", "all")])
                        self.load(vtm, vtm[:], Vtm[t0:t0 + 128, :], src_bufs=[self.dreg("Vtm", "all")])
                        if d == 1:
                            self.load(gg, gg[:], GGtm[t0:t0 + 128, :], src_bufs=[self.dreg("GGtm", "all")])
                            self.load(oacc, oacc[:], OACC[t0:t0 + 128, :], src_bufs=[self.dreg("OACC", t0 // 128)])
                        for c3 in range(3):
                            ps = self.ps.next()
                            self.mm(ps, ps[:, :], dla[:, :], dup[:, d, c3 * 512:(c3 + 1) * 512], reads=[dla, dup], start=True, stop=True)
                            sl = slice(c3 * 512, (c3 + 1) * 512)
                            self.A(lambda e, ps=ps, sl=sl: e.activation(out=la[:, sl], in_=ps[:, :], func=AF.Exp, scale=-1.0), reads=[ps], partial=[la])
                        self.A(lambda e: e.activation(out=la[:], in_=la[:], func=AF.Ln, bias=1.0), reads=[la], writes=[la])
                        self.V(lambda e: e.tensor_scalar(out=la[:], in0=la[:], scalar1=-1.0 / 16.0, scalar2=-1.0, op0=ALU.mult, op1=ALU.max), reads=[la], writes=[la])
                        for c3 in range(3):
                            sl = slice(c3 * 512, (c3 + 1) * 512)
                            p1 = self.ps.next(); p2 = self.ps.next()
                            self.mm(p1, p1[:, :], tri2[d][:, :], la[:, sl], reads=[tri2[d], la], start=True, stop=True)
                            self.mm(p2, p2[:, :], blk[:, :], la[:, sl], reads=[blk, la], start=True, stop=True)
                            self.A(lambda e, p1=p1, sl=sl: e.copy(bsb[:, sl], p1[:, :]), reads=[p1], partial=[bsb])
                            self.V(lambda e, p2=p2, sl=sl: e.tensor_tensor(out=kend[:, sl], in0=p2[:, :], in1=bsb[:, sl], op=ALU.subtract), reads=[p2, bsb], partial=[kend])
                        self.A(lambda e: e.activation(out=kend[:], in_=kend[:], func=AF.Exp), reads=[kend], writes=[kend])
                        self.V(lambda e, ktm=ktm: e.tensor_tensor(out=kend[:], in0=kend[:], in1=ktm[:], op=ALU.mult), reads=[kend, ktm], writes=[kend])
                        for k4 in range(3):
                            ps = self.ps.next()
                            for j in range(4):
                                kb = k4 * 4 + j
                                self.mm(ps, ps[:, j * 128:(j + 1) * 128], la[:, kb * 128:(kb + 1) * 128], tri2[d][:, :], reads=[la, tri2[d]], start=True, stop=True, first=(j == 0))
                            self.A(lambda e, ps=ps, k4=k4: e.activation(out=eb[:, k4 * 4:(k4 + 1) * 4, :], in_=ps[:, :].rearrange("p (k t) -> p k t", t=128), func=AF.Exp), reads=[ps], partial=[eb])
                            self.A(lambda e, ps=ps, k4=k4: e.activation(out=kinv[:, k4 * 4:(k4 + 1) * 4, :], in_=ps[:, :].rearrange("p (k t) -> p k t", t=128), func=AF.Exp, scale=-1.0), reads=[ps], partial=[kinv])
                        self.V(lambda e, qT=qT: e.scalar_tensor_tensor(out=qdec[:], in0=qT[:], scalar=1.0 / 16.0, in1=eb[:], op0=ALU.mult, op1=ALU.mult), reads=[qT, eb], writes=[qdec])
                        self.V(lambda e, kT=kT: e.tensor_tensor(out=kinv[:], in0=kinv[:], in1=kT[:], op=ALU.mult), reads=[kinv, kT], writes=[kinv])
                        for h in range(GH):
                            ps_a = self.ps.next(); ps_o = self.ps.next()
                            at = att.next()
                            for ci, c in enumerate((0, 1) if d == 0 else (1, 0)):
                                cs = slice(64 * c, 64 * c + 64)
                                col = 64 * c + (63 if d == 0 else 0)
                                for j in range(2):
                                    kb = 2 * h + j
                                    self.mm(ps_a, ps_a[cs, 0:64], kinv[:, kb, cs], qdec[:, kb, cs], reads=[kinv, qdec], start=(j == 0), stop=(j == 1), first=(j == 0 and ci == 0))
                                self.V(lambda e, ps_a=ps_a, at=at, cs=cs, d=d: e.tensor_tensor(out=at[cs, :], in0=ps_a[cs, 0:64], in1=self.m_incl[d][cs, cs], op=ALU.mult),
                                       reads=[ps_a, self.m_incl[d]], partial=[at])
                                self.mm(ps_o, ps_o[cs, :], at[cs, :], vtm[cs, h * GDV:(h + 1) * GDV], reads=[at, vtm], start=True, stop=False, first=(ci == 0))
                                for j in range(2):
                                    kb = 2 * h + j
                                    self.mm(ps_o, ps_o[cs, :], qdec[:, kb, cs], S[h][:, j, :], reads=[qdec, S[h]], start=False, stop=(j == 1))
                                for j in range(2):
                                    kb = 2 * h + j
                                    ps_s = self.ps.next()
                                    self.mm(ps_s, ps_s[:, :], kend[cs, kb * 128:(kb + 1) * 128], vtm[cs, h * GDV:(h + 1) * GDV], reads=[kend, vtm], start=True, stop=True)
                                    self.V(lambda e, ps_s=ps_s, h=h, j=j, kb=kb, col=col: e.scalar_tensor_tensor(out=S[h][:, j, :], in0=S[h][:, j, :], scalar=eb[:, kb, col:col + 1], in1=ps_s[:, :], op0=ALU.mult, op1=ALU.add),
                                           reads=[S[h], eb, ps_s], partial=[S[h]])
                            hs = slice(h * GDV, (h + 1) * GDV)
                            if d == 0:
                                self.ev(oacc[:, hs], ps_o[:, :], reads=[ps_o], partial=[oacc])
                            else:
                                self.V(lambda e, ps_o=ps_o, hs=hs: e.tensor_tensor(out=oacc[:, hs], in0=oacc[:, hs], in1=ps_o[:, :], op=ALU.add), reads=[ps_o, oacc], partial=[oacc])
                        if d == 0:
                            self.store(oacc, OACC[t0:t0 + 128, :], oacc[:], dst_bufs=[self.dreg("OACC", t0 // 128)])
                        else:
                            for h in range(GH):
                                hs = slice(h * GDV, (h + 1) * GDV)
                                self.A(lambda e, h=h, hs=hs: e.activation(out=la[:, 0:GDV], in_=oacc[:, hs], func=AF.Square, accum_out=ssq[:, h:h + 1]), reads=[oacc], partial=[la, ssq])
                            self.V(lambda e: e.tensor_scalar(out=ssq[:], in0=ssq[:], scalar1=1.0 / GDV, scalar2=EPS, op0=ALU.mult, op1=ALU.add), reads=[ssq, la], writes=[ssq])
                            self.A(lambda e: e.activation(out=ssq[:], in_=ssq[:], func=AF.Sqrt), reads=[ssq], writes=[ssq])
                            self.V(lambda e: e.reciprocal(ssq[:], ssq[:]), reads=[ssq], writes=[ssq])
                            self.A(lambda e: e.activation(out=gg[:], in_=gg[:], func=AF.Silu), reads=[gg], writes=[gg])
                            for h in range(GH):
                                hs = slice(h * GDV, (h + 1) * GDV)
                                self.V(lambda e, h=h, hs=hs: e.scalar_tensor_tensor(out=oacc[:, hs], in0=oacc[:, hs], scalar=ssq[:, h:h + 1], in1=nw[:, hs], op0=ALU.mult, op1=ALU.mult),
                                       reads=[oacc, ssq, nw], partial=[oacc])
                            self.V(lambda e: e.tensor_tensor(out=oacc[:], in0=oacc[:], in1=gg[:], op=ALU.mult), reads=[oacc, gg], writes=[oacc])
                            for k4 in range(6):
                                ps = self.ps.next(); oT = oT_r.next()
                                for j in range(4):
                                    kb = k4 * 4 + j
                                    self.tr(ps, ps[:, j * 128:(j + 1) * 128], oacc[:, kb * 128:(kb + 1) * 128], reads=[oacc])
                                self.ev(oT[:], ps[:, :].rearrange("p (k t) -> p k t", t=128), reads=[ps], writes=[oT])
                                self.store(oT, MXv[:, k4 * 4:(k4 + 1) * 4, t0:t0 + 128], oT[:], partial_bufs=[self.dreg("MIXT", "all")])
                    if kind == 1:
                        j = (q0 - LS) // LP
                        for h in range(GH):
                            self.store(S[h], o_gla[j, d, h].rearrange("(k p) v -> p k v", p=128), S[h][:])
            self.S.barrier()
            self.S.release([x.b for x in [dup, nw, vtm, gg, oacc] + S + dla_r.items + qT_r.items + kT_r.items + ktm_r.items + oT_r.items])

    def outproj(self, layer, K, w_name, srcname):
        NT, LS = self.NT, self.LS
        w = self.din(w_name, [K, D])
        X = self.scr["X"]
        SRCv = self.scr[srcname].rearrange("(k p) t -> p k t", p=128)
        KC = K // 128

        def loader(si, t0, TS, dst):
            for kq in range(0, KC, 4):
                self.load(dst, dst[:, kq:kq + 4, :], SRCv[:, kq:kq + 4, t0:t0 + TS], src_bufs=[self.dreg(srcname, "all")], partial=(kq > 0))

        with ExitStack() as st:
            gate = [self.sb(st, f"op{layer}_gate{c}", [128, D]) for c in range(2)]
            gT = self.gateT[layer]
            for c in range(2):
                for k4 in range(4):
                    ps = self.ps.next()
                    for j in range(4):
                        k = k4 * 4 + j
                        self.mm(ps, ps[:, j * 128:(j + 1) * 128], gT[:, k, c:c + 1].to_broadcast([128, 128]), self.ident[:, :], reads=[gT, self.ident], start=True, stop=True, first=(j == 0))
                    self.ev(gate[c][:, k4 * 512:(k4 + 1) * 512], ps[:, :], reads=[ps], partial=[gate[c]])
            xr = self.ring(st, f"op{layer}_x", [128, 512], 3)

            def epi(ps, ncb, col0, tok0):
                c = 0 if tok0 < LS else 1
                xt = xr.next()
                reg = self.dreg("X", (tok0 // 128, col0))
                self.load(xt, xt[:, 0:ncb], X[tok0:tok0 + 128, col0:col0 + ncb], src_bufs=[self.dreg("X", tok0 // 128), reg])
                self.V(lambda e: e.tensor_tensor(out=ps[:, 0:ncb], in0=ps[:, 0:ncb], in1=gate[c][:, col0:col0 + ncb], op=ALU.mult), reads=[gate[c]], writes=[ps])
                self.V(lambda e: e.tensor_tensor(out=xt[:, 0:ncb], in0=xt[:, 0:ncb], in1=ps[:, 0:ncb], op=ALU.add), reads=[ps, xt], writes=[xt])
                self.store(xt, X[tok0:tok0 + 128, col0:col0 + ncb], xt[:, 0:ncb], dst_bufs=[reg], partial_bufs=[self.dreg("X", tok0 // 128)])

            TS, CB = (256, 256) if K > 2048 else (512, 512)
            self.lin(f"op{layer}", K, w, [(0, D, "T", epi)], TS, CB, loader)
            self.S.barrier()
            self.S.release([x.b for x in xr.items])


    def inproj1(self):
        NT = self.NT
        w = self.din("o_w_in", [D, ODD_IN])
        muT = self.din("rwkv_muT", [2, 128, KD])
        HTv = self.scr["HT"].rearrange("(k p) t -> p k t", p=128)
        names = ["Rtm", "K1tm", "V1tm", "Gtm"]
        for nm in names:
            self.dscr(nm, [NT, D])
        WAT = self.dscr("WAT", [384, NT])
        starts = set(q0 for (q0, L, k) in self.seqs)
        ends = set(q0 + L for (q0, L, k) in self.seqs)
        with ExitStack() as st:
            mu = self.sb(st, "ip1_mu", [128, 2, KD]); c0 = self.sb(st, "ip1_c0", [128, KD])
            self.load(mu, mu[:, 0, :], muT[0]); self.load(mu, mu[:, 1, :], muT[1], partial=True)
            self.V(lambda e: e.tensor_tensor(out=c0[:], in0=mu[:, 0, :], in1=mu[:, 1, :], op=ALU.add), reads=[mu], writes=[c0])
            self.V(lambda e: e.tensor_scalar(out=c0[:], in0=c0[:], scalar1=-1.0, scalar2=1.0, op0=ALU.mult, op1=ALU.add), reads=[c0], writes=[c0])
            TS = 256
            hp = self.sb(st, "ip1_hp", [128, KD, TS]); hn = self.sb(st, "ip1_hn", [128, KD, TS])
            src_all = [self.dreg("HT", "all")]

            def loader(si, t0, TS, dst):
                for kq in range(0, KD, 4):
                    ks = slice(kq, kq + 4)
                    self.load(dst, dst[:, ks, :], HTv[:, ks, t0:t0 + TS], src_bufs=src_all, partial=(kq > 0))
                    if t0 > 0:
                        self.load(hp, hp[:, ks, :], HTv[:, ks, t0 - 1:t0 + TS - 1], src_bufs=src_all, partial=(kq > 0))
                    else:
                        self.load(hp, hp[:, ks, 1:TS], HTv[:, ks, 0:TS - 1], src_bufs=src_all, partial=(kq > 0))
                    if t0 + TS < NT:
                        self.load(hn, hn[:, ks, :], HTv[:, ks, t0 + 1:t0 + TS + 1], src_bufs=src_all, partial=(kq > 0))
                    else:
                        self.load(hn, hn[:, ks, 0:TS - 1], HTv[:, ks, t0 + 1:t0 + TS], src_bufs=src_all, partial=(kq > 0))
                for j in range(TS):
                    if (t0 + j) in starts:
                        self.V(lambda e, j=j: e.memset(hp[:, :, j:j + 1], 0.0), reads=[hp], writes=[hp])
                    if (t0 + j + 1) in ends:
                        self.V(lambda e, j=j: e.memset(hn[:, :, j:j + 1], 0.0), reads=[hn], writes=[hn])
                for k in range(KD):
                    self.V(lambda e, k=k: e.tensor_scalar(out=dst[:, k, :], in0=dst[:, k, :], scalar1=c0[:, k:k + 1], scalar2=None, op0=ALU.mult), reads=[dst, c0], writes=[dst])
                    self.V(lambda e, k=k: e.scalar_tensor_tensor(out=dst[:, k, :], in0=hp[:, k, :], scalar=mu[:, 0, k:k + 1], in1=dst[:, k, :], op0=ALU.mult, op1=ALU.add), reads=[hp, mu, dst], writes=[dst])
                    self.V(lambda e, k=k: e.scalar_tensor_tensor(out=dst[:, k, :], in0=hn[:, k, :], scalar=mu[:, 1, k:k + 1], in1=dst[:, k, :], op0=ALU.mult, op1=ALU.add), reads=[hn, mu, dst], writes=[dst])

            orr = self.ring(st, "ip1_o", [128, 512], 4)

            def epiT(ps, ncb, col0, tok0):
                o = orr.next()
                self.ev(o[:, 0:ncb], ps[:, 0:ncb], reads=[ps], writes=[o])
                nm = names[col0 // D]
                cc = col0 % D
                self.store(o, self.scr[nm][tok0:tok0 + 128, cc:cc + ncb], o[:, 0:ncb], partial_bufs=[self.dreg(nm, "all")])

            def epiF(ps, mc, col0, t0, TS):
                o = orr.next()
                self.ev(o[0:mc, 0:TS], ps[0:mc, 0:TS], reads=[ps], writes=[o])
                r0 = col0 - 4 * D
                self.store(o, WAT[r0:r0 + mc, t0:t0 + TS], o[0:mc, 0:TS], partial_bufs=[self.dreg("WAT", "all")])

            self.lin("ip1", D, w, [(0, 4 * D, "T", epiT), (4 * D, ODD_IN, "F", epiF)], TS, 512, loader)
            self.S.barrier()
            self.S.release([mu.b, hp.b, hn.b] + [o.b for o in orr.items])

    def rwkv_prep(self):
        NT = self.NT
        w2a = self.din("rwkv_w2a", [2, 97, D]); a2a = self.din("rwkv_a2a", [2, 97, D])
        kk_d = self.din("rwkv_k_k", [D]); ka_d = self.din("rwkv_k_a", [D]); rk_d = self.din("rwkv_r_k", [D])
        for nm in ["LW0", "LW1", "KD0", "KD1", "BV0", "BV1", "AV", "BONUS"]:
            self.dscr(nm, [NT, D])
        WAT = self.scr["WAT"]
        with ExitStack() as st:
            w2 = self.sb(st, "rp_w2", [97, 2, D]); a2 = self.sb(st, "rp_a2", [97, 2, D])
            for d in range(2):
                self.load(w2, w2[:, d, :], w2a[d], partial=(d > 0)); self.load(a2, a2[:, d, :], a2a[d], partial=(d > 0))
            kkb = self.sb(st, "rp_kk", [128, D]); kab = self.sb(st, "rp_ka", [128, D]); rkb = self.sb(st, "rp_rk", [128, D])
            self.load(kkb, kkb[:], kk_d.partition_broadcast(128)); self.load(kab, kab[:], ka_d.partition_broadcast(128)); self.load(rkb, rkb[:], rk_d.partition_broadcast(128))
            wl = [self.sb(st, f"rp_wl{d}", [97, 128]) for d in range(2)]; al = [self.sb(st, f"rp_al{d}", [97, 128]) for d in range(2)]
            for t in wl + al:
                self.V(lambda e, t=t: e.memset(t[96:97, :], 1.0), writes=[t])
            r = self.sb(st, "rp_r", [128, D]); k = self.sb(st, "rp_k", [128, D]); v = self.sb(st, "rp_v", [128, D])
            kkn = self.sb(st, "rp_kkn", [128, D]); t1 = self.sb(st, "rp_t1", [128, D]); t2 = self.sb(st, "rp_t2", [128, D]); t3 = self.sb(st, "rp_t3", [128, D])
            ssq = self.sb(st, "rp_ssq", [128, RH]); bs = self.sb(st, "rp_bs", [128, 2, RH])

            def h3(t):
                return t[:].rearrange("p (h j) -> p h j", j=RN)

            def bc(t):
                return t.unsqueeze(2).to_broadcast([128, RH, RN])

            for ti in range(NT // 128):
                t0 = ti * 128
                ts = slice(t0, t0 + 128)
                self.load(r, r[:], self.scr["Rtm"][ts, :], src_bufs=[self.dreg("Rtm", "all")])
                self.load(k, k[:], self.scr["K1tm"][ts, :], src_bufs=[self.dreg("K1tm", "all")])
                self.load(v, v[:], self.scr["V1tm"][ts, :], src_bufs=[self.dreg("V1tm", "all")])
                for d in range(2):
                    self.load(wl[d], wl[d][0:96, :], WAT[96 * d:96 * d + 96, ts], src_bufs=[self.dreg("WAT", "all")], partial=True)
                    self.load(al[d], al[d][0:96, :], WAT[192 + 96 * d:192 + 96 * d + 96, ts], src_bufs=[self.dreg("WAT", "all")], partial=True)
                    self.A(lambda e, d=d: e.activation(out=wl[d][0:96, :], in_=wl[d][0:96, :], func=AF.Tanh), reads=[wl[d]], partial=[wl[d]])
                self.V(lambda e: e.tensor_tensor(out=kkn[:], in0=k[:], in1=kkb[:], op=ALU.mult), reads=[k, kkb], writes=[kkn])
                self.A(lambda e: e.activation(out=t1[:], in_=kkn[:], func=AF.Square), reads=[kkn], writes=[t1])
                self.V(lambda e: e.tensor_reduce(out=ssq[:], in_=h3(t1), axis=AX.X, op=ALU.add), reads=[t1], writes=[ssq])
                self.A(lambda e: e.activation(out=ssq[:], in_=ssq[:], func=AF.Sqrt), reads=[ssq], writes=[ssq])
                self.V(lambda e: e.tensor_scalar(out=ssq[:], in0=ssq[:], scalar1=1e-12, scalar2=None, op0=ALU.max), reads=[ssq], writes=[ssq])
                self.V(lambda e: e.reciprocal(ssq[:], ssq[:]), reads=[ssq], writes=[ssq])
                self.V(lambda e: e.tensor_tensor(out=h3(kkn), in0=h3(kkn), in1=bc(ssq[:]), op=ALU.mult), reads=[kkn, ssq], writes=[kkn])
                self.V(lambda e: e.tensor_scalar(out=t1[:], in0=kkn[:], scalar1=-1.0, scalar2=None, op0=ALU.mult), reads=[kkn], writes=[t1])
                self.store(t1, self.scr["AV"][ts, :], t1[:], partial_bufs=[self.dreg("AV", "all")])
                self.V(lambda e: e.tensor_tensor(out=r[:], in0=r[:], in1=rkb[:], op=ALU.mult), reads=[r, rkb], writes=[r])
                for d in range(2):
                    pw_ = [self.ps.next() for _ in range(4)]
                    for q in range(4):
                        self.mm(pw_[q], pw_[q][:, :], wl[d][:, :], w2[:, d, q * 512:(q + 1) * 512], reads=[wl[d], w2], start=True, stop=True)
                        qs = slice(q * 512, (q + 1) * 512)
                        self.A(lambda e, q=q, qs=qs: e.activation(out=t2[:, qs], in_=pw_[q][:, :], func=AF.Exp, scale=-1.0), reads=[pw_[q]], partial=[t2])
                    self.A(lambda e: e.activation(out=t2[:], in_=t2[:], func=AF.Ln, bias=1.0), reads=[t2], writes=[t2])
                    self.A(lambda e: e.activation(out=t2[:], in_=t2[:], func=AF.Exp, scale=-1.0, bias=-0.5), reads=[t2], writes=[t2])
                    self.V(lambda e: e.tensor_scalar(out=t2[:], in0=t2[:], scalar1=-1.0, scalar2=None, op0=ALU.mult), reads=[t2], writes=[t2])
                    self.store(t2, self.scr[f"LW{d}"][ts, :], t2[:], partial_bufs=[self.dreg(f"LW{d}", "all")])
                    pa_ = [self.ps.next() for _ in range(4)]
                    for q in range(4):
                        self.mm(pa_[q], pa_[q][:, :], al[d][:, :], a2[:, d, q * 512:(q + 1) * 512], reads=[al[d], a2], start=True, stop=True)
                        qs = slice(q * 512, (q + 1) * 512)
                        self.A(lambda e, q=q, qs=qs: e.activation(out=t3[:, qs], in_=pa_[q][:, :], func=AF.Sigmoid), reads=[pa_[q]], partial=[t3])
                    self.V(lambda e: e.tensor_tensor(out=t1[:], in0=kkn[:], in1=t3[:], op=ALU.mult), reads=[kkn, t3], writes=[t1])
                    self.store(t1, self.scr[f"BV{d}"][ts, :], t1[:], partial_bufs=[self.dreg(f"BV{d}", "all")])
                    self.V(lambda e: e.scalar_tensor_tensor(out=t3[:], in0=t3[:], scalar=-1.0, in1=kab[:], op0=ALU.add, op1=ALU.mult), reads=[t3, kab], writes=[t3])
                    self.V(lambda e: e.scalar_tensor_tensor(out=t3[:], in0=t3[:], scalar=1.0, in1=k[:], op0=ALU.add, op1=ALU.mult), reads=[t3, k], writes=[t3])
                    self.store(t3, self.scr[f"KD{d}"][ts, :], t3[:], partial_bufs=[self.dreg(f"KD{d}", "all")])
                    self.V(lambda e: e.tensor_tensor(out=t2[:], in0=t3[:], in1=r[:], op=ALU.mult), reads=[t3, r], writes=[t2])
                    self.V(lambda e, d=d: e.tensor_reduce(out=bs[:, d, :], in_=h3(t2), axis=AX.X, op=ALU.add), reads=[t2], partial=[bs])
                self.V(lambda e: e.tensor_tensor(out=bs[:, 0, :], in0=bs[:, 0, :], in1=bs[:, 1, :], op=ALU.add), reads=[bs], writes=[bs])
                self.V(lambda e: e.tensor_tensor(out=h3(t1), in0=h3(v), in1=bc(bs[:, 0, :]), op=ALU.mult), reads=[v, bs], writes=[t1])
                self.store(t1, self.scr["BONUS"][ts, :], t1[:], partial_bufs=[self.dreg("BONUS", "all")])
            self.S.barrier()
            self.S.release([x.b for x in [w2, a2, kkb, kab, rkb, r, k, v, t1, t2, t3] + wl + al])


    def rwkv(self):
        NT, LS, NPR = self.NT, self.LS, self.NPR
        s0T = self.din("rwkv_s0T", [2, 16, 128, RN])
        lw_d = self.din("rwkv_lnx_w", [D]); lb_d = self.din("rwkv_lnx_b", [D])
        import os
        RW = int(os.environ.get("RW_STOP", "99"))
        o_st = self.dout("new_rwkv", [NPR, 2, RH, RN, RN])
        WACC = self.dscr("WACC", [NT, D]); MIX1T = self.dscr("MIX1T", [D, NT])
        M1v = MIX1T.rearrange("(k p) t -> p k t", p=128)
        sc = self.scr
        with ExitStack() as st:
            lnw = self.sb(st, "rw_lnw", [128, D]); lnb = self.sb(st, "rw_lnb", [128, D])
            self.load(lnw, lnw[:], lw_d.partition_broadcast(128)); self.load(lnb, lnb[:], lb_d.partition_broadcast(128))
            MK = [self.sb(st, f"rw_mk{d}", [128, 512]) for d in range(2)]
            for d in range(2):
                for q in range(4):
                    m = self.m_strict[d] if q % 2 == 0 else self.m_incl[d]
                    self.V(lambda e, d=d, q=q, m=m: e.tensor_copy(MK[d][:, q * 128:(q + 1) * 128], m[:]), reads=[m], partial=[MK[d]])
            lw = self.sb(st, "rw_lw", [128, D]); kd = self.sb(st, "rw_kd", [128, D]); bv = self.sb(st, "rw_bv", [128, D])
            r = self.sb(st, "rw_r", [128, D]); v = self.sb(st, "rw_v", [128, D]); av = self.sb(st, "rw_av", [128, D])
            bt = self.sb(st, "rw_bt", [128, D]); kt = self.sb(st, "rw_kt", [128, D])
            x1 = self.sb(st, "rw_x1", [128, D]); x2 = self.sb(st, "rw_x2", [128, D])
            AR = self.sb(st, "rw_AR", [128, 16, 2, 128]); BK = self.sb(st, "rw_BK", [128, 16, 2, 128])
            gCT = self.sb(st, "rw_gCT", [128, 16])
            ST = [self.sb(st, f"rw_ST{p}", [128, RN]) for p in range(16)]
            AMr = None
            Xr = None
            Pr = None
            PPr = None
            XRr = None
            slot_rings = []
            for sl_ in range(3):
                slot_rings.append({"AM": self.ring(st, f"rw_sAM{sl_}", [128, 512], 2), "XR": self.ring(st, f"rw_sXR{sl_}", [128, 256], 1),
                                   "X": self.ring(st, f"rw_sX{sl_}", [128, 256], 4), "PP": self.ring(st, f"rw_sPP{sl_}", [128, 256], 2),
                                   "P": self.ring(st, f"rw_sP{sl_}", [128, 128], 1), "WU": self.ring(st, f"rw_sWU{sl_}", [128, 128], 1)})
            BD32 = self.sb(st, "rw_bd32", [128, 128]); BD64 = self.sb(st, "rw_bd64", [128, 128])
            OFF64 = self.sb(st, "rw_off64", [128, 128]); OFF128 = self.sb(st, "rw_off128", [128, 128])
            self.V(lambda e: e.memset(BD32[:], 0.0), writes=[BD32]); self.V(lambda e: e.memset(BD64[:], 0.0), writes=[BD64])
            for i_ in range(4):
                self.V(lambda e, i_=i_: e.memset(BD32[32 * i_:32 * i_ + 32, 32 * i_:32 * i_ + 32], 1.0), writes=[BD32])
            for i_ in range(2):
                self.V(lambda e, i_=i_: e.memset(BD64[64 * i_:64 * i_ + 64, 64 * i_:64 * i_ + 64], 1.0), writes=[BD64])
            self.V(lambda e: e.tensor_tensor(out=OFF64[:], in0=BD64[:], in1=BD32[:], op=ALU.subtract), reads=[BD64, BD32], writes=[OFF64])
            self.V(lambda e: e.tensor_scalar(out=OFF128[:], in0=BD64[:], scalar1=-1.0, scalar2=1.0, op0=ALU.mult, op1=ALU.add), reads=[BD64], writes=[OFF128])
            WUr = None
            wkv = self.sb(st, "rw_wkv", [128, D])
            sq = self.sb(st, "rw_sq", [128, RH]); mean = self.sb(st, "rw_mean", [128, RH])
            oTr = self.ring(st, "rw_oT", [128, 4, 128], 2)
            stT = self.ring(st, "rw_stT", [64, 128], 2)

            def h3(t):
                return t[:].rearrange("p (h j) -> p h j", j=RN)

            def bc(a):
                return a.unsqueeze(2).to_broadcast([128, RH, RN])

            for d in range(2):
                for (q0, L, kind) in self.seqs:
                    for p in range(16):
                        if kind == 0:
                            self.load(ST[p], ST[p][:], s0T[d, p])
                        else:
                            self.V(lambda e, p=p: e.memset(ST[p][:], 0.0), writes=[ST[p]])
                    nb = L // 128
                    for bi in ((range(nb) if d == 0 else range(nb - 1, -1, -1)) if RW >= 0 else []):
                        t0 = q0 + bi * 128
                        ts = slice(t0, t0 + 128)
                        for (tb, nm) in [(lw, f"LW{d}"), (kd, f"KD{d}"), (bv, f"BV{d}"), (r, "Rtm"), (v, "V1tm"), (av, "AV")]:
                            self.load(tb, tb[:], sc[nm][ts, :], src_bufs=[self.dreg(nm, "all")])
                        if RW < 1:
                            continue
                        cps = [self.ps.next() for _ in range(4)]
                        tps = [self.ps.next() for _ in range(4)]
                        for q in range(4):
                            qs = slice(q * 512, (q + 1) * 512)
                            self.mm(cps[q], cps[q][:, :], self.m_incl[d][:, :], lw[:, qs], reads=[self.m_incl[d], lw], start=True, stop=True)
                            self.mm(tps[q], tps[q][:, :], self.ones[:, :], lw[:, qs], reads=[self.ones, lw], start=True, stop=True)
                        RS = int(os.environ.get("RW_SUB", "9"))
                        if RS < 1:
                            continue
                        for q in range(4):
                            qs = slice(q * 512, (q + 1) * 512)
                            self.A(lambda e, q=q, qs=qs: e.activation(out=x1[:, qs], in_=cps[q][:, :], func=AF.Exp), reads=[cps[q]], partial=[x1])
                            self.V(lambda e, q=q, qs=qs: e.tensor_tensor(out=x2[:, qs], in0=cps[q][:, :], in1=lw[:, qs], op=ALU.subtract), reads=[cps[q], lw, x1], partial=[x2])
                        if RS < 2:
                            continue
                        self.V(lambda e: e.tensor_tensor(out=r[:], in0=r[:], in1=x1[:], op=ALU.mult), reads=[r, x1], writes=[r])
                        self.A(lambda e: e.activation(out=x2[:], in_=x2[:], func=AF.Exp), reads=[x2], writes=[x2])
                        self.V(lambda e: e.tensor_tensor(out=av[:], in0=av[:], in1=x2[:], op=ALU.mult), reads=[av, x2], writes=[av])
                        for q in range(4):
                            qs = slice(q * 512, (q + 1) * 512)
                            self.A(lambda e, q=q, qs=qs: e.activation(out=x1[:, qs], in_=cps[q][:, :], func=AF.Exp, scale=-1.0), reads=[cps[q], r], partial=[x1])
                            self.V(lambda e, q=q, qs=qs: e.tensor_copy(x2[:, qs], cps[q][:, :]), reads=[cps[q], av, x1], partial=[x2])
                        for q in range(4):
                            qs = slice(q * 512, (q + 1) * 512)
                            self.V(lambda e, q=q, qs=qs: e.tensor_tensor(out=x2[:, qs], in0=tps[q][:, :], in1=x2[:, qs], op=ALU.subtract), reads=[tps[q], x2], partial=[x2])
                        if RS < 3:
                            continue
                        self.V(lambda e: e.tensor_tensor(out=bt[:], in0=bv[:], in1=x1[:], op=ALU.mult), reads=[bv, x1], writes=[bt])
                        self.V(lambda e: e.tensor_tensor(out=kt[:], in0=kd[:], in1=x1[:], op=ALU.mult), reads=[kd, x1], writes=[kt])
                        self.A(lambda e: e.activation(out=x2[:], in_=x2[:], func=AF.Exp), reads=[x2], writes=[x2])
                        self.V(lambda e: e.tensor_tensor(out=bv[:], in0=bv[:], in1=x2[:], op=ALU.mult), reads=[bv, x2, bt], writes=[bv])
                        self.V(lambda e: e.tensor_tensor(out=kd[:], in0=kd[:], in1=x2[:], op=ALU.mult), reads=[kd, x2, kt], writes=[kd])
                        if RW == 1:
                            continue
                        gps = self.ps.next()
                        for p in range(16):
                            self.mm(gps, gps[:, p:p + 1], lw[:, p * 128:(p + 1) * 128], self.ones[:, 0:1], reads=[lw, self.ones], start=True, stop=True, first=(p == 0))
                        self.A(lambda e, gps=gps: e.activation(out=gCT[:], in_=gps[:, 0:16], func=AF.Exp), reads=[gps], writes=[gCT])
                        if RW <= 2:
                            continue
                        for (src, dst, idx) in [(av, AR, 0), (r, AR, 1), (bt, BK, 0), (kt, BK, 1)]:
                            for p4 in range(4):
                                ps = self.ps.next()
                                for j in range(4):
                                    p = p4 * 4 + j
                                    self.tr(ps, ps[:, j * 128:(j + 1) * 128], src[:, p * 128:(p + 1) * 128], reads=[src])
                                self.ev(dst[:, p4 * 4:(p4 + 1) * 4, idx, :], ps[:, :].rearrange("p (k t) -> p k t", t=128), reads=[ps], partial=[dst])
                        psY = None
                        full_ring = self.ps
                        ybank = full_ring.items[6:8]
                        def head_gen(h, slot):
                            myps = Ring(full_ring.items[2 * slot:2 * slot + 2])
                            R_ = slot_rings[slot]
                            p, hf = h // 2, h % 2
                            rows = slice(64 * hf, 64 * hf + 64)
                            hc = slice(h * RN, (h + 1) * RN)
                            ps1 = myps.next(); ps2 = myps.next()
                            arv = AR[rows, p, :, :].rearrange("p a t -> p (a t)")
                            self.mm(ps1, ps1[:, 0:256], BK[rows, p, 0, :], arv, reads=[BK, AR], start=True, stop=True)
                            self.mm(ps1, ps1[:, 256:512], BK[rows, p, 1, :], arv, reads=[BK, AR], start=True, stop=True, first=False)
                            self.mm(ps2, ps2[:, 0:128], AR[rows, p, 0, :], BK[rows, p, 0, :], reads=[BK, AR], start=True, stop=True)
                            AM = R_['AM'].next(); XR = R_['XR'].next(); XX = R_['X'].next(); PP = R_['PP'].next()
                            self.V(lambda e, ps1=ps1, AM=AM, d=d: e.tensor_tensor(out=AM[:], in0=ps1[:, :], in1=MK[d][:], op=ALU.mult), reads=[ps1, MK[d]], writes=[AM])
                            self.V(lambda e, XR=XR, AM=AM: e.tensor_copy(XR[:, 0:128], AM[:, 0:128]), reads=[AM], partial=[XR])
                            self.V(lambda e, ps2=ps2, XR=XR, d=d: e.tensor_tensor(out=XR[:, 128:256], in0=ps2[:, 0:128], in1=self.m_strict[1 - d][:], op=ALU.mult), reads=[ps2, self.m_strict[1 - d]], partial=[XR])
                            self.V(lambda e, XX=XX, XR=XR: e.tensor_tensor(out=XX[:].rearrange("p (a t) -> p a t", a=2), in0=XR[:].rearrange("p (a t) -> p a t", a=2),
                                                                              in1=BD32[:].unsqueeze(1).to_broadcast([128, 2, 128]), op=ALU.mult), reads=[XR, BD32], writes=[XX])
                            self.V(lambda e, XX=XX, PP=PP: e.tensor_tensor(out=PP[:].rearrange("p (a t) -> p a t", a=2), in0=XX[:].rearrange("p (a t) -> p a t", a=2),
                                                                              in1=self.ident[:].unsqueeze(1).to_broadcast([128, 2, 128]), op=ALU.add), reads=[XX, self.ident], writes=[PP])
                            yield
                            for l in range(1, 5):
                                ps3 = myps.next()
                                XN = R_['X'].next()
                                if l < 4:
                                    self.mm(ps3, ps3[:, 0:128], XX[:, 128:256], XX[:, 0:128], reads=[XX], start=True, stop=True)
                                    self.mm(ps3, ps3[:, 128:256], XX[:, 0:128], XX[:, 128:256], reads=[XX], start=True, stop=True, first=False)
                                    self.A(lambda e, ps3=ps3, XN=XN: e.copy(XN[:], ps3[:, 0:256]), reads=[ps3], writes=[XN])
                                else:
                                    self.mm(ps3, ps3[:, 128:256], XX[:, 0:128], XX[:, 128:256], reads=[XX], start=True, stop=True)
                                    self.A(lambda e, ps3=ps3, XN=XN: e.copy(XN[:, 128:256], ps3[:, 128:256]), reads=[ps3], writes=[XN])
                                yield
                                self.mm(ps3, ps3[:, 256:384], XN[:, 128:256], PP[:, 0:128], reads=[XN, PP], start=True, stop=True, first=False)
                                self.mm(ps3, ps3[:, 384:512], PP[:, 0:128], XN[:, 128:256], reads=[XN, PP], start=True, stop=True, first=False)
                                PN = R_['PP'].next()
                                self.V(lambda e, ps3=ps3, PP=PP, PN=PN: e.tensor_tensor(out=PN[:], in0=PP[:], in1=ps3[:, 256:512], op=ALU.add), reads=[ps3, PP], writes=[PN])
                                XX, PP = XN, PN
                                yield
                            NO = R_['X'].next(); TT = R_['X'].next()
                            self.V(lambda e, NO=NO, XR=XR: e.tensor_tensor(out=NO[:].rearrange("p (a t) -> p a t", a=2), in0=XR[:].rearrange("p (a t) -> p a t", a=2),
                                                                              in1=OFF64[:].unsqueeze(1).to_broadcast([128, 2, 128]), op=ALU.mult), reads=[XR, OFF64], writes=[NO])
                            ps3 = myps.next()
                            self.mm(ps3, ps3[:, 0:128], NO[:, 128:256], PP[:, 0:128], reads=[NO, PP], start=True, stop=True)
                            self.mm(ps3, ps3[:, 128:256], NO[:, 0:128], PP[:, 128:256], reads=[NO, PP], start=True, stop=True, first=False)
                            self.A(lambda e, ps3=ps3, TT=TT: e.copy(TT[:], ps3[:, 0:256]), reads=[ps3], writes=[TT])
                            yield
                            self.mm(ps3, ps3[:, 256:384], PP[:, 128:256], TT[:, 0:128], reads=[TT, PP], start=True, stop=True, first=False)
                            self.mm(ps3, ps3[:, 384:512], PP[:, 0:128], TT[:, 128:256], reads=[TT, PP], start=True, stop=True, first=False)
                            PN = R_['PP'].next()
                            self.V(lambda e, ps3=ps3, PP=PP, PN=PN: e.tensor_tensor(out=PN[:], in0=PP[:], in1=ps3[:, 256:512], op=ALU.add), reads=[ps3, PP], writes=[PN])
                            PP = PN
                            yield
                            NO = R_['X'].next(); TT = R_['X'].next()
                            self.V(lambda e, NO=NO, XR=XR: e.tensor_tensor(out=NO[:, 128:256], in0=XR[:, 128:256], in1=OFF128[:], op=ALU.mult), reads=[XR, OFF128], writes=[NO])
                            ps3 = myps.next()
                            self.mm(ps3, ps3[:, 0:128], NO[:, 128:256], PP[:, 0:128], reads=[NO, PP], start=True, stop=True)
                            self.A(lambda e, ps3=ps3, TT=TT: e.copy(TT[:, 0:128], ps3[:, 0:128]), reads=[ps3], writes=[TT])
                            yield
                            self.mm(ps3, ps3[:, 128:256], PP[:, 128:256], TT[:, 0:128], reads=[TT, PP], start=True, stop=True, first=False)
                            P = R_['P'].next()
                            self.V(lambda e, ps3=ps3, PP=PP, P=P: e.tensor_tensor(out=P[:], in0=PP[:, 0:128], in1=ps3[:, 128:256], op=ALU.add), reads=[ps3, PP], writes=[P])
                            yield
                            ps5 = myps.next(); WU = R_['WU'].next()
                            self.mm(ps5, ps5[:, 0:64], AR[rows, p, 0, :], ST[p][rows, :], reads=[AR, ST[p]], start=True, stop=False)
                            self.mm(ps5, ps5[:, 0:64], AM[:, 256:384], v[:, hc], reads=[AM, v], start=False, stop=True)
                            self.A(lambda e, ps5=ps5, WU=WU: e.copy(WU[:, 0:64], ps5[:, 0:64]), reads=[ps5], partial=[WU])
                            yield
                            self.mm(ps5, ps5[:, 64:128], P[:, :], WU[:, 0:64], reads=[P, WU], start=True, stop=True, first=False)
                            self.A(lambda e, ps5=ps5, WU=WU: e.copy(WU[:, 64:128], ps5[:, 64:128]), reads=[ps5], partial=[WU])
                            yield
                            psY = ybank[(h // 8) % 2]
                            yc = slice((h % 8) * 64, (h % 8) * 64 + 64)
                            self.mm(psY, psY[:, yc], AR[rows, p, 1, :], ST[p][rows, :], reads=[AR, ST[p]], start=True, stop=False, first=(h % 8 == 0))
                            self.mm(psY, psY[:, yc], AM[:, 128:256], WU[:, 64:128], reads=[AM, WU], start=False, stop=False)
                            self.mm(psY, psY[:, yc], AM[:, 384:512], v[:, hc], reads=[AM, v], start=False, stop=True)
                            ps6 = myps.next()
                            self.mm(ps6, ps6[rows, 0:64], bv[:, hc], WU[:, 64:128], reads=[bv, WU], start=True, stop=False)
                            self.mm(ps6, ps6[rows, 0:64], kd[:, hc], v[:, hc], reads=[kd, v], start=False, stop=True)
                            self.V(lambda e, ps6=ps6, p=p, rows=rows: e.scalar_tensor_tensor(out=ST[p][rows, :], in0=ST[p][rows, :], scalar=gCT[rows, p:p + 1], in1=ps6[rows, 0:64], op0=ALU.mult, op1=ALU.add),
                                   reads=[ST[p], gCT, ps6], partial=[ST[p]])
                            if h % 8 == 7:
                                q = h // 8
                                qs = slice(q * 512, (q + 1) * 512)
                                self.ev(wkv[:, qs], psY[:, :], reads=[psY], partial=[wkv])
                        GI = 3
                        for g0 in range(0, RH if RW >= 4 else 0, GI):
                            gens = [head_gen(h, h - g0) for h in range(g0, min(RH, g0 + GI))]
                            while gens:
                                for g in list(gens):
                                    try:
                                        next(g)
                                    except StopIteration:
                                        gens.remove(g)
                        self.ps = full_ring
                        if d == 0:
                            self.store(wkv, WACC[ts, :], wkv[:], dst_bufs=[self.dreg("WACC", t0 // 128)])
                        else:
                            self.load(x1, x1[:], WACC[ts, :], src_bufs=[self.dreg("WACC", t0 // 128)])
                            self.load(x2, x2[:], sc["BONUS"][ts, :], src_bufs=[self.dreg("BONUS", "all")])
                            self.load(lw, lw[:], sc["Gtm"][ts, :], src_bufs=[self.dreg("Gtm", "all")])
                            self.V(lambda e: e.tensor_tensor(out=wkv[:], in0=wkv[:], in1=x1[:], op=ALU.add), reads=[wkv, x1], writes=[wkv])
                            self.V(lambda e: e.tensor_reduce(out=mean[:], in_=h3(wkv), axis=AX.X, op=ALU.add), reads=[wkv], writes=[mean])
                            self.V(lambda e: e.tensor_scalar(out=mean[:], in0=mean[:], scalar1=1.0 / RN, scalar2=None, op0=ALU.mult), reads=[mean], writes=[mean])
                            self.V(lambda e: e.tensor_tensor(out=h3(wkv), in0=h3(wkv), in1=bc(mean[:]), op=ALU.subtract), reads=[wkv, mean], writes=[wkv])
                            self.A(lambda e: e.activation(out=x1[:], in_=wkv[:], func=AF.Square), reads=[wkv], writes=[x1])
                            self.V(lambda e: e.tensor_reduce(out=sq[:], in_=h3(x1), axis=AX.X, op=ALU.add), reads=[x1], writes=[sq])
                            self.V(lambda e: e.tensor_scalar(out=sq[:], in0=sq[:], scalar1=1.0 / RN, scalar2=LNX_EPS, op0=ALU.mult, op1=ALU.add), reads=[sq], writes=[sq])
                            self.A(lambda e: e.activation(out=sq[:], in_=sq[:], func=AF.Sqrt), reads=[sq], writes=[sq])
                            self.V(lambda e: e.reciprocal(sq[:], sq[:]), reads=[sq], writes=[sq])
                            self.V(lambda e: e.tensor_tensor(out=h3(wkv), in0=h3(wkv), in1=bc(sq[:]), op=ALU.mult), reads=[wkv, sq], writes=[wkv])
                            self.V(lambda e: e.tensor_tensor(out=wkv[:], in0=wkv[:], in1=lnw[:], op=ALU.mult), reads=[wkv, lnw], writes=[wkv])
                            self.V(lambda e: e.tensor_tensor(out=wkv[:], in0=wkv[:], in1=lnb[:], op=ALU.add), reads=[wkv, lnb], writes=[wkv])
                            self.V(lambda e: e.tensor_tensor(out=wkv[:], in0=wkv[:], in1=x2[:], op=ALU.add), reads=[wkv, x2], writes=[wkv])
                            self.A(lambda e: e.activation(out=lw[:], in_=lw[:], func=AF.Silu), reads=[lw], writes=[lw])
                            self.V(lambda e: e.tensor_tensor(out=wkv[:], in0=wkv[:], in1=lw[:], op=ALU.mult), reads=[wkv, lw], writes=[wkv])
                            for k4 in range(4):
                                ps = self.ps.next(); oT = oTr.next()
                                for j in range(4):
                                    kb = k4 * 4 + j
                                    self.tr(ps, ps[:, j * 128:(j + 1) * 128], wkv[:, kb * 128:(kb + 1) * 128], reads=[wkv])
                                self.ev(oT[:], ps[:, :].rearrange("p (k t) -> p k t", t=128), reads=[ps], writes=[oT])
                                self.store(oT, M1v[:, k4 * 4:(k4 + 1) * 4, ts], oT[:], partial_bufs=[self.dreg("MIX1T", "all")])
                    if kind == 1 and RW != -2:
                        jq = (q0 - LS) // LP
                        for p in range(16):
                            ps = self.ps.next(); sT = stT.next()
                            self.tr(ps, ps[0:64, 0:128], ST[p][:, :], reads=[ST[p]])
                            self.ev(sT[:, :], ps[0:64, 0:128], reads=[ps], writes=[sT])
                            for hf in range(2):
                                self.store(sT, o_st[jq, d, 2 * p + hf], sT[:, hf * 64:(hf + 1) * 64])
            self.S.barrier()
            self.S.release([x.b for x in [lnw, lnb, lw, kd, bv, r, v, av, wkv, x1, x2] + ST + oTr.items + stT.items])


def build_program(LS, NPR, upto="all", debug=()):
    P = Prog(LS, NPR, debug)
    nc = P.nc
    NT = P.NT
    P.din("xs", [LS, D]); P.din("xp", [NPR * LP, D])
    P.dscr("X", [NT, D]); P.dscr("HT", [D, NT])
    if upto == "rwonly":
        with ExitStack() as pst:
            P.pst = pst
            P.setup(pst)
            for nm in ["LW0", "LW1", "KD0", "KD1", "BV0", "BV1", "AV", "BONUS", "Rtm", "V1tm", "Gtm"]:
                P.scr[nm] = P.din(nm, [NT, D])
            P.rwkv()
            with nc.Block() as block:
                P.S.finish(block)
        return P
    with ExitStack() as pst:
        P.pst = pst
        P.setup(pst)
        P.adaln()
        if upto != "ada":
            P.prologue(0)
        if upto not in ("p0", "ada"):
            P.inproj0()
        if upto not in ("p0", "ada", "ip0"):
            P.s5()
            P.glu()
        if upto not in ("p0", "ada", "ip0", "s5"):
            P.gla()
            P.outproj(0, 4096, "e_w_out", "MIXT")
        if upto not in ("p0", "ada", "ip0", "s5", "l0"):
            P.prologue(1)
            P.inproj1()
            P.rwkv_prep()
        if upto not in ("p0", "ada", "ip0", "s5", "l0", "prep"):
            P.rwkv()
            P.outproj(1, 2048, "o_w_out", "MIX1T")
        if upto == "all":
            P.final_norm()
        if upto in ("ada", "p0"):
            dbg = P.dout("dbg_ada", [128, 6 * KD * 2])
            for i, t in enumerate([P.shiftT[0], P.multT[0], P.gateT[0], P.shiftT[1], P.multT[1], P.gateT[1]]):
                P.store(t, dbg[:, i * 32:(i + 1) * 32], t[:].rearrange("p k c -> p (k c)"))
        with nc.Block() as block:
            P.S.finish(block)
    return P


def _f32(a):
    return np.ascontiguousarray(np.asarray(a, dtype=np.float32))


def prep_core_inputs(inp, b, prompt_ids, names):
    out = {}

    def fm(v, nchunk):
        return _f32(np.asarray(v).reshape(nchunk, 128).T)

    for n in names:
        if n == "xs":
            out[n] = _f32(inp["x_sample"][b])
        elif n == "xp":
            out[n] = _f32(np.concatenate([inp["x_prompt"][j] for j in prompt_ids], axis=0))
        elif n == "condT":
            c2 = np.stack([np.asarray(inp["c"][b]), np.asarray(inp["c_ctx"])], axis=-1)
            out[n] = _f32(c2.reshape(KD, 128, 2).transpose(1, 0, 2))
        elif n == "ada_w":
            out[n] = _f32(inp["ada_w"])
        elif n == "ada_bT":
            out[n] = _f32(np.stack([fm(inp["ada_b"][l], 48) for l in range(2)]))
        elif n == "norm_wT":
            out[n] = _f32(np.stack([fm(inp["norm_w"][l], KD) for l in range(2)]))
        elif n == "e_w_in":
            out[n] = _f32(inp["e_w_in"][0])
        elif n == "s5_lamC":
            o = np.zeros((2, 128, 3, 32), np.float32)
            for d in range(2):
                for ci, key in enumerate(["s5_lambda_re", "s5_lambda_im"]):
                    a = np.asarray(inp[key][0, d])
                    o[d, :, ci, :] = a.reshape(32, 128).T
                ls = np.repeat(np.asarray(inp["s5_log_step"][0, d])[:, None], P_S5, axis=1)
                o[d, :, 2, :] = ls.reshape(32, 128).T
            out[n] = o
        elif n == "s5_bT":
            o = np.zeros((2, 2, 32, 32, 128), np.float32)
            for d in range(2):
                for ci, key in enumerate(["s5_b_re", "s5_b_im"]):
                    a = np.asarray(inp[key][0, d])
                    for gl in range(2):
                        blk = a[gl::2]
                        o[d, ci, gl * 16:(gl + 1) * 16, :, gl * 64:(gl + 1) * 64] = blk.transpose(2, 0, 1)
            out[n] = o
        elif n == "s5_cT":
            o = np.zeros((2, 2, 128, 32, 32), np.float32)
            for d in range(2):
                for ci, key in enumerate(["s5_c_re", "s5_c_im"]):
                    a = np.asarray(inp[key][0, d])
                    for gl in range(2):
                        blk = a[gl::2]
                        o[d, ci, gl * 64:(gl + 1) * 64, :, gl * 16:(gl + 1) * 16] = blk.transpose(2, 0, 1)
            out[n] = o
        elif n == "s5_h0C":
            o = np.zeros((2, 2, 128, 32), np.float32)
            for d in range(2):
                for ci, key in enumerate(["state_s5_re", "state_s5_im"]):
                    a = np.asarray(inp[key][b, 0, d])
                    o[d, ci] = a.reshape(32, 128).T
            out[n] = o
        elif n == "s5_dT":
            out[n] = _f32(np.asarray(inp["s5_d"][0]).reshape(32, 32).T)
        elif n == "gla_dup":
            o = np.zeros((2, 64, GDKW), np.float32)
            for d in range(2):
                o[d, 16 * d:16 * d + 16] = np.asarray(inp["gla_decay_up"][0, d])
                o[d, 32] = np.asarray(inp["gla_decay_b"][0, d])
            out[n] = o
        elif n == "gla_nw":
            out[n] = _f32(inp["gla_norm_w"][0])
        elif n == "gla_s0":
            out[n] = _f32(inp["state_gla"][b, 0])
        elif n == "e_w_out":
            out[n] = _f32(inp["e_w_out"][0])
        elif n == "o_w_in":
            out[n] = _f32(inp["o_w_in"][0])
        elif n == "o_w_out":
            out[n] = _f32(inp["o_w_out"][0])
        elif n == "rwkv_muT":
            out[n] = _f32(np.stack([fm(inp["rwkv_mu"][0, i], KD) for i in range(2)]))
        elif n == "rwkv_w2a":
            out[n] = _f32(np.stack([np.concatenate([inp["rwkv_w2"][0, d], np.asarray(inp["rwkv_w0"][0, d])[None]], 0) for d in range(2)]))
        elif n == "rwkv_a2a":
            out[n] = _f32(np.stack([np.concatenate([inp["rwkv_a2"][0, d], np.asarray(inp["rwkv_a0"][0, d])[None]], 0) for d in range(2)]))
        elif n in ("rwkv_k_k", "rwkv_k_a", "rwkv_lnx_w", "rwkv_lnx_b"):
            out[n] = _f32(inp[n][0])
        elif n == "rwkv_r_k":
            out[n] = _f32(np.asarray(inp[n][0]).reshape(-1))
        elif n == "rwkv_s0T":
            a = np.asarray(inp["state_rwkv"][b, 0])
            out[n] = _f32(a.reshape(2, 16, 2, 64, 64).transpose(0, 1, 2, 4, 3).reshape(2, 16, 128, 64))
        elif n == "final_norm_w":
            out[n] = _f32(inp["final_norm_w"])
        elif n == "s5_glu_w":
            out[n] = _f32(inp["s5_glu_w"][0])
        elif n == "s5_glu_bT":
            out[n] = fm(inp["s5_glu_b"][0], 8)
        else:
            raise KeyError(n)
    return out


_PROG = {}


def kernel(**inputs):
    LS, NPR = 4096, 4
    inp = {k: np.asarray(v) for k, v in inputs.items()}
    if "prog" not in _PROG:
        _PROG["prog"] = build_program(LS, NPR, upto="all")
    P = _PROG["prog"]
    names = list(P.inp.keys())
    per_core = []
    for core in range(8):
        b = core % 4
        per_core.append(prep_core_inputs(inp, b, [4 * b + j for j in range(NPR)], names))
    res = run_bass_kernel_spmd(P.nc, per_core, core_ids=list(range(8)))
    B, Bd = 16, 4
    y_prompt = np.zeros((B, LP, D), np.float32)
    y_sample = np.zeros((Bd, LS, D), np.float32)
    new_s5_re = np.zeros((B, 1, 2, G_S5, P_S5), np.float32)
    new_s5_im = np.zeros((B, 1, 2, G_S5, P_S5), np.float32)
    new_gla = np.zeros((B, 1, 2, GH, GDK, GDV), np.float32)
    new_rwkv = np.zeros((B, 1, 2, RH, RN, RN), np.float32)
    for b in range(4):
        r = res.results[b]
        y_sample[b] = r["y"][:LS]
        for j in range(NPR):
            y_prompt[4 * b + j] = r["y"][LS + j * LP:LS + (j + 1) * LP]
            new_s5_re[4 * b + j, 0] = r["new_s5_re"][j]
            new_s5_im[4 * b + j, 0] = r["new_s5_im"][j]
            new_gla[4 * b + j, 0] = r["new_gla"][j]
            new_rwkv[4 * b + j, 0] = r["new_rwkv"][j]
    return (y_prompt, y_sample, new_s5_re, new_s5_im, new_gla, new_rwkv)
```

```python
import math
from contextlib import ExitStack
import numpy as np
import concourse.bass as bass
import concourse.mybir as mybir
from concourse.bass_utils import run_bass_kernel_spmd

F32 = mybir.dt.float32
I32 = mybir.dt.int32
AF = mybir.ActivationFunctionType
ALU = mybir.AluOpType
AX = mybir.AxisListType

D = 2048
KD = D // 128
LP = 256
EPS = 1e-6
S5_W = 1024
G_S5 = 64
P_S5 = 64
H_S5 = 16
GH = 6
GDK = 256
GDV = 512
GDKW = GH * GDK
GDVW = GH * GDV
EVEN_IN = 11296
RH = 32
RN = 64
ODD_IN = 8576
LNX_EPS = 64e-5
TWO_PI = 2.0 * math.pi


class Buf:
    __slots__ = ("name", "writers", "readers", "dma_sem", "dma_total")

    def __init__(self, name):
        self.name = name
        self.writers = {}
        self.readers = {}
        self.dma_sem = None
        self.dma_total = 0


class _Rec:
    def __init__(self):
        self.call = None

    def __getattr__(self, name):
        def f(*a, **kw):
            self.call = (name, a, kw)
            return self
        return f


class Sched:
    COMPUTE = ("tensor", "vector", "scalar", "gpsimd")

    def __init__(self, nc):
        self.nc = nc
        self.ops = {e: [] for e in ("tensor", "vector", "scalar", "gpsimd", "sync")}
        self.sem = {}
        self.count = {}
        self.seen = {e: {} for e in self.ops}
        self.sems = {}
        for e in self.COMPUTE:
            self.sem[e] = ("E", e)
            self.sems[("E", e)] = nc.alloc_semaphore(name=f"prog_{e}")
            self.count[e] = 0
        self.free_sems = []
        self.pending = {e: [] for e in self.ops}
        self.sem_total = {}
        self.nsem = 0
        self.nops = 0

    def _collect(self, E, reads, writes, partial=()):
        need = {}
        for b in reads:
            for k, (v, en) in b.writers.items():
                if en == E and E == "tensor":
                    continue
                if need.get(k, 0) < v:
                    need[k] = v
        for b in writes:
            for k, (v, en) in b.writers.items():
                if en == E:
                    continue
                if need.get(k, 0) < v:
                    need[k] = v
            for k, (v, en) in b.readers.items():
                if en == E:
                    continue
                if need.get(k, 0) < v:
                    need[k] = v
        for b in partial:
            for k, (v, en) in b.readers.items():
                if en == E:
                    continue
                if need.get(k, 0) < v:
                    need[k] = v
        seen = self.seen[E]
        out = []
        for k, v in need.items():
            if seen.get(k, 0) < v:
                seen[k] = v
                out.append((k, v))
        return out

    def _record(self, k, v, en, reads, writes, partial):
        for b in writes:
            b.writers = {k: (v, en)}
            b.readers = {}
        for b in partial:
            b.writers[k] = (v, en)
        for b in reads:
            b.readers[k] = (v, en)

    def barrier(self):
        for E in self.ops:
            seen = self.seen[E]
            for e2 in self.COMPUTE:
                if e2 != E and seen.get(self.sem[e2], 0) < self.count[e2]:
                    seen[self.sem[e2]] = self.count[e2]
                    self.pending[E].append((self.sem[e2], self.count[e2]))
            for k, v in self.sem_total.items():
                if seen.get(k, 0) < v:
                    seen[k] = v
                    self.pending[E].append((k, v))

    def op(self, E, fn, reads=(), writes=(), partial=()):
        waits = self.pending[E] + self._collect(E, reads, writes, partial)
        self.pending[E] = []
        self.count[E] += 1
        rec = _Rec()
        fn(rec)
        self.ops[E].append((waits, rec.call, self.sem[E], 1))
        self._record(self.sem[E], self.count[E], E, reads, writes, partial)
        self.nops += 1

    def dma(self, Q, fn, sb, reads=(), writes=(), partial=()):
        waits = self.pending[Q] + self._collect(Q, reads, writes, partial)
        self.pending[Q] = []
        if sb.dma_sem is None:
            if self.free_sems:
                key, total = self.free_sems.pop()
                sb.dma_sem = key
                sb.dma_total = total
                if self.seen[Q].get(key, 0) < total:
                    self.seen[Q][key] = total
                    waits.append((key, total))
            elif self.nsem < 88:
                key = ("D", self.nsem)
                self.sems[key] = self.nc.alloc_semaphore(name=f"dma_{self.nsem}")
                self.nsem += 1
                sb.dma_sem = key
            else:
                raise RuntimeError("too many dma semaphores")
        sb.dma_total += 16
        self.sem_total[sb.dma_sem] = sb.dma_total
        rec = _Rec()
        fn(rec)
        self.ops[Q].append((waits, rec.call, sb.dma_sem, 16))
        self._record(sb.dma_sem, sb.dma_total, None, reads, writes, partial)
        self.nops += 1

    def release(self, bufs):
        for b in bufs:
            if b.dma_sem is not None:
                self.free_sems.append((b.dma_sem, b.dma_total))
                b.dma_sem = None

    def finish(self, block):
        fin = list(self.sem_total.items())
        sems = self.sems
        ops = self.ops

        def replay(eng, name):
            for waits, fn, semkey, inc in ops[name]:
                for k, v in waits:
                    eng.wait_ge(sems[k], v)
                getattr(eng, fn[0])(*fn[1], **fn[2]).then_inc(sems[semkey], inc)
            if name == "sync":
                for k, v in fin:
                    eng.wait_ge(sems[k], v)

        block.sync(lambda e: replay(e, "sync"))
        block.tensor(lambda e: replay(e, "tensor"))
        block.vector(lambda e: replay(e, "vector"))
        block.scalar(lambda e: replay(e, "scalar"))
        block.gpsimd(lambda e: replay(e, "gpsimd"))


class TB:
    __slots__ = ("t", "b")

    def __init__(self, t, name):
        self.t = t
        self.b = Buf(name)

    def __getitem__(self, k):
        return self.t[k]


class Ring:
    def __init__(self, items):
        self.items = items
        self.i = 0

    def next(self):
        it = self.items[self.i % len(self.items)]
        self.i += 1
        return it


class Prog:
    def __init__(self, LS, NPR, debug=()):
        self.LS = LS
        self.NPR = NPR
        self.NT = LS + NPR * LP
        assert LS % 512 == 0 and self.NT % 512 == 0
        self.debug = set(debug)
        self.nc = bass.Bass("TRN2", target_bir_lowering=False)
        self.S = Sched(self.nc)
        self.inp = {}
        self.outp = {}
        self.scr = {}
        self.dbuf = {}
        self.seqs = [(0, LS, 0)] + [(LS + j * LP, LP, 1) for j in range(NPR)]

    def din(self, name, shape):
        self.inp[name] = self.nc.dram_tensor(name, list(shape), F32, kind="ExternalInput").ap()
        self.dbuf[name] = Buf(name)
        return self.inp[name]

    def dout(self, name, shape):
        self.outp[name] = self.nc.dram_tensor(name, list(shape), F32, kind="ExternalOutput").ap()
        self.dbuf[name] = Buf(name)
        return self.outp[name]

    def dscr(self, name, shape):
        kind = "ExternalOutput" if name in self.debug else "Internal"
        self.scr[name] = self.nc.dram_tensor(name, list(shape), F32, kind=kind).ap()
        self.dbuf[name] = Buf(name)
        return self.scr[name]

    def dreg(self, name, key):
        k = (name, key)
        if k not in self.dbuf:
            self.dbuf[k] = Buf(str(k))
        return self.dbuf[k]

    def sb(self, st, name, shape, dt=F32):
        return TB(st.enter_context(self.nc.sbuf_tensor(name, list(shape), dt)), name)

    def ring(self, st, name, shape, n, dt=F32):
        return Ring([self.sb(st, f"{name}{i}", shape, dt) for i in range(n)])

    def V(self, fn, reads=(), writes=(), partial=()):
        self.S.op("vector", fn, [r.b if isinstance(r, TB) else r for r in reads],
                  [w.b if isinstance(w, TB) else w for w in writes],
                  [w.b if isinstance(w, TB) else w for w in partial])

    def A(self, fn, reads=(), writes=(), partial=()):
        self.S.op("scalar", fn, [r.b if isinstance(r, TB) else r for r in reads],
                  [w.b if isinstance(w, TB) else w for w in writes],
                  [w.b if isinstance(w, TB) else w for w in partial])

    def G(self, fn, reads=(), writes=(), partial=()):
        self.S.op("gpsimd", fn, [r.b if isinstance(r, TB) else r for r in reads],
                  [w.b if isinstance(w, TB) else w for w in writes],
                  [w.b if isinstance(w, TB) else w for w in partial])

    def T(self, fn, reads=(), writes=(), partial=()):
        self.S.op("tensor", fn, [r.b if isinstance(r, TB) else r for r in reads],
                  [w.b if isinstance(w, TB) else w for w in writes],
                  [w.b if isinstance(w, TB) else w for w in partial])

    def load(self, dst, dst_ap, src_ap, src_bufs=(), partial=False):
        if partial:
            self.S.dma("sync", lambda e: e.dma_start(out=dst_ap, in_=src_ap), dst.b,
                       reads=list(src_bufs), partial=[dst.b])
        else:
            self.S.dma("sync", lambda e: e.dma_start(out=dst_ap, in_=src_ap), dst.b,
                       reads=list(src_bufs), writes=[dst.b])

    def store(self, src, dst_ap, src_ap, dst_bufs=(), partial_bufs=()):
        import os
        self.S.dma(os.environ.get("STQ", "gpsimd"), lambda e: e.dma_start(out=dst_ap, in_=src_ap), src.b,
                   reads=[src.b], writes=list(dst_bufs), partial=list(partial_bufs))

    def mm(self, ps, out_ap, lhsT_ap, rhs_ap, reads, start, stop, first=None):
        if first is None:
            first = start
        self.T(lambda e: e.matmul(out_ap, lhsT_ap, rhs_ap, start=start, stop=stop),
               reads=reads, writes=[ps] if first else (), partial=() if first else [ps])

    def tr(self, ps, out_ap, in_ap, reads):
        ident = self.ident
        self.T(lambda e: e.matmul(out_ap, in_ap, ident[:], start=True, stop=True, is_transpose=True), reads=list(reads) + [ident], partial=[ps])

    def ev(self, out_ap, in_ap, reads, writes=(), partial=()):
        self._evc = getattr(self, "_evc", 0) + 1
        if self._evc % 2:
            self.V(lambda e: e.tensor_copy(out_ap, in_ap), reads=reads, writes=writes, partial=partial)
        else:
            self.A(lambda e: e.copy(out_ap, in_ap), reads=reads, writes=writes, partial=partial)

    def setup(self, st):
        nc = self.nc
        self.ps = Ring([TB(st.enter_context(nc.psum_tensor(f"ps{i}", [128, 512], F32)), f"ps{i}") for i in range(8)])
        self.ident = self.sb(st, "ident", [128, 128])
        self.ones = self.sb(st, "ones", [128, 128])
        tmp = self.sb(st, "setup_tmp", [128, 128])
        self.G(lambda e: e.iota(tmp[:], [[1, 128]], base=0, channel_multiplier=-1,
                                allow_small_or_imprecise_dtypes=True), writes=[tmp])
        ident = self.ident
        self.V(lambda e: e.tensor_single_scalar(ident[:], tmp[:], 0.0, op=ALU.is_equal), reads=[tmp], writes=[ident])
        ones = self.ones
        self.V(lambda e: e.memset(ones[:], 1.0), writes=[ones])
        self.m_incl = [self.sb(st, f"m_incl{d}", [128, 128]) for d in range(2)]
        self.m_strict = [self.sb(st, f"m_strict{d}", [128, 128]) for d in range(2)]
        for d in range(2):
            mi, ms = self.m_incl[d], self.m_strict[d]
            op_i = ALU.is_ge if d == 0 else ALU.is_le
            op_s = ALU.is_gt if d == 0 else ALU.is_lt
            self.V(lambda e, mi=mi, op_i=op_i: e.tensor_single_scalar(mi[:], tmp[:], 0.0, op=op_i), reads=[tmp], writes=[mi])
            self.V(lambda e, ms=ms, op_s=op_s: e.tensor_single_scalar(ms[:], tmp[:], 0.0, op=op_s), reads=[tmp], writes=[ms])

    def adaln(self):
        nc = self.nc
        condT = self.din("condT", [128, KD, 2])
        ada_w = self.din("ada_w", [2, D, 3 * D])
        ada_bT = self.din("ada_bT", [2, 128, 48])
        norm_wT = self.din("norm_wT", [2, 128, KD])
        st0 = self.pst
        self.pad0 = self.sb(st0, "pad0", [128, 64])
        self.shiftT = [self.sb(st0, f"shiftT{l}", [128, KD, 2]) for l in range(2)]
        self.multT = [self.sb(st0, f"multT{l}", [128, KD, 2]) for l in range(2)]
        self.gateT = [self.sb(st0, f"gateT{l}", [128, KD, 2]) for l in range(2)]
        with ExitStack() as st:
            sc = self.sb(st, "ada_sc", [128, KD, 2])
            bt = self.sb(st, "ada_bt", [128, 48])
            nw = self.sb(st, "ada_nw", [128, KD])
            mt = self.sb(st, "ada_mt", [128, 48, 2])
            wr = self.ring(st, "ada_w", [128, KD, 512], 2)
            self.load(sc, sc[:], condT[:, :, :])
            self.A(lambda e: e.activation(out=sc[:], in_=sc[:], func=AF.Silu), reads=[sc], writes=[sc])
            for l in range(2):
                self.load(bt, bt[:], ada_bT[l, :, :])
                self.load(nw, nw[:], norm_wT[l, :, :])
                ps = self.ps.next()
                wv = ada_w[l].rearrange("(k p) n -> p k n", p=128)
                for cb in range(12):
                    wt = wr.next()
                    for kq in range(4):
                        self.load(wt, wt[:, kq * 4:(kq + 1) * 4, :], wv[:, kq * 4:(kq + 1) * 4, cb * 512:(cb + 1) * 512], partial=(kq > 0))
                    for m in range(4):
                        nb = cb * 4 + m
                        for k in range(KD):
                            self.mm(ps, ps[:, 2 * nb:2 * nb + 2], wt[:, k, m * 128:(m + 1) * 128], sc[:, k, :],
                                    reads=[wt, sc], start=(k == 0), stop=(k == KD - 1), first=(k == 0 and nb == 0))
                self.V(lambda e, ps=ps: e.tensor_tensor(out=mt[:], in0=ps[:, 0:96].rearrange("p (n c) -> p n c", c=2),
                                                        in1=bt[:].unsqueeze(2).to_broadcast([128, 48, 2]), op=ALU.add),
                       reads=[ps, bt], writes=[mt])
                sh, mu, ga = self.shiftT[l], self.multT[l], self.gateT[l]
                self.V(lambda e, sh=sh: e.tensor_copy(sh[:], mt[:, 0:16, :]), reads=[mt], writes=[sh])
                self.V(lambda e, ga=ga: e.tensor_copy(ga[:], mt[:, 32:48, :]), reads=[mt], writes=[ga])
                self.V(lambda e, mu=mu: e.scalar_tensor_tensor(out=mu[:], in0=mt[:, 16:32, :], scalar=1.0,
                                                                in1=nw[:].unsqueeze(2).to_broadcast([128, KD, 2]),
                                                                op0=ALU.add, op1=ALU.mult),
                       reads=[mt, nw], writes=[mu])
            self.S.barrier()
            self.S.release([sc.b, bt.b, nw.b] + [w.b for w in wr.items])

    def rr_sin(self, st, out_t, ang_t, shape, pfx):
        ni = self.sb(st, pfx + "_ni", shape, I32)
        nf = self.sb(st, pfx + "_nf", shape, F32)
        self.V(lambda e: e.tensor_single_scalar(nf[:], ang_t[:], 1.0 / TWO_PI, op=ALU.mult), reads=[ang_t], writes=[nf])
        self.V(lambda e: e.tensor_copy(ni[:], nf[:]), reads=[nf], writes=[ni])
        self.V(lambda e: e.tensor_copy(nf[:], ni[:]), reads=[ni], writes=[nf])
        self.V(lambda e: e.scalar_tensor_tensor(out=ang_t[:], in0=nf[:], scalar=-TWO_PI, in1=ang_t[:],
                                                op0=ALU.mult, op1=ALU.add), reads=[nf, ang_t], writes=[ang_t])
        self.V(lambda e: e.tensor_scalar(out=ang_t[:], in0=ang_t[:], scalar1=math.pi, scalar2=-math.pi,
                                         op0=ALU.min, op1=ALU.max), reads=[ang_t], writes=[ang_t])
        self.A(lambda e: e.activation(out=out_t[:], in_=ang_t[:], func=AF.Sin), reads=[ang_t], writes=[out_t])

    def posembed(self, st):
        LS = self.LS
        E = self.sb(st, "pe_E", [64, 1024])
        with ExitStack() as s2:
            om = self.sb(s2, "pe_om", [64, 512])
            kk = self.sb(s2, "pe_k", [64, 1])
            a1 = self.sb(s2, "pe_a1", [64, 512])
            a2 = self.sb(s2, "pe_a2", [64, 512])
            o1 = self.sb(s2, "pe_o1", [64, 512])
            self.G(lambda e: e.iota(om[:], [[1, 512]], base=0, channel_multiplier=0, allow_small_or_imprecise_dtypes=True), writes=[om])
            self.G(lambda e: e.iota(kk[:], [[1, 1]], base=0, channel_multiplier=1, allow_small_or_imprecise_dtypes=True), writes=[kk])
            self.A(lambda e: e.activation(out=om[:], in_=om[:], func=AF.Exp, scale=-math.log(10000.0) / 512.0), reads=[om], writes=[om])
            self.V(lambda e: e.tensor_scalar(out=a1[:], in0=om[:], scalar1=kk[:, 0:1], scalar2=None, op0=ALU.mult), reads=[om, kk], writes=[a1])
            self.V(lambda e: e.tensor_scalar(out=a2[:], in0=a1[:], scalar1=math.pi / 2, scalar2=None, op0=ALU.add), reads=[a1], writes=[a2])
            self.rr_sin(s2, o1, a1, [64, 512], "pe_r1")
            self.V(lambda e: e.tensor_copy(E[:, 0:512], o1[:]), reads=[o1], partial=[E])
            self.rr_sin(s2, o1, a2, [64, 512], "pe_r2")
            self.V(lambda e: e.tensor_copy(E[:, 512:1024], o1[:]), reads=[o1], partial=[E])
            self.S.barrier()
        self.pe_E = E
        selrow = self.sb(st, "pe_selrow", [64, LS])
        selcol = self.sb(st, "pe_selcol", [64, 128])
        with ExitStack() as s2:
            v1 = self.sb(s2, "pe_v1", [64, LS])
            m1 = self.sb(s2, "pe_m1", [64, LS])
            self.G(lambda e: e.iota(v1[:], [[1, LS]], base=0, channel_multiplier=-64, allow_small_or_imprecise_dtypes=True), writes=[v1])
            self.V(lambda e: e.tensor_single_scalar(m1[:], v1[:], 0.0, op=ALU.is_ge), reads=[v1], writes=[m1])
            self.V(lambda e: e.scalar_tensor_tensor(out=selrow[:], in0=v1[:], scalar=64.0, in1=m1[:], op0=ALU.is_lt, op1=ALU.mult),
                   reads=[v1, m1], writes=[selrow])
            self.G(lambda e: e.iota(v1[:, 0:128], [[1, 128]], base=0, channel_multiplier=-1, allow_small_or_imprecise_dtypes=True), writes=[v1])
            self.V(lambda e: e.tensor_single_scalar(m1[:, 0:128], v1[:, 0:128], 0.0, op=ALU.is_equal), reads=[v1], writes=[m1])
            self.V(lambda e: e.scalar_tensor_tensor(out=selcol[:], in0=v1[:, 0:128], scalar=64.0, in1=m1[:, 0:128], op0=ALU.is_equal, op1=ALU.add),
                   reads=[v1, m1], writes=[selcol])
            self.S.barrier()
        self.pe_selrow = selrow
        posc = self.sb(st, "pe_posc", [128, 1024])
        for hb in range(2):
            ps = self.ps.next()
            self.mm(ps, ps[:, :], selcol[:, :], E[:, hb * 512:(hb + 1) * 512], reads=[selcol, E], start=True, stop=True)
            self.ev(posc[:, hb * 512:(hb + 1) * 512], ps[:, :], reads=[ps], partial=[posc])
        self.pe_posc = posc

    def prologue(self, layer):
        NT, LS = self.NT, self.LS
        X, HT = self.scr["X"], self.scr["HT"]
        HTv = HT.rearrange("(k p) t -> p k t", p=128)
        with ExitStack() as st:
            if layer == 0:
                self.posembed(st)
            xr = self.ring(st, f"p{layer}_x", [128, D], 2)
            xn = self.sb(st, f"p{layer}_xn", [128, D])
            hr = self.ring(st, f"p{layer}_h", [128, KD, 128], 2)
            ss = self.sb(st, f"p{layer}_ss", [128, 1])
            rs = self.sb(st, f"p{layer}_rs", [128, 1])
            import os
            PS_ = int(os.environ.get("PRO_STOP", "9"))
            for ti in range(NT // 128 if PS_ > 0 else 0):
                t0 = ti * 128
                c = 0 if t0 < LS else 1
                xt = xr.next()
                if layer == 0:
                    src = self.inp["xs"][t0:t0 + 128, :] if c == 0 else self.inp["xp"][t0 - LS:t0 - LS + 128, :]
                    self.load(xt, xt[:], src)
                    if c == 0:
                        posc, E, selrow = self.pe_posc, self.pe_E, self.pe_selrow
                        self.V(lambda e, xt=xt: e.tensor_tensor(out=xt[:, 1024:2048], in0=xt[:, 1024:2048], in1=posc[:], op=ALU.add),
                               reads=[xt, posc], writes=[xt])
                        for hb in range(2):
                            ps = self.ps.next()
                            self.mm(ps, ps[:, :], selrow[:, t0:t0 + 128], E[:, hb * 512:(hb + 1) * 512], reads=[selrow, E], start=True, stop=True)
                            self.V(lambda e, xt=xt, ps=ps, hb=hb: e.tensor_tensor(out=xt[:, hb * 512:(hb + 1) * 512], in0=xt[:, hb * 512:(hb + 1) * 512], in1=ps[:, :], op=ALU.add),
                                   reads=[xt, ps], writes=[xt])
                    self.store(xt, X[t0:t0 + 128, :], xt[:], dst_bufs=[self.dreg("X", ti)])
                else:
                    self.load(xt, xt[:], X[t0:t0 + 128, :], src_bufs=[self.dreg("X", ti)])
                if PS_ < 2:
                    continue
                self.A(lambda e, xt=xt: e.activation(out=xn[:], in_=xt[:], func=AF.Square, accum_out=ss[:, 0:1]), reads=[xt], writes=[xn, ss])
                self.V(lambda e: e.tensor_scalar(out=rs[:], in0=ss[:], scalar1=1.0 / D, scalar2=EPS, op0=ALU.mult, op1=ALU.add), reads=[ss], writes=[rs])
                self.A(lambda e: e.activation(out=rs[:], in_=rs[:], func=AF.Sqrt), reads=[rs], writes=[rs])
                self.V(lambda e: e.reciprocal(rs[:], rs[:]), reads=[rs], writes=[rs])
                self.V(lambda e, xt=xt: e.tensor_scalar(out=xn[:], in0=xt[:], scalar1=rs[:, 0:1], scalar2=None, op0=ALU.mult), reads=[xt, rs], writes=[xn])
                if PS_ < 3:
                    continue
                ht = hr.next()
                mu, sh = self.multT[layer], self.shiftT[layer]
                for kb in range(4):
                    ps = self.ps.next()
                    for j in range(4):
                        k = kb * 4 + j
                        self.tr(ps, ps[:, j * 128:(j + 1) * 128], xn[:, k * 128:(k + 1) * 128], reads=[xn])
                    for j in range(4):
                        k = kb * 4 + j
                        fn = (lambda e, ht=ht, ps=ps, k=k, j=j, c=c: e.tensor_scalar(
                            out=ht[:, k, :], in0=ps[:, j * 128:(j + 1) * 128], scalar1=mu[:, k, c:c + 1], scalar2=sh[:, k, c:c + 1],
                            op0=ALU.mult, op1=ALU.add))
                        EVM = os.environ.get("EVMODE", "2")
                        if EVM == "0":
                            self.ev(ht[:, k, :], ps[:, j * 128:(j + 1) * 128], reads=[ps], partial=[ht])
                        elif j % 2 or EVM == "2":
                            self.V(fn, reads=[ps, mu, sh], partial=[ht])
                        else:
                            fn2 = (lambda e, ht=ht, ps=ps, k=k, j=j, c=c: e.activation(
                                out=ht[:, k, :], in_=ps[:, j * 128:(j + 1) * 128], func=AF.Identity,
                                scale=mu[:, k, c:c + 1], bias=sh[:, k, c:c + 1]))
                            self.A(fn2, reads=[ps, mu, sh], partial=[ht])
                for kq in range(0, KD, 4):
                    self.store(ht, HTv[:, kq:kq + 4, t0:t0 + 128], ht[:, kq:kq + 4, :], partial_bufs=[self.dreg("HT", "all")])
            self.S.barrier()
            self.S.release([x.b for x in xr.items] + [h.b for h in hr.items])

    def lin(self, name, K, W_ap, blocks, TS, CB, src_loader, src_reads_key=None):
        NT = self.NT
        KC = K // 128
        Wv = W_ap.rearrange("(k p) n -> p k n", p=128)
        with ExitStack() as st:
            sr = self.ring(st, f"{name}_src", [128, KC, TS], 2)
            wr = self.ring(st, f"{name}_w", [128, KC, CB], 2)
            self.lin_st = st
            for si in range(NT // TS):
                t0 = si * TS
                src = sr.next()
                src_loader(si, t0, TS, src)
                for (c0, c1, mode, epi) in blocks:
                    for cb0 in range(c0, c1, CB):
                        ncb = min(CB, c1 - cb0)
                        wt = wr.next()
                        for kq in range(0, KC, 4):
                            self.load(wt, wt[:, kq:kq + 4, 0:ncb], Wv[:, kq:kq + 4, cb0:cb0 + ncb], partial=(kq > 0))
                        if mode == "F":
                            for m0 in range(0, ncb, 128):
                                mc = min(128, ncb - m0)
                                ps = self.ps.next()
                                for k in range(KC):
                                    self.mm(ps, ps[0:mc, 0:TS], wt[:, k, m0:m0 + mc], src[:, k, :], reads=[wt, src],
                                            start=(k == 0), stop=(k == KC - 1))
                                epi(ps, mc, cb0 + m0, t0, TS)
                        else:
                            for tt in range(TS // 128):
                                ps = self.ps.next()
                                for k in range(KC):
                                    self.mm(ps, ps[:, 0:ncb], src[:, k, tt * 128:(tt + 1) * 128], wt[:, k, 0:ncb], reads=[wt, src],
                                            start=(k == 0), stop=(k == KC - 1))
                                epi(ps, ncb, cb0, t0 + tt * 128)
            self.S.barrier()
            self.S.release([x.b for x in sr.items] + [w.b for w in wr.items])

    def inproj0(self):
        NT = self.NT
        w = self.din("e_w_in", [D, EVEN_IN])
        UT = self.dscr("UT", [1024, NT]); SGT = self.dscr("SGT", [1024, NT])
        QT = self.dscr("QT", [GDKW, NT]); KT = self.dscr("KT", [GDKW, NT])
        DLT = self.dscr("DLT", [32, NT])
        Ktm = self.dscr("Ktm", [NT, GDKW]); Vtm = self.dscr("Vtm", [NT, GDVW]); GGtm = self.dscr("GGtm", [NT, GDVW])
        HTv = self.scr["HT"].rearrange("(k p) t -> p k t", p=128)

        def loader(si, t0, TS, dst):
            for kq in range(0, KD, 4):
                self.load(dst, dst[:, kq:kq + 4, :], HTv[:, kq:kq + 4, t0:t0 + TS], src_bufs=[self.dreg("HT", "all")], partial=(kq > 0))

        fmap = [(0, 1024, UT, "UT"), (1024, 2048, SGT, "SGT"), (2048, 3584, QT, "QT"), (3584, 5120, KT, "KT"), (11264, 11296, DLT, "DLT")]
        tmap = [(3584, 5120, Ktm, "Ktm"), (5120, 8192, Vtm, "Vtm"), (8192, 11264, GGtm, "GGtm")]

        def epiF(ps, mc, col0, t0, TS):
            o = self.ip_or.next()
            self.ev(o[0:mc, 0:TS], ps[0:mc, 0:TS], reads=[ps], writes=[o])
            for (a, b, ten, nm) in fmap:
                if a <= col0 < b:
                    self.store(o, ten[col0 - a:col0 - a + mc, t0:t0 + TS], o[0:mc, 0:TS], partial_bufs=[self.dreg(nm, "all")])

        def epiT(ps, ncb, col0, tok0):
            o = self.ip_or.next()
            self.ev(o[:, 0:ncb], ps[:, 0:ncb], reads=[ps], writes=[o])
            for (a, b, ten, nm) in tmap:
                if a <= col0 < b:
                    self.store(o, ten[tok0:tok0 + 128, col0 - a:col0 - a + ncb], o[:, 0:ncb], partial_bufs=[self.dreg(nm, "all")])

        with ExitStack() as st:
            self.ip_or = self.ring(st, "ip0_o", [128, 512], 4)
            blocks = [(0, 5120, "F", epiF), (11264, 11296, "F", epiF), (3584, 11264, "T", epiT)]
            self.lin("ip0", D, w, blocks, 512, 512, loader)
            self.S.barrier()
            self.S.release([o.b for o in self.ip_or.items])


    def cplx_lambda_bar(self, st, lam, pfx):
        sh = [128, 32]
        dt = self.sb(st, pfx + "dt", sh); mag = self.sb(st, pfx + "mag", sh)
        a1 = self.sb(st, pfx + "a1", sh); a2 = self.sb(st, pfx + "a2", sh)
        sn = self.sb(st, pfx + "sn", sh); cs = self.sb(st, pfx + "cs", sh)
        abre = self.sb(st, pfx + "abre", sh); abim = self.sb(st, pfx + "abim", sh)
        self.A(lambda e: e.activation(out=dt[:], in_=lam[:, 2, :], func=AF.Exp), reads=[lam], writes=[dt])
        self.V(lambda e: e.tensor_tensor(out=mag[:], in0=lam[:, 0, :], in1=dt[:], op=ALU.mult), reads=[lam, dt], writes=[mag])
        self.A(lambda e: e.activation(out=mag[:], in_=mag[:], func=AF.Exp), reads=[mag], writes=[mag])
        self.V(lambda e: e.tensor_tensor(out=a1[:], in0=lam[:, 1, :], in1=dt[:], op=ALU.mult), reads=[lam, dt], writes=[a1])
        self.V(lambda e: e.tensor_scalar(out=a2[:], in0=a1[:], scalar1=math.pi / 2, scalar2=None, op0=ALU.add), reads=[a1], writes=[a2])
        self.rr_sin(st, sn, a1, sh, pfx + "r1")
        self.rr_sin(st, cs, a2, sh, pfx + "r2")
        self.V(lambda e: e.tensor_tensor(out=abre[:], in0=mag[:], in1=cs[:], op=ALU.mult), reads=[mag, cs], writes=[abre])
        self.V(lambda e: e.tensor_tensor(out=abim[:], in0=mag[:], in1=sn[:], op=ALU.mult), reads=[mag, sn], writes=[abim])
        return abre, abim

    def cmul(self, st, ar, ai, br, bi, pfx, shape):
        orr = self.sb(st, pfx + "re", shape); oi = self.sb(st, pfx + "im", shape); tm = self.sb(st, pfx + "tm", shape)
        self.V(lambda e: e.tensor_tensor(out=orr[:], in0=ar[:], in1=br[:], op=ALU.mult), reads=[ar, br], writes=[orr])
        self.V(lambda e: e.tensor_tensor(out=tm[:], in0=ai[:], in1=bi[:], op=ALU.mult), reads=[ai, bi], writes=[tm])
        self.V(lambda e: e.tensor_tensor(out=orr[:], in0=orr[:], in1=tm[:], op=ALU.subtract), reads=[orr, tm], writes=[orr])
        self.V(lambda e: e.tensor_tensor(out=oi[:], in0=ar[:], in1=bi[:], op=ALU.mult), reads=[ar, bi], writes=[oi])
        self.V(lambda e: e.tensor_tensor(out=tm[:], in0=ai[:], in1=br[:], op=ALU.mult), reads=[ai, br], writes=[tm])
        self.V(lambda e: e.tensor_tensor(out=oi[:], in0=oi[:], in1=tm[:], op=ALU.add), reads=[oi, tm], writes=[oi])
        return orr, oi

    def s5(self):
        NT, LS, NPR = self.NT, self.LS, self.NPR
        lamC = self.din("s5_lamC", [2, 128, 3, 32])
        bT = self.din("s5_bT", [2, 2, 32, 32, 128])
        cT = self.din("s5_cT", [2, 2, 128, 32, 32])
        h0C = self.din("s5_h0C", [2, 2, 128, 32])
        dTd = self.din("s5_dT", [32, 32])
        o_re = self.dout("new_s5_re", [NPR, 2, G_S5, P_S5]); o_im = self.dout("new_s5_im", [NPR, 2, G_S5, P_S5])
        UT = self.scr["UT"]; GYT = self.dscr("GYT", [1024, NT])
        nlev_s = int(math.log2(LS)); nlev_p = 8
        nlev = max(nlev_s, nlev_p)
        with ExitStack() as st:
            pw = []
            fco = []
            h0a = []
            cts = []
            for d in range(2):
                lam = self.sb(st, f"s5lam{d}", [128, 3, 32])
                self.load(lam, lam[:], lamC[d])
                abre, abim = self.cplx_lambda_bar(st, lam, f"s5lb{d}")
                sh = [128, 32]
                den = self.sb(st, f"s5den{d}", sh); t1 = self.sb(st, f"s5t1{d}", sh); t2 = self.sb(st, f"s5t2{d}", sh)
                fre = self.sb(st, f"s5fre{d}", sh); fim = self.sb(st, f"s5fim{d}", sh); nfim = self.sb(st, f"s5nfim{d}", sh)
                self.V(lambda e, lam=lam, den=den: e.tensor_tensor(out=den[:], in0=lam[:, 0, :], in1=lam[:, 0, :], op=ALU.mult), reads=[lam], writes=[den])
                self.V(lambda e, lam=lam, t2=t2: e.tensor_tensor(out=t2[:], in0=lam[:, 1, :], in1=lam[:, 1, :], op=ALU.mult), reads=[lam], writes=[t2])
                self.V(lambda e, den=den, t2=t2: e.tensor_tensor(out=den[:], in0=den[:], in1=t2[:], op=ALU.add), reads=[den, t2], writes=[den])
                self.V(lambda e, den=den: e.reciprocal(den[:], den[:]), reads=[den], writes=[den])
                self.V(lambda e, t1=t1, abre=abre: e.tensor_scalar(out=t1[:], in0=abre[:], scalar1=-1.0, scalar2=None, op0=ALU.add), reads=[abre], writes=[t1])
                self.V(lambda e, fre=fre, t1=t1, lam=lam: e.tensor_tensor(out=fre[:], in0=t1[:], in1=lam[:, 0, :], op=ALU.mult), reads=[t1, lam], writes=[fre])
                self.V(lambda e, t2=t2, abim=abim, lam=lam: e.tensor_tensor(out=t2[:], in0=abim[:], in1=lam[:, 1, :], op=ALU.mult), reads=[abim, lam], writes=[t2])
                self.V(lambda e, fre=fre, t2=t2: e.tensor_tensor(out=fre[:], in0=fre[:], in1=t2[:], op=ALU.add), reads=[fre, t2], writes=[fre])
                self.V(lambda e, fre=fre, den=den: e.tensor_tensor(out=fre[:], in0=fre[:], in1=den[:], op=ALU.mult), reads=[fre, den], writes=[fre])
                self.V(lambda e, fim=fim, abim=abim, lam=lam: e.tensor_tensor(out=fim[:], in0=abim[:], in1=lam[:, 0, :], op=ALU.mult), reads=[abim, lam], writes=[fim])
                self.V(lambda e, t2=t2, t1=t1, lam=lam: e.tensor_tensor(out=t2[:], in0=t1[:], in1=lam[:, 1, :], op=ALU.mult), reads=[t1, lam], writes=[t2])
                self.V(lambda e, fim=fim, t2=t2: e.tensor_tensor(out=fim[:], in0=fim[:], in1=t2[:], op=ALU.subtract), reads=[fim, t2], writes=[fim])
                self.V(lambda e, fim=fim, den=den: e.tensor_tensor(out=fim[:], in0=fim[:], in1=den[:], op=ALU.mult), reads=[fim, den], writes=[fim])
                self.V(lambda e, fim=fim, nfim=nfim: e.tensor_scalar(out=nfim[:], in0=fim[:], scalar1=-1.0, scalar2=None, op0=ALU.mult), reads=[fim], writes=[nfim])
                fco.append((fre, fim, nfim))
                h0 = self.sb(st, f"s5h0{d}", [128, 2, 32])
                self.load(h0, h0[:, 0, :], h0C[d, 0]); self.load(h0, h0[:, 1, :], h0C[d, 1], partial=True)
                h0r = self.sb(st, f"s5h0r{d}", sh); h0i = self.sb(st, f"s5h0i{d}", sh)
                self.V(lambda e, h0=h0, h0r=h0r: e.tensor_copy(h0r[:], h0[:, 0, :]), reads=[h0], writes=[h0r])
                self.V(lambda e, h0=h0, h0i=h0i: e.tensor_copy(h0i[:], h0[:, 1, :]), reads=[h0], writes=[h0i])
                h0a.append(self.cmul(st, abre, abim, h0r, h0i, f"s5h0a{d}", sh))
                lv = []
                cr, ci = abre, abim
                for l in range(nlev):
                    ni = self.sb(st, f"s5pwn{d}_{l}", sh)
                    self.V(lambda e, ni=ni, ci=ci: e.tensor_scalar(out=ni[:], in0=ci[:], scalar1=-1.0, scalar2=None, op0=ALU.mult), reads=[ci], writes=[ni])
                    lv.append((cr, ci, ni))
                    if l < nlev - 1:
                        cr, ci = self.cmul(st, cr, ci, cr, ci, f"s5pw{d}_{l}", sh)
                pw.append(lv)
                ctr = self.sb(st, f"s5ctr{d}", [128, 32, 32]); cti = self.sb(st, f"s5cti{d}", [128, 32, 32])
                self.load(ctr, ctr[:], cT[d, 0]); self.load(cti, cti[:], cT[d, 1])
                self.V(lambda e, cti=cti: e.tensor_scalar(out=cti[:], in0=cti[:], scalar1=-1.0, scalar2=None, op0=ALU.mult), reads=[cti], writes=[cti])
                cts.append((ctr, cti))
            dT = self.sb(st, "s5dT", [32, 32])
            self.load(dT, dT[:], dTd[:, :])
            LB = max(LS, NPR * LP)
            bufs = [[self.sb(st, f"s5b{a}{c}", [128, LB]) for c in range(2)] for a in range(2)]
            uT = self.sb(st, "s5uT", [32, NT]); yacc = self.sb(st, "s5yacc", [32, NT]); tg = self.sb(st, "s5tg", [32, NT])
            btr = self.ring(st, "s5bt", [32, 2, 2, 128], 2)
            fin = self.sb(st, "s5fin", [128, 32 * 2 * NPR * 2])
            groups = [(0, LS, 1, LS, nlev_s, True)] + ([(LS, NPR * LP, NPR, LP, nlev_p, False)] if NPR else [])
            for i in range(32):
                self.load(uT, uT[:], UT[32 * i:32 * i + 32, :], src_bufs=[self.dreg("UT", "all")])
                bt = btr.next()
                for d in range(2):
                    for c in range(2):
                        self.load(bt, bt[:, d, c, :], bT[d, c, :, i, :], partial=(d + c > 0))
                for (g0, glen, nseq, L, nl, is_sample) in groups:
                    for d in range(2):
                        fre, fim, nfim = fco[d]
                        cur = 0
                        A_re, A_im = bufs[cur]
                        for b0 in range(0, glen, 512):
                            bl = min(512, glen - b0)
                            p1 = self.ps.next(); p2 = self.ps.next()
                            self.mm(p1, p1[:, 0:bl], bt[:, d, 0, :], uT[:, g0 + b0:g0 + b0 + bl], reads=[bt, uT], start=True, stop=True)
                            self.mm(p2, p2[:, 0:bl], bt[:, d, 1, :], uT[:, g0 + b0:g0 + b0 + bl], reads=[bt, uT], start=True, stop=True)
                            self.V(lambda e, A_re=A_re, p1=p1, b0=b0, bl=bl, fre=fre, i=i: e.tensor_scalar(out=A_re[:, b0:b0 + bl], in0=p1[:, 0:bl], scalar1=fre[:, i:i + 1], scalar2=None, op0=ALU.mult),
                                   reads=[p1, fre], partial=[A_re])
                            self.V(lambda e, A_re=A_re, p2=p2, b0=b0, bl=bl, nfim=nfim, i=i: e.scalar_tensor_tensor(out=A_re[:, b0:b0 + bl], in0=p2[:, 0:bl], scalar=nfim[:, i:i + 1], in1=A_re[:, b0:b0 + bl], op0=ALU.mult, op1=ALU.add),
                                   reads=[p2, nfim, A_re], partial=[A_re])
                            self.V(lambda e, A_im=A_im, p2=p2, b0=b0, bl=bl, fre=fre, i=i: e.tensor_scalar(out=A_im[:, b0:b0 + bl], in0=p2[:, 0:bl], scalar1=fre[:, i:i + 1], scalar2=None, op0=ALU.mult),
                                   reads=[p2, fre], partial=[A_im])
                            self.V(lambda e, A_im=A_im, p1=p1, b0=b0, bl=bl, fim=fim, i=i: e.scalar_tensor_tensor(out=A_im[:, b0:b0 + bl], in0=p1[:, 0:bl], scalar=fim[:, i:i + 1], in1=A_im[:, b0:b0 + bl], op0=ALU.mult, op1=ALU.add),
                                   reads=[p1, fim, A_im], partial=[A_im])
                        if is_sample:
                            col = 0 if d == 0 else L - 1
                            for c in range(2):
                                tgt = (A_re, A_im)[c]; add = h0a[d][c]
                                self.V(lambda e, tgt=tgt, add=add, col=col, i=i: e.tensor_tensor(out=tgt[:, col:col + 1], in0=tgt[:, col:col + 1], in1=add[:, i:i + 1], op=ALU.add),
                                       reads=[tgt, add], writes=[tgt])
                        for l in range(nl):
                            s = 1 << l
                            cr, ci, nci = pw[d][l]
                            src_re, src_im = bufs[cur]; dst_re, dst_im = bufs[1 - cur]
                            def v3(t, a, b):
                                return t[:, 0:glen].rearrange("p (n l) -> p n l", l=L)[:, :, a:b]
                            if d == 0:
                                o_sl, sh_sl, keep = (s, L), (0, L - s), (0, s)
                            else:
                                o_sl, sh_sl, keep = (0, L - s), (s, L), (L - s, L)
                            self.A(lambda e, dst_re=dst_re, src_re=src_re, keep=keep: e.copy(v3(dst_re, *keep), v3(src_re, *keep)), reads=[src_re], partial=[dst_re])
                            self.A(lambda e, dst_im=dst_im, src_im=src_im, keep=keep: e.copy(v3(dst_im, *keep), v3(src_im, *keep)), reads=[src_im], partial=[dst_im])
                            self.V(lambda e, dst_re=dst_re, src_re=src_re, o_sl=o_sl, sh_sl=sh_sl, cr=cr, i=i: e.scalar_tensor_tensor(
                                out=v3(dst_re, *o_sl), in0=v3(src_re, *sh_sl), scalar=cr[:, i:i + 1], in1=v3(src_re, *o_sl), op0=ALU.mult, op1=ALU.add),
                                reads=[src_re, cr], partial=[dst_re])
                            self.V(lambda e, dst_re=dst_re, src_im=src_im, o_sl=o_sl, sh_sl=sh_sl, nci=nci, i=i: e.scalar_tensor_tensor(
                                out=v3(dst_re, *o_sl), in0=v3(src_im, *sh_sl), scalar=nci[:, i:i + 1], in1=v3(dst_re, *o_sl), op0=ALU.mult, op1=ALU.add),
                                reads=[src_im, nci, dst_re], partial=[dst_re])
                            self.V(lambda e, dst_im=dst_im, src_re=src_re, src_im=src_im, o_sl=o_sl, sh_sl=sh_sl, ci=ci, i=i: e.scalar_tensor_tensor(
                                out=v3(dst_im, *o_sl), in0=v3(src_re, *sh_sl), scalar=ci[:, i:i + 1], in1=v3(src_im, *o_sl), op0=ALU.mult, op1=ALU.add),
                                reads=[src_re, src_im, ci], partial=[dst_im])
                            self.V(lambda e, dst_im=dst_im, src_im=src_im, o_sl=o_sl, sh_sl=sh_sl, cr=cr, i=i: e.scalar_tensor_tensor(
                                out=v3(dst_im, *o_sl), in0=v3(src_im, *sh_sl), scalar=cr[:, i:i + 1], in1=v3(dst_im, *o_sl), op0=ALU.mult, op1=ALU.add),
                                reads=[src_im, cr, dst_im], partial=[dst_im])
                            cur = 1 - cur
                        H_re, H_im = bufs[cur]
                        if not is_sample:
                            for j in range(nseq):
                                col = j * L + (L - 1 if d == 0 else 0)
                                for c in range(2):
                                    idx = ((i * 2 + d) * NPR + j) * 2 + c
                                    src = (H_re, H_im)[c]
                                    self.V(lambda e, src=src, col=col, idx=idx: e.tensor_copy(fin[:, idx:idx + 1], src[:, col:col + 1]), reads=[src], partial=[fin])
                        ctr, ncti = cts[d]
                        for b0 in range(0, glen, 512):
                            bl = min(512, glen - b0)
                            p1 = self.ps.next()
                            self.mm(p1, p1[0:32, 0:bl], ctr[:, i, :], H_re[:, b0:b0 + bl], reads=[ctr, H_re], start=True, stop=False)
                            self.mm(p1, p1[0:32, 0:bl], ncti[:, i, :], H_im[:, b0:b0 + bl], reads=[ncti, H_im], start=False, stop=True)
                            if d == 0:
                                self.ev(yacc[:, g0 + b0:g0 + b0 + bl], p1[0:32, 0:bl], reads=[p1], partial=[yacc])
                            else:
                                self.V(lambda e, p1=p1, b0=b0, bl=bl, g0=g0: e.tensor_tensor(out=yacc[:, g0 + b0:g0 + b0 + bl], in0=yacc[:, g0 + b0:g0 + b0 + bl], in1=p1[0:32, 0:bl], op=ALU.add),
                                       reads=[p1, yacc], partial=[yacc])
                self.V(lambda e, i=i: e.scalar_tensor_tensor(out=yacc[:], in0=uT[:], scalar=dT[:, i:i + 1], in1=yacc[:], op0=ALU.mult, op1=ALU.add), reads=[uT, dT, yacc], writes=[yacc])
                self.A(lambda e: e.activation(out=tg[:], in_=yacc[:], func=AF.Square), reads=[yacc], writes=[tg])
                self.V(lambda e: e.tensor_scalar(out=tg[:], in0=tg[:], scalar1=0.044715, scalar2=1.0, op0=ALU.mult, op1=ALU.add), reads=[tg], writes=[tg])
                self.V(lambda e: e.tensor_tensor(out=tg[:], in0=tg[:], in1=yacc[:], op=ALU.mult), reads=[tg, yacc], writes=[tg])
                self.A(lambda e: e.activation(out=tg[:], in_=tg[:], func=AF.Sigmoid, scale=2.0 * math.sqrt(2.0 / math.pi)), reads=[tg], writes=[tg])
                self.V(lambda e: e.tensor_tensor(out=tg[:], in0=tg[:], in1=yacc[:], op=ALU.mult), reads=[tg, yacc], writes=[tg])
                self.store(tg, GYT[32 * i:32 * i + 32, :], tg[:], partial_bufs=[self.dreg("GYT", "all")])
            for i in range(32):
                for d in range(2):
                    for j in range(NPR):
                        for c in range(2):
                            idx = ((i * 2 + d) * NPR + j) * 2 + c
                            o = (o_re, o_im)[c]
                            self.store(fin, o[j, d, 2 * i:2 * i + 2, :].rearrange("g (p o) -> (g p) o", o=1), fin[:, idx:idx + 1])
            self.S.barrier()
            self.S.release([uT.b, tg.b, fin.b, dT.b] + [b.b for b in btr.items] + [x.b for pr in cts for x in pr])

    def glu(self):
        NT = self.NT
        w = self.din("s5_glu_w", [1024, 1024]); gb = self.din("s5_glu_bT", [128, 8])
        GYT, SGT = self.scr["GYT"], self.scr["SGT"]
        MIXT = self.dscr("MIXT", [4096, NT])
        GYv = GYT.rearrange("(k p) t -> p k t", p=128)

        def loader(si, t0, TS, dst):
            for kq in range(0, 8, 4):
                self.load(dst, dst[:, kq:kq + 4, :], GYv[:, kq:kq + 4, t0:t0 + TS], src_bufs=[self.dreg("GYT", "all")], partial=(kq > 0))

        with ExitStack() as st:
            gbt = self.sb(st, "glu_b", [128, 8])
            self.load(gbt, gbt[:], gb[:, :])
            orr = self.ring(st, "glu_o", [128, 512], 2); g1r = self.ring(st, "glu_g1", [128, 512], 2); g2r = self.ring(st, "glu_g2", [128, 512], 2)

            def epi(ps, mc, col0, t0, TS):
                o = orr.next(); g1 = g1r.next(); g2 = g2r.next()
                cb = col0 // 128
                self.A(lambda e: e.activation(out=o[:, 0:TS], in_=ps[:, 0:TS], func=AF.Sigmoid, bias=gbt[:, cb:cb + 1]), reads=[ps, gbt], writes=[o])
                self.load(g1, g1[:, 0:TS], GYT[col0:col0 + 128, t0:t0 + TS], src_bufs=[self.dreg("GYT", "all")])
                self.load(g2, g2[:, 0:TS], SGT[col0:col0 + 128, t0:t0 + TS], src_bufs=[self.dreg("SGT", "all")])
                self.A(lambda e: e.activation(out=g2[:, 0:TS], in_=g2[:, 0:TS], func=AF.Silu), reads=[g2], writes=[g2])
                self.V(lambda e: e.tensor_tensor(out=o[:, 0:TS], in0=o[:, 0:TS], in1=g1[:, 0:TS], op=ALU.mult), reads=[o, g1], writes=[o])
                self.V(lambda e: e.tensor_tensor(out=o[:, 0:TS], in0=o[:, 0:TS], in1=g2[:, 0:TS], op=ALU.mult), reads=[o, g2], writes=[o])
                self.store(o, MIXT[col0:col0 + 128, t0:t0 + TS], o[:, 0:TS], partial_bufs=[self.dreg("MIXT", "all")])

            self.lin("glu", 1024, w, [(0, 1024, "F", epi)], 512, 512, loader)
            self.S.barrier()
            self.S.release([gbt.b] + [x.b for r in (orr, g1r, g2r) for x in r.items])


    def final_norm(self):
        NT = self.NT
        fw = self.din("final_norm_w", [D])
        y = self.dout("y", [NT, D])
        X = self.scr["X"]
        with ExitStack() as st:
            fwb = self.sb(st, "fn_w", [128, D])
            self.load(fwb, fwb[:], fw.partition_broadcast(128))
            xr = self.ring(st, "fn_x", [128, D], 2)
            xn = self.sb(st, "fn_xn", [128, D])
            ss = self.sb(st, "fn_ss", [128, 1]); rs = self.sb(st, "fn_rs", [128, 1])
            for ti in range(NT // 128):
                t0 = ti * 128
                xt = xr.next()
                self.load(xt, xt[:], X[t0:t0 + 128, :], src_bufs=[self.dreg("X", ti)])
                self.A(lambda e: e.activation(out=xn[:], in_=xt[:], func=AF.Square, accum_out=ss[:, 0:1]), reads=[xt], writes=[xn, ss])
                self.V(lambda e: e.tensor_scalar(out=rs[:], in0=ss[:], scalar1=1.0 / D, scalar2=EPS, op0=ALU.mult, op1=ALU.add), reads=[ss], writes=[rs])
                self.A(lambda e: e.activation(out=rs[:], in_=rs[:], func=AF.Sqrt), reads=[rs], writes=[rs])
                self.V(lambda e: e.reciprocal(rs[:], rs[:]), reads=[rs], writes=[rs])
                self.V(lambda e: e.scalar_tensor_tensor(out=xt[:], in0=xt[:], scalar=rs[:, 0:1], in1=fwb[:], op0=ALU.mult, op1=ALU.mult), reads=[xt, rs, fwb], writes=[xt])
                self.store(xt, y[t0:t0 + 128, :], xt[:])
            self.S.barrier()
            self.S.release([fwb.b] + [x.b for x in xr.items])


    def gla(self):
        NT, LS, NPR = self.NT, self.LS, self.NPR
        dup_d = self.din("gla_dup", [2, 64, GDKW])
        nw_d = self.din("gla_nw", [GDVW])
        s0_d = self.din("gla_s0", [2, GH, GDK, GDV])
        o_gla = self.dout("new_gla", [NPR, 2, GH, GDK, GDV])
        QT, KT, DLT = self.scr["QT"], self.scr["KT"], self.scr["DLT"]
        Ktm, Vtm, GGtm = self.scr["Ktm"], self.scr["Vtm"], self.scr["GGtm"]
        OACC = self.dscr("OACC", [NT, GDVW])
        MIXT = self.scr["MIXT"]
        QTv = QT.rearrange("(k p) t -> p k t", p=128); KTv = KT.rearrange("(k p) t -> p k t", p=128)
        MXv = MIXT[1024:4096, :].rearrange("(k p) t -> p k t", p=128)
        with ExitStack() as st:
            dup = self.sb(st, "gl_dup", [64, 2, GDKW])
            for d in range(2):
                self.load(dup, dup[:, d, :], dup_d[d], partial=(d > 0))
            nw = self.sb(st, "gl_nw", [128, GDVW])
            self.load(nw, nw[:], nw_d.partition_broadcast(128))
            blk = self.sb(st, "gl_blk", [128, 128])
            self.V(lambda e: e.memset(blk[:], 0.0), writes=[blk])
            self.V(lambda e: e.memset(blk[0:64, 0:64], 1.0), writes=[blk])
            self.V(lambda e: e.memset(blk[64:128, 64:128], 1.0), writes=[blk])
            tri2 = [self.sb(st, f"gl_tri{d}", [128, 128]) for d in range(2)]
            for d in range(2):
                self.V(lambda e, d=d: e.tensor_tensor(out=tri2[d][:], in0=self.m_incl[d][:], in1=blk[:], op=ALU.mult), reads=[self.m_incl[d], blk], writes=[tri2[d]])
            dla_r = self.ring(st, "gl_dla", [64, 128], 2)
            for dla in dla_r.items:
                self.V(lambda e, dla=dla: e.memset(dla[:], 0.0), writes=[dla])
                self.V(lambda e, dla=dla: e.memset(dla[32:33, :], 1.0), writes=[dla])
            qT_r = self.ring(st, "gl_qT", [128, 12, 128], 2); kT_r = self.ring(st, "gl_kT", [128, 12, 128], 2)
            ktm_r = self.ring(st, "gl_ktm", [128, GDKW], 2)
            vtm = self.sb(st, "gl_vtm", [128, GDVW]); gg = self.sb(st, "gl_gg", [128, GDVW]); oacc = self.sb(st, "gl_oacc", [128, GDVW])
            la = self.sb(st, "gl_la", [128, GDKW]); bsb = self.sb(st, "gl_b", [128, GDKW]); kend = self.sb(st, "gl_kend", [128, GDKW])
            eb = self.sb(st, "gl_eb", [128, 12, 128]); qdec = self.sb(st, "gl_qdec", [128, 12, 128]); kinv = self.sb(st, "gl_kinv", [128, 12, 128])
            att = self.ring(st, "gl_att", [128, 64], 2)
            S = [self.sb(st, f"gl_S{h}", [128, 2, GDV]) for h in range(GH)]
            ssq = self.sb(st, "gl_ssq", [128, GH]); oT_r = self.ring(st, "gl_oT", [128, 4, 128], 2)
            for d in range(2):
                for (q0, L, kind) in self.seqs:
                    for h in range(GH):
                        if kind == 0:
                            self.load(S[h], S[h][:], s0_d[d, h].rearrange("(k p) v -> p k v", p=128))
                        else:
                            self.V(lambda e, h=h: e.memset(S[h][:], 0.0), writes=[S[h]])
                    nb = L // 128
                    order = range(nb) if d == 0 else range(nb - 1, -1, -1)
                    for bi in order:
                        t0 = q0 + bi * 128
                        dla = dla_r.next(); qT = qT_r.next(); kT = kT_r.next(); ktm = ktm_r.next()
                        self.load(dla, dla[0:32, :], DLT[:, t0:t0 + 128], src_bufs=[self.dreg("DLT", "all")], partial=True)
                        for kq in range(0, 12, 4):
                            self.load(qT, qT[:, kq:kq + 4, :], QTv[:, kq:kq + 4, t0:t0 + 128], src_bufs=[self.dreg("QT", "all")], partial=(kq > 0))
                            self.load(kT, kT[:, kq:kq + 4, :], KTv[:, kq:kq + 4, t0:t0 + 128], src_bufs=[self.dreg("KT", "all")], partial=(kq > 0))
                        self.load(ktm, ktm[:], Ktm[t0:t0 + 128, :], src_bufs=[self.dreg("Ktm", "all")])
                        self.load(vtm, vtm[:], Vtm[t0:t0 + 128, :], src_bufs=[self.dreg("Vtm", "all")])
                        if d == 1:
                            self.load(gg, gg[:], GGtm[t0:t0 + 128, :], src_bufs=[self.dreg("GGtm", "all")])
                            self.load(oacc, oacc[:], OACC[t0:t0 + 128, :], src_bufs=[self.dreg("OACC", t0 // 128)])
                        for c3 in range(3):
                            ps = self.ps.next()
                            self.mm(ps, ps[:, :], dla[:, :], dup[:, d, c3 * 512:(c3 + 1) * 512], reads=[dla, dup], start=True, stop=True)
                            sl = slice(c3 * 512, (c3 + 1) * 512)
                            self.A(lambda e, ps=ps, sl=sl: e.activation(out=la[:, sl], in_=ps[:, :], func=AF.Exp, scale=-1.0), reads=[ps], partial=[la])
                        self.A(lambda e: e.activation(out=la[:], in_=la[:], func=AF.Ln, bias=1.0), reads=[la], writes=[la])
                        self.V(lambda e: e.tensor_scalar(out=la[:], in0=la[:], scalar1=-1.0 / 16.0, scalar2=-1.0, op0=ALU.mult, op1=ALU.max), reads=[la], writes=[la])
                        for c3 in range(3):
                            sl = slice(c3 * 512, (c3 + 1) * 512)
                            p1 = self.ps.next(); p2 = self.ps.next()
                            self.mm(p1, p1[:, :], tri2[d][:, :], la[:, sl], reads=[tri2[d], la], start=True, stop=True)
                            self.mm(p2, p2[:, :], blk[:, :], la[:, sl], reads=[blk, la], start=True, stop=True)
                            self.A(lambda e, p1=p1, sl=sl: e.copy(bsb[:, sl], p1[:, :]), reads=[p1], partial=[bsb])
                            self.V(lambda e, p2=p2, sl=sl: e.tensor_tensor(out=kend[:, sl], in0=p2[:, :], in1=bsb[:, sl], op=ALU.subtract), reads=[p2, bsb], partial=[kend])
                        self.A(lambda e: e.activation(out=kend[:], in_=kend[:], func=AF.Exp), reads=[kend], writes=[kend])
                        self.V(lambda e, ktm=ktm: e.tensor_tensor(out=kend[:], in0=kend[:], in1=ktm[:], op=ALU.mult), reads=[kend, ktm], writes=[kend])
                        for k4 in range(3):
                            ps = self.ps.next()
                            for j in range(4):
                                kb = k4 * 4 + j
                                self.mm(ps, ps[:, j * 128:(j + 1) * 128], la[:, kb * 128:(kb + 1) * 128], tri2[d][:, :], reads=[la, tri2[d]], start=True, stop=True, first=(j == 0))
                            self.A(lambda e, ps=ps, k4=k4: e.activation(out=eb[:, k4 * 4:(k4 + 1) * 4, :], in_=ps[:, :].rearrange("p (k t) -> p k t", t=128), func=AF.Exp), reads=[ps], partial=[eb])
                            self.A(lambda e, ps=ps, k4=k4: e.activation(out=kinv[:, k4 * 4:(k4 + 1) * 4, :], in_=ps[:, :].rearrange("p (k t) -> p k t", t=128), func=AF.Exp, scale=-1.0), reads=[ps], partial=[kinv])
                        self.V(lambda e, qT=qT: e.scalar_tensor_tensor(out=qdec[:], in0=qT[:], scalar=1.0 / 16.0, in1=eb[:], op0=ALU.mult, op1=ALU.mult), reads=[qT, eb], writes=[qdec])
                        self.V(lambda e, kT=kT: e.tensor_tensor(out=kinv[:], in0=kinv[:], in1=kT[:], op=ALU.mult), reads=[kinv, kT], writes=[kinv])
                        banks = self.ps.items

                        def gla_head(h, slot):
                            ps_a = banks[4 * slot]; ps_o = banks[4 * slot + 1]
                            sring = Ring(banks[4 * slot + 2:4 * slot + 4])
                            at = att.items[slot]
                            for ci, c in enumerate((0, 1) if d == 0 else (1, 0)):
                                cs = slice(64 * c, 64 * c + 64)
                                col = 64 * c + (63 if d == 0 else 0)
                                for j in range(2):
                                    kb = 2 * h + j
                                    self.mm(ps_a, ps_a[cs, 0:64], kinv[:, kb, cs], qdec[:, kb, cs], reads=[kinv, qdec], start=(j == 0), stop=(j == 1), first=(j == 0 and ci == 0))
                                self.V(lambda e, ps_a=ps_a, at=at, cs=cs, d=d: e.tensor_tensor(out=at[cs, :], in0=ps_a[cs, 0:64], in1=self.m_incl[d][cs, cs], op=ALU.mult),
                                       reads=[ps_a, self.m_incl[d]], partial=[at])
                                yield
                                self.mm(ps_o, ps_o[cs, :], at[cs, :], vtm[cs, h * GDV:(h + 1) * GDV], reads=[at, vtm], start=True, stop=False, first=(ci == 0))
                                for j in range(2):
                                    kb = 2 * h + j
                                    self.mm(ps_o, ps_o[cs, :], qdec[:, kb, cs], S[h][:, j, :], reads=[qdec, S[h]], start=False, stop=(j == 1))
                                for j in range(2):
                                    kb = 2 * h + j
                                    ps_s = sring.next()
                                    self.mm(ps_s, ps_s[:, :], kend[cs, kb * 128:(kb + 1) * 128], vtm[cs, h * GDV:(h + 1) * GDV], reads=[kend, vtm], start=True, stop=True)
                                    self.V(lambda e, ps_s=ps_s, h=h, j=j, kb=kb, col=col: e.scalar_tensor_tensor(out=S[h][:, j, :], in0=S[h][:, j, :], scalar=eb[:, kb, col:col + 1], in1=ps_s[:, :], op0=ALU.mult, op1=ALU.add),
                                           reads=[S[h], eb, ps_s], partial=[S[h]])
                                yield
                            hs = slice(h * GDV, (h + 1) * GDV)
                            if d == 0:
                                self.ev(oacc[:, hs], ps_o[:, :], reads=[ps_o], partial=[oacc])
                            else:
                                self.V(lambda e, ps_o=ps_o, hs=hs: e.tensor_tensor(out=oacc[:, hs], in0=oacc[:, hs], in1=ps_o[:, :], op=ALU.add), reads=[ps_o, oacc], partial=[oacc])

                        for g0 in range(0, GH, 2):
                            gens = [gla_head(h, h - g0) for h in range(g0, g0 + 2)]
                            while gens:
                                for g in list(gens):
                                    try:
                                        next(g)
                                    except StopIteration:
                                        gens.remove(g)
                        if d == 0:
                            self.store(oacc, OACC[t0:t0 + 128, :], oacc[:], dst_bufs=[self.dreg("OACC", t0 // 128)])
                        else:
                            for h in range(GH):
                                hs = slice(h * GDV, (h + 1) * GDV)
                                self.A(lambda e, h=h, hs=hs: e.activation(out=la[:, 0:GDV], in_=oacc[:, hs], func=AF.Square, accum_out=ssq[:, h:h + 1]), reads=[oacc], partial=[la, ssq])
                            self.V(lambda e: e.tensor_scalar(out=ssq[:], in0=ssq[:], scalar1=1.0 / GDV, scalar2=EPS, op0=ALU.mult, op1=ALU.add), reads=[ssq, la], writes=[ssq])
                            self.A(lambda e: e.activation(out=ssq[:], in_=ssq[:], func=AF.Sqrt), reads=[ssq], writes=[ssq])
                            self.V(lambda e: e.reciprocal(ssq[:], ssq[:]), reads=[ssq], writes=[ssq])
                            self.A(lambda e: e.activation(out=gg[:], in_=gg[:], func=AF.Silu), reads=[gg], writes=[gg])
                            for h in range(GH):
                                hs = slice(h * GDV, (h + 1) * GDV)
                                self.V(lambda e, h=h, hs=hs: e.scalar_tensor_tensor(out=oacc[:, hs], in0=oacc[:, hs], scalar=ssq[:, h:h + 1], in1=nw[:, hs], op0=ALU.mult, op1=ALU.mult),
                                       reads=[oacc, ssq, nw], partial=[oacc])
                            self.V(lambda e: e.tensor_tensor(out=oacc[:], in0=oacc[:], in1=gg[:], op=ALU.mult), reads=[oacc, gg], writes=[oacc])
                            for k4 in range(6):
                                ps = self.ps.next(); oT = oT_r.next()
                                for j in range(4):
                                    kb = k4 * 4 + j
                                    self.tr(ps, ps[:, j * 128:(j + 1) * 128], oacc[:, kb * 128:(kb + 1) * 128], reads=[oacc])
                                self.ev(oT[:], ps[:, :].rearrange("p (k t) -> p k t", t=128), reads=[ps], writes=[oT])
                                self.store(oT, MXv[:, k4 * 4:(k4 + 1) * 4, t0:t0 + 128], oT[:], partial_bufs=[self.dreg("MIXT", "all")])
                    if kind == 1:
                        j = (q0 - LS) // LP
                        for h in range(GH):
                            self.store(S[h], o_gla[j, d, h].rearrange("(k p) v -> p k v", p=128), S[h][:])
            self.S.barrier()
            self.S.release([x.b for x in [dup, nw, vtm, gg, oacc] + S + dla_r.items + qT_r.items + kT_r.items + ktm_r.items + oT_r.items])

    def outproj(self, layer, K, w_name, srcname):
        NT, LS = self.NT, self.LS
        w = self.din(w_name, [K, D])
        X = self.scr["X"]
        SRCv = self.scr[srcname].rearrange("(k p) t -> p k t", p=128)
        KC = K // 128

        def loader(si, t0, TS, dst):
            for kq in range(0, KC, 4):
                self.load(dst, dst[:, kq:kq + 4, :], SRCv[:, kq:kq + 4, t0:t0 + TS], src_bufs=[self.dreg(srcname, "all")], partial=(kq > 0))

        with ExitStack() as st:
            gate = [self.sb(st, f"op{layer}_gate{c}", [128, D]) for c in range(2)]
            gT = self.gateT[layer]
            for c in range(2):
                for k4 in range(4):
                    ps = self.ps.next()
                    for j in range(4):
                        k = k4 * 4 + j
                        self.mm(ps, ps[:, j * 128:(j + 1) * 128], gT[:, k, c:c + 1].to_broadcast([128, 128]), self.ident[:, :], reads=[gT, self.ident], start=True, stop=True, first=(j == 0))
                    self.ev(gate[c][:, k4 * 512:(k4 + 1) * 512], ps[:, :], reads=[ps], partial=[gate[c]])
            xr = self.ring(st, f"op{layer}_x", [128, 512], 3)

            def epi(ps, ncb, col0, tok0):
                c = 0 if tok0 < LS else 1
                xt = xr.next()
                reg = self.dreg("X", (tok0 // 128, col0))
                self.load(xt, xt[:, 0:ncb], X[tok0:tok0 + 128, col0:col0 + ncb], src_bufs=[self.dreg("X", tok0 // 128), reg])
                self.V(lambda e: e.tensor_tensor(out=ps[:, 0:ncb], in0=ps[:, 0:ncb], in1=gate[c][:, col0:col0 + ncb], op=ALU.mult), reads=[gate[c]], writes=[ps])
                self.V(lambda e: e.tensor_tensor(out=xt[:, 0:ncb], in0=xt[:, 0:ncb], in1=ps[:, 0:ncb], op=ALU.add), reads=[ps, xt], writes=[xt])
                self.store(xt, X[tok0:tok0 + 128, col0:col0 + ncb], xt[:, 0:ncb], dst_bufs=[reg], partial_bufs=[self.dreg("X", tok0 // 128)])

            TS, CB = (256, 256) if K > 2048 else (512, 512)
            self.lin(f"op{layer}", K, w, [(0, D, "T", epi)], TS, CB, loader)
            self.S.barrier()
            self.S.release([x.b for x in xr.items])


    def inproj1(self):
        NT = self.NT
        w = self.din("o_w_in", [D, ODD_IN])
        muT = self.din("rwkv_muT", [2, 128, KD])
        HTv = self.scr["HT"].rearrange("(k p) t -> p k t", p=128)
        names = ["Rtm", "K1tm", "V1tm", "Gtm"]
        for nm in names:
            self.dscr(nm, [NT, D])
        WAT = self.dscr("WAT", [384, NT])
        starts = set(q0 for (q0, L, k) in self.seqs)
        ends = set(q0 + L for (q0, L, k) in self.seqs)
        with ExitStack() as st:
            mu = self.sb(st, "ip1_mu", [128, 2, KD]); c0 = self.sb(st, "ip1_c0", [128, KD])
            self.load(mu, mu[:, 0, :], muT[0]); self.load(mu, mu[:, 1, :], muT[1], partial=True)
            self.V(lambda e: e.tensor_tensor(out=c0[:], in0=mu[:, 0, :], in1=mu[:, 1, :], op=ALU.add), reads=[mu], writes=[c0])
            self.V(lambda e: e.tensor_scalar(out=c0[:], in0=c0[:], scalar1=-1.0, scalar2=1.0, op0=ALU.mult, op1=ALU.add), reads=[c0], writes=[c0])
            TS = 256
            hp = self.sb(st, "ip1_hp", [128, KD, TS]); hn = self.sb(st, "ip1_hn", [128, KD, TS])
            src_all = [self.dreg("HT", "all")]

            def loader(si, t0, TS, dst):
                for kq in range(0, KD, 4):
                    ks = slice(kq, kq + 4)
                    self.load(dst, dst[:, ks, :], HTv[:, ks, t0:t0 + TS], src_bufs=src_all, partial=(kq > 0))
                    if t0 > 0:
                        self.load(hp, hp[:, ks, :], HTv[:, ks, t0 - 1:t0 + TS - 1], src_bufs=src_all, partial=(kq > 0))
                    else:
                        self.load(hp, hp[:, ks, 1:TS], HTv[:, ks, 0:TS - 1], src_bufs=src_all, partial=(kq > 0))
                    if t0 + TS < NT:
                        self.load(hn, hn[:, ks, :], HTv[:, ks, t0 + 1:t0 + TS + 1], src_bufs=src_all, partial=(kq > 0))
                    else:
                        self.load(hn, hn[:, ks, 0:TS - 1], HTv[:, ks, t0 + 1:t0 + TS], src_bufs=src_all, partial=(kq > 0))
                for j in range(TS):
                    if (t0 + j) in starts:
                        self.V(lambda e, j=j: e.memset(hp[:, :, j:j + 1], 0.0), reads=[hp], writes=[hp])
                    if (t0 + j + 1) in ends:
                        self.V(lambda e, j=j: e.memset(hn[:, :, j:j + 1], 0.0), reads=[hn], writes=[hn])
                for k in range(KD):
                    self.V(lambda e, k=k: e.tensor_scalar(out=dst[:, k, :], in0=dst[:, k, :], scalar1=c0[:, k:k + 1], scalar2=None, op0=ALU.mult), reads=[dst, c0], writes=[dst])
                    self.V(lambda e, k=k: e.scalar_tensor_tensor(out=dst[:, k, :], in0=hp[:, k, :], scalar=mu[:, 0, k:k + 1], in1=dst[:, k, :], op0=ALU.mult, op1=ALU.add), reads=[hp, mu, dst], writes=[dst])
                    self.V(lambda e, k=k: e.scalar_tensor_tensor(out=dst[:, k, :], in0=hn[:, k, :], scalar=mu[:, 1, k:k + 1], in1=dst[:, k, :], op0=ALU.mult, op1=ALU.add), reads=[hn, mu, dst], writes=[dst])

            orr = self.ring(st, "ip1_o", [128, 512], 4)

            def epiT(ps, ncb, col0, tok0):
                o = orr.next()
                self.ev(o[:, 0:ncb], ps[:, 0:ncb], reads=[ps], writes=[o])
                nm = names[col0 // D]
                cc = col0 % D
                self.store(o, self.scr[nm][tok0:tok0 + 128, cc:cc + ncb], o[:, 0:ncb], partial_bufs=[self.dreg(nm, "all")])

            def epiF(ps, mc, col0, t0, TS):
                o = orr.next()
                self.ev(o[0:mc, 0:TS], ps[0:mc, 0:TS], reads=[ps], writes=[o])
                r0 = col0 - 4 * D
                self.store(o, WAT[r0:r0 + mc, t0:t0 + TS], o[0:mc, 0:TS], partial_bufs=[self.dreg("WAT", "all")])

            self.lin("ip1", D, w, [(0, 4 * D, "T", epiT), (4 * D, ODD_IN, "F", epiF)], TS, 512, loader)
            self.S.barrier()
            self.S.release([mu.b, hp.b, hn.b] + [o.b for o in orr.items])

    def rwkv_prep(self):
        NT = self.NT
        w2a = self.din("rwkv_w2a", [2, 97, D]); a2a = self.din("rwkv_a2a", [2, 97, D])
        kk_d = self.din("rwkv_k_k", [D]); ka_d = self.din("rwkv_k_a", [D]); rk_d = self.din("rwkv_r_k", [D])
        for nm in ["LW0", "LW1", "KD0", "KD1", "BV0", "BV1", "AV", "BONUS"]:
            self.dscr(nm, [NT, D])
        WAT = self.scr["WAT"]
        with ExitStack() as st:
            w2 = self.sb(st, "rp_w2", [97, 2, D]); a2 = self.sb(st, "rp_a2", [97, 2, D])
            for d in range(2):
                self.load(w2, w2[:, d, :], w2a[d], partial=(d > 0)); self.load(a2, a2[:, d, :], a2a[d], partial=(d > 0))
            kkb = self.sb(st, "rp_kk", [128, D]); kab = self.sb(st, "rp_ka", [128, D]); rkb = self.sb(st, "rp_rk", [128, D])
            self.load(kkb, kkb[:], kk_d.partition_broadcast(128)); self.load(kab, kab[:], ka_d.partition_broadcast(128)); self.load(rkb, rkb[:], rk_d.partition_broadcast(128))
            wl = [self.sb(st, f"rp_wl{d}", [97, 128]) for d in range(2)]; al = [self.sb(st, f"rp_al{d}", [97, 128]) for d in range(2)]
            for t in wl + al:
                self.V(lambda e, t=t: e.memset(t[96:97, :], 1.0), writes=[t])
            r = self.sb(st, "rp_r", [128, D]); k = self.sb(st, "rp_k", [128, D]); v = self.sb(st, "rp_v", [128, D])
            kkn = self.sb(st, "rp_kkn", [128, D]); t1 = self.sb(st, "rp_t1", [128, D]); t2 = self.sb(st, "rp_t2", [128, D]); t3 = self.sb(st, "rp_t3", [128, D])
            ssq = self.sb(st, "rp_ssq", [128, RH]); bs = self.sb(st, "rp_bs", [128, 2, RH])

            def h3(t):
                return t[:].rearrange("p (h j) -> p h j", j=RN)

            def bc(t):
                return t.unsqueeze(2).to_broadcast([128, RH, RN])

            for ti in range(NT // 128):
                t0 = ti * 128
                ts = slice(t0, t0 + 128)
                self.load(r, r[:], self.scr["Rtm"][ts, :], src_bufs=[self.dreg("Rtm", "all")])
                self.load(k, k[:], self.scr["K1tm"][ts, :], src_bufs=[self.dreg("K1tm", "all")])
                self.load(v, v[:], self.scr["V1tm"][ts, :], src_bufs=[self.dreg("V1tm", "all")])
                for d in range(2):
                    self.load(wl[d], wl[d][0:96, :], WAT[96 * d:96 * d + 96, ts], src_bufs=[self.dreg("WAT", "all")], partial=True)
                    self.load(al[d], al[d][0:96, :], WAT[192 + 96 * d:192 + 96 * d + 96, ts], src_bufs=[self.dreg("WAT", "all")], partial=True)
                    self.A(lambda e, d=d: e.activation(out=wl[d][0:96, :], in_=wl[d][0:96, :], func=AF.Tanh), reads=[wl[d]], partial=[wl[d]])
                self.V(lambda e: e.tensor_tensor(out=kkn[:], in0=k[:], in1=kkb[:], op=ALU.mult), reads=[k, kkb], writes=[kkn])
                self.A(lambda e: e.activation(out=t1[:], in_=kkn[:], func=AF.Square), reads=[kkn], writes=[t1])
                self.V(lambda e: e.tensor_reduce(out=ssq[:], in_=h3(t1), axis=AX.X, op=ALU.add), reads=[t1], writes=[ssq])
                self.A(lambda e: e.activation(out=ssq[:], in_=ssq[:], func=AF.Sqrt), reads=[ssq], writes=[ssq])
                self.V(lambda e: e.tensor_scalar(out=ssq[:], in0=ssq[:], scalar1=1e-12, scalar2=None, op0=ALU.max), reads=[ssq], writes=[ssq])
                self.V(lambda e: e.reciprocal(ssq[:], ssq[:]), reads=[ssq], writes=[ssq])
                self.V(lambda e: e.tensor_tensor(out=h3(kkn), in0=h3(kkn), in1=bc(ssq[:]), op=ALU.mult), reads=[kkn, ssq], writes=[kkn])
                self.V(lambda e: e.tensor_scalar(out=t1[:], in0=kkn[:], scalar1=-1.0, scalar2=None, op0=ALU.mult), reads=[kkn], writes=[t1])
                self.store(t1, self.scr["AV"][ts, :], t1[:], partial_bufs=[self.dreg("AV", "all")])
                self.V(lambda e: e.tensor_tensor(out=r[:], in0=r[:], in1=rkb[:], op=ALU.mult), reads=[r, rkb], writes=[r])
                for d in range(2):
                    pw_ = [self.ps.next() for _ in range(4)]
                    for q in range(4):
                        self.mm(pw_[q], pw_[q][:, :], wl[d][:, :], w2[:, d, q * 512:(q + 1) * 512], reads=[wl[d], w2], start=True, stop=True)
                        qs = slice(q * 512, (q + 1) * 512)
                        self.A(lambda e, q=q, qs=qs: e.activation(out=t2[:, qs], in_=pw_[q][:, :], func=AF.Exp, scale=-1.0), reads=[pw_[q]], partial=[t2])
                    self.A(lambda e: e.activation(out=t2[:], in_=t2[:], func=AF.Ln, bias=1.0), reads=[t2], writes=[t2])
                    self.A(lambda e: e.activation(out=t2[:], in_=t2[:], func=AF.Exp, scale=-1.0, bias=-0.5), reads=[t2], writes=[t2])
                    self.V(lambda e: e.tensor_scalar(out=t2[:], in0=t2[:], scalar1=-1.0, scalar2=None, op0=ALU.mult), reads=[t2], writes=[t2])
                    self.store(t2, self.scr[f"LW{d}"][ts, :], t2[:], partial_bufs=[self.dreg(f"LW{d}", "all")])
                    pa_ = [self.ps.next() for _ in range(4)]
                    for q in range(4):
                        self.mm(pa_[q], pa_[q][:, :], al[d][:, :], a2[:, d, q * 512:(q + 1) * 512], reads=[al[d], a2], start=True, stop=True)
                        qs = slice(q * 512, (q + 1) * 512)
                        self.A(lambda e, q=q, qs=qs: e.activation(out=t3[:, qs], in_=pa_[q][:, :], func=AF.Sigmoid), reads=[pa_[q]], partial=[t3])
                    self.V(lambda e: e.tensor_tensor(out=t1[:], in0=kkn[:], in1=t3[:], op=ALU.mult), reads=[kkn, t3], writes=[t1])
                    self.store(t1, self.scr[f"BV{d}"][ts, :], t1[:], partial_bufs=[self.dreg(f"BV{d}", "all")])
                    self.V(lambda e: e.scalar_tensor_tensor(out=t3[:], in0=t3[:], scalar=-1.0, in1=kab[:], op0=ALU.add, op1=ALU.mult), reads=[t3, kab], writes=[t3])
                    self.V(lambda e: e.scalar_tensor_tensor(out=t3[:], in0=t3[:], scalar=1.0, in1=k[:], op0=ALU.add, op1=ALU.mult), reads=[t3, k], writes=[t3])
                    self.store(t3, self.scr[f"KD{d}"][ts, :], t3[:], partial_bufs=[self.dreg(f"KD{d}", "all")])
                    self.V(lambda e: e.tensor_tensor(out=t2[:], in0=t3[:], in1=r[:], op=ALU.mult), reads=[t3, r], writes=[t2])
                    self.V(lambda e, d=d: e.tensor_reduce(out=bs[:, d, :], in_=h3(t2), axis=AX.X, op=ALU.add), reads=[t2], partial=[bs])
                self.V(lambda e: e.tensor_tensor(out=bs[:, 0, :], in0=bs[:, 0, :], in1=bs[:, 1, :], op=ALU.add), reads=[bs], writes=[bs])
                self.V(lambda e: e.tensor_tensor(out=h3(t1), in0=h3(v), in1=bc(bs[:, 0, :]), op=ALU.mult), reads=[v, bs], writes=[t1])
                self.store(t1, self.scr["BONUS"][ts, :], t1[:], partial_bufs=[self.dreg("BONUS", "all")])
            self.S.barrier()
            self.S.release([x.b for x in [w2, a2, kkb, kab, rkb, r, k, v, t1, t2, t3] + wl + al])


    def rwkv(self):
        NT, LS, NPR = self.NT, self.LS, self.NPR
        s0T = self.din("rwkv_s0T", [2, 16, 128, RN])
        lw_d = self.din("rwkv_lnx_w", [D]); lb_d = self.din("rwkv_lnx_b", [D])
        import os
        RW = int(os.environ.get("RW_STOP", "99"))
        o_st = self.dout("new_rwkv", [NPR, 2, RH, RN, RN])
        WACC = self.dscr("WACC", [NT, D]); MIX1T = self.dscr("MIX1T", [D, NT])
        M1v = MIX1T.rearrange("(k p) t -> p k t", p=128)
        sc = self.scr
        with ExitStack() as st:
            lnw = self.sb(st, "rw_lnw", [128, D]); lnb = self.sb(st, "rw_lnb", [128, D])
            self.load(lnw, lnw[:], lw_d.partition_broadcast(128)); self.load(lnb, lnb[:], lb_d.partition_broadcast(128))
            MK = [self.sb(st, f"rw_mk{d}", [128, 512]) for d in range(2)]
            for d in range(2):
                for q in range(4):
                    m = self.m_strict[d] if q % 2 == 0 else self.m_incl[d]
                    self.V(lambda e, d=d, q=q, m=m: e.tensor_copy(MK[d][:, q * 128:(q + 1) * 128], m[:]), reads=[m], partial=[MK[d]])
            lw = self.sb(st, "rw_lw", [128, D]); kd = self.sb(st, "rw_kd", [128, D]); bv = self.sb(st, "rw_bv", [128, D])
            r = self.sb(st, "rw_r", [128, D]); v = self.sb(st, "rw_v", [128, D]); av = self.sb(st, "rw_av", [128, D])
            bt = self.sb(st, "rw_bt", [128, D]); kt = self.sb(st, "rw_kt", [128, D])
            x1 = self.sb(st, "rw_x1", [128, D]); x2 = self.sb(st, "rw_x2", [128, D])
            AR = self.sb(st, "rw_AR", [128, 16, 2, 128]); BK = self.sb(st, "rw_BK", [128, 16, 2, 128])
            gCT = self.sb(st, "rw_gCT", [128, 16])
            ST = [self.sb(st, f"rw_ST{p}", [128, RN]) for p in range(16)]
            AMr = None
            Xr = None
            Pr = None
            PPr = None
            XRr = None
            slot_rings = []
            for sl_ in range(3):
                slot_rings.append({"AM": self.ring(st, f"rw_sAM{sl_}", [128, 512], 2), "XR": self.ring(st, f"rw_sXR{sl_}", [128, 256], 1),
                                   "X": self.ring(st, f"rw_sX{sl_}", [128, 256], 4), "PP": self.ring(st, f"rw_sPP{sl_}", [128, 256], 2),
                                   "P": self.ring(st, f"rw_sP{sl_}", [128, 128], 1), "WU": self.ring(st, f"rw_sWU{sl_}", [128, 128], 1)})
            BD32 = self.sb(st, "rw_bd32", [128, 128]); BD64 = self.sb(st, "rw_bd64", [128, 128])
            OFF64 = self.sb(st, "rw_off64", [128, 128]); OFF128 = self.sb(st, "rw_off128", [128, 128])
            self.V(lambda e: e.memset(BD32[:], 0.0), writes=[BD32]); self.V(lambda e: e.memset(BD64[:], 0.0), writes=[BD64])
            for i_ in range(4):
                self.V(lambda e, i_=i_: e.memset(BD32[32 * i_:32 * i_ + 32, 32 * i_:32 * i_ + 32], 1.0), writes=[BD32])
            for i_ in range(2):
                self.V(lambda e, i_=i_: e.memset(BD64[64 * i_:64 * i_ + 64, 64 * i_:64 * i_ + 64], 1.0), writes=[BD64])
            self.V(lambda e: e.tensor_tensor(out=OFF64[:], in0=BD64[:], in1=BD32[:], op=ALU.subtract), reads=[BD64, BD32], writes=[OFF64])
            self.V(lambda e: e.tensor_scalar(out=OFF128[:], in0=BD64[:], scalar1=-1.0, scalar2=1.0, op0=ALU.mult, op1=ALU.add), reads=[BD64], writes=[OFF128])
            WUr = None
            wkv = self.sb(st, "rw_wkv", [128, D])
            sq = self.sb(st, "rw_sq", [128, RH]); mean = self.sb(st, "rw_mean", [128, RH])
            oTr = self.ring(st, "rw_oT", [128, 4, 128], 2)
            stT = self.ring(st, "rw_stT", [64, 128], 2)

            def h3(t):
                return t[:].rearrange("p (h j) -> p h j", j=RN)

            def bc(a):
                return a.unsqueeze(2).to_broadcast([128, RH, RN])

            for d in range(2):
                for (q0, L, kind) in self.seqs:
                    for p in range(16):
                        if kind == 0:
                            self.load(ST[p], ST[p][:], s0T[d, p])
                        else:
                            self.V(lambda e, p=p: e.memset(ST[p][:], 0.0), writes=[ST[p]])
                    nb = L // 128
                    for bi in ((range(nb) if d == 0 else range(nb - 1, -1, -1)) if RW >= 0 else []):
                        t0 = q0 + bi * 128
                        ts = slice(t0, t0 + 128)
                        for (tb, nm) in [(lw, f"LW{d}"), (kd, f"KD{d}"), (bv, f"BV{d}"), (r, "Rtm"), (v, "V1tm"), (av, "AV")]:
                            self.load(tb, tb[:], sc[nm][ts, :], src_bufs=[self.dreg(nm, "all")])
                        if RW < 1:
                            continue
                        cps = [self.ps.next() for _ in range(4)]
                        tps = [self.ps.next() for _ in range(4)]
                        for q in range(4):
                            qs = slice(q * 512, (q + 1) * 512)
                            self.mm(cps[q], cps[q][:, :], self.m_incl[d][:, :], lw[:, qs], reads=[self.m_incl[d], lw], start=True, stop=True)
                            self.mm(tps[q], tps[q][:, :], self.ones[:, :], lw[:, qs], reads=[self.ones, lw], start=True, stop=True)
                        RS = int(os.environ.get("RW_SUB", "9"))
                        if RS < 1:
                            continue
                        for q in range(4):
                            qs = slice(q * 512, (q + 1) * 512)
                            self.A(lambda e, q=q, qs=qs: e.activation(out=x1[:, qs], in_=cps[q][:, :], func=AF.Exp), reads=[cps[q]], partial=[x1])
                            self.V(lambda e, q=q, qs=qs: e.tensor_tensor(out=x2[:, qs], in0=cps[q][:, :], in1=lw[:, qs], op=ALU.subtract), reads=[cps[q], lw, x1], partial=[x2])
                        if RS < 2:
                            continue
                        self.V(lambda e: e.tensor_tensor(out=r[:], in0=r[:], in1=x1[:], op=ALU.mult), reads=[r, x1], writes=[r])
                        self.A(lambda e: e.activation(out=x2[:], in_=x2[:], func=AF.Exp), reads=[x2], writes=[x2])
                        self.V(lambda e: e.tensor_tensor(out=av[:], in0=av[:], in1=x2[:], op=ALU.mult), reads=[av, x2], writes=[av])
                        for q in range(4):
                            qs = slice(q * 512, (q + 1) * 512)
                            self.A(lambda e, q=q, qs=qs: e.activation(out=x1[:, qs], in_=cps[q][:, :], func=AF.Exp, scale=-1.0), reads=[cps[q], r], partial=[x1])
                            self.V(lambda e, q=q, qs=qs: e.tensor_copy(x2[:, qs], cps[q][:, :]), reads=[cps[q], av, x1], partial=[x2])
                        for q in range(4):
                            qs = slice(q * 512, (q + 1) * 512)
                            self.V(lambda e, q=q, qs=qs: e.tensor_tensor(out=x2[:, qs], in0=tps[q][:, :], in1=x2[:, qs], op=ALU.subtract), reads=[tps[q], x2], partial=[x2])
                        if RS < 3:
                            continue
                        self.V(lambda e: e.tensor_tensor(out=bt[:], in0=bv[:], in1=x1[:], op=ALU.mult), reads=[bv, x1], writes=[bt])
                        self.V(lambda e: e.tensor_tensor(out=kt[:], in0=kd[:], in1=x1[:], op=ALU.mult), reads=[kd, x1], writes=[kt])
                        self.A(lambda e: e.activation(out=x2[:], in_=x2[:], func=AF.Exp), reads=[x2], writes=[x2])
                        self.V(lambda e: e.tensor_tensor(out=bv[:], in0=bv[:], in1=x2[:], op=ALU.mult), reads=[bv, x2, bt], writes=[bv])
                        self.V(lambda e: e.tensor_tensor(out=kd[:], in0=kd[:], in1=x2[:], op=ALU.mult), reads=[kd, x2, kt], writes=[kd])
                        if RW == 1:
                            continue
                        gps = self.ps.next()
                        for p in range(16):
                            self.mm(gps, gps[:, p:p + 1], lw[:, p * 128:(p + 1) * 128], self.ones[:, 0:1], reads=[lw, self.ones], start=True, stop=True, first=(p == 0))
                        self.A(lambda e, gps=gps: e.activation(out=gCT[:], in_=gps[:, 0:16], func=AF.Exp), reads=[gps], writes=[gCT])
                        if RW <= 2:
                            continue
                        for (src, dst, idx) in [(av, AR, 0), (r, AR, 1), (bt, BK, 0), (kt, BK, 1)]:
                            for p4 in range(4):
                                ps = self.ps.next()
                                for j in range(4):
                                    p = p4 * 4 + j
                                    self.tr(ps, ps[:, j * 128:(j + 1) * 128], src[:, p * 128:(p + 1) * 128], reads=[src])
                                self.ev(dst[:, p4 * 4:(p4 + 1) * 4, idx, :], ps[:, :].rearrange("p (k t) -> p k t", t=128), reads=[ps], partial=[dst])
                        psY = None
                        full_ring = self.ps
                        ybank = full_ring.items[6:8]
                        def head_gen(h, slot):
                            myps = Ring(full_ring.items[2 * slot:2 * slot + 2])
                            R_ = slot_rings[slot]
                            p, hf = h // 2, h % 2
                            rows = slice(64 * hf, 64 * hf + 64)
                            hc = slice(h * RN, (h + 1) * RN)
                            ps1 = myps.next(); ps2 = myps.next()
                            arv = AR[rows, p, :, :].rearrange("p a t -> p (a t)")
                            self.mm(ps1, ps1[:, 0:256], BK[rows, p, 0, :], arv, reads=[BK, AR], start=True, stop=True)
                            self.mm(ps1, ps1[:, 256:512], BK[rows, p, 1, :], arv, reads=[BK, AR], start=True, stop=True, first=False)
                            self.mm(ps2, ps2[:, 0:128], AR[rows, p, 0, :], BK[rows, p, 0, :], reads=[BK, AR], start=True, stop=True)
                            AM = R_['AM'].next(); XR = R_['XR'].next(); XX = R_['X'].next(); PP = R_['PP'].next()
                            self.V(lambda e, ps1=ps1, AM=AM, d=d: e.tensor_tensor(out=AM[:], in0=ps1[:, :], in1=MK[d][:], op=ALU.mult), reads=[ps1, MK[d]], writes=[AM])
                            self.V(lambda e, XR=XR, AM=AM: e.tensor_copy(XR[:, 0:128], AM[:, 0:128]), reads=[AM], partial=[XR])
                            self.V(lambda e, ps2=ps2, XR=XR, d=d: e.tensor_tensor(out=XR[:, 128:256], in0=ps2[:, 0:128], in1=self.m_strict[1 - d][:], op=ALU.mult), reads=[ps2, self.m_strict[1 - d]], partial=[XR])
                            self.V(lambda e, XX=XX, XR=XR: e.tensor_tensor(out=XX[:].rearrange("p (a t) -> p a t", a=2), in0=XR[:].rearrange("p (a t) -> p a t", a=2),
                                                                              in1=BD32[:].unsqueeze(1).to_broadcast([128, 2, 128]), op=ALU.mult), reads=[XR, BD32], writes=[XX])
                            self.V(lambda e, XX=XX, PP=PP: e.tensor_tensor(out=PP[:].rearrange("p (a t) -> p a t", a=2), in0=XX[:].rearrange("p (a t) -> p a t", a=2),
                                                                              in1=self.ident[:].unsqueeze(1).to_broadcast([128, 2, 128]), op=ALU.add), reads=[XX, self.ident], writes=[PP])
                            yield
                            for l in range(1, 5):
                                ps3 = myps.next()
                                XN = R_['X'].next()
                                if l < 4:
                                    self.mm(ps3, ps3[:, 0:128], XX[:, 128:256], XX[:, 0:128], reads=[XX], start=True, stop=True)
                                    self.mm(ps3, ps3[:, 128:256], XX[:, 0:128], XX[:, 128:256], reads=[XX], start=True, stop=True, first=False)
                                    self.A(lambda e, ps3=ps3, XN=XN: e.copy(XN[:], ps3[:, 0:256]), reads=[ps3], writes=[XN])
                                else:
                                    self.mm(ps3, ps3[:, 128:256], XX[:, 0:128], XX[:, 128:256], reads=[XX], start=True, stop=True)
                                    self.A(lambda e, ps3=ps3, XN=XN: e.copy(XN[:, 128:256], ps3[:, 128:256]), reads=[ps3], writes=[XN])
                                yield
                                self.mm(ps3, ps3[:, 256:384], XN[:, 128:256], PP[:, 0:128], reads=[XN, PP], start=True, stop=True, first=False)
                                self.mm(ps3, ps3[:, 384:512], PP[:, 0:128], XN[:, 128:256], reads=[XN, PP], start=True, stop=True, first=False)
                                PN = R_['PP'].next()
                                self.V(lambda e, ps3=ps3, PP=PP, PN=PN: e.tensor_tensor(out=PN[:], in0=PP[:], in1=ps3[:, 256:512], op=ALU.add), reads=[ps3, PP], writes=[PN])
                                XX, PP = XN, PN
                                yield
                            NO = R_['X'].next(); TT = R_['X'].next()
                            self.V(lambda e, NO=NO, XR=XR: e.tensor_tensor(out=NO[:].rearrange("p (a t) -> p a t", a=2), in0=XR[:].rearrange("p (a t) -> p a t", a=2),
                                                                              in1=OFF64[:].unsqueeze(1).to_broadcast([128, 2, 128]), op=ALU.mult), reads=[XR, OFF64], writes=[NO])
                            ps3 = myps.next()
                            self.mm(ps3, ps3[:, 0:128], NO[:, 128:256], PP[:, 0:128], reads=[NO, PP], start=True, stop=True)
                            self.mm(ps3, ps3[:, 128:256], NO[:, 0:128], PP[:, 128:256], reads=[NO, PP], start=True, stop=True, first=False)
                            self.A(lambda e, ps3=ps3, TT=TT: e.copy(TT[:], ps3[:, 0:256]), reads=[ps3], writes=[TT])
                            yield
                            self.mm(ps3, ps3[:, 256:384], PP[:, 128:256], TT[:, 0:128], reads=[TT, PP], start=True, stop=True, first=False)
                            self.mm(ps3, ps3[:, 384:512], PP[:, 0:128], TT[:, 128:256], reads=[TT, PP], start=True, stop=True, first=False)
                            PN = R_['PP'].next()
                            self.V(lambda e, ps3=ps3, PP=PP, PN=PN: e.tensor_tensor(out=PN[:], in0=PP[:], in1=ps3[:, 256:512], op=ALU.add), reads=[ps3, PP], writes=[PN])
                            PP = PN
                            yield
                            NO = R_['X'].next(); TT = R_['X'].next()
                            self.V(lambda e, NO=NO, XR=XR: e.tensor_tensor(out=NO[:, 128:256], in0=XR[:, 128:256], in1=OFF128[:], op=ALU.mult), reads=[XR, OFF128], writes=[NO])
                            ps3 = myps.next()
                            self.mm(ps3, ps3[:, 0:128], NO[:, 128:256], PP[:, 0:128], reads=[NO, PP], start=True, stop=True)
                            self.A(lambda e, ps3=ps3, TT=TT: e.copy(TT[:, 0:128], ps3[:, 0:128]), reads=[ps3], writes=[TT])
                            yield
                            self.mm(ps3, ps3[:, 128:256], PP[:, 128:256], TT[:, 0:128], reads=[TT, PP], start=True, stop=True, first=False)
                            P = R_['P'].next()
                            self.V(lambda e, ps3=ps3, PP=PP, P=P: e.tensor_tensor(out=P[:], in0=PP[:, 0:128], in1=ps3[:, 128:256], op=ALU.add), reads=[ps3, PP], writes=[P])
                            yield
                            ps5 = myps.next(); WU = R_['WU'].next()
                            self.mm(ps5, ps5[:, 0:64], AR[rows, p, 0, :], ST[p][rows, :], reads=[AR, ST[p]], start=True, stop=False)
                            self.mm(ps5, ps5[:, 0:64], AM[:, 256:384], v[:, hc], reads=[AM, v], start=False, stop=True)
                            self.A(lambda e, ps5=ps5, WU=WU: e.copy(WU[:, 0:64], ps5[:, 0:64]), reads=[ps5], partial=[WU])
                            yield
                            self.mm(ps5, ps5[:, 64:128], P[:, :], WU[:, 0:64], reads=[P, WU], start=True, stop=True, first=False)
                            self.A(lambda e, ps5=ps5, WU=WU: e.copy(WU[:, 64:128], ps5[:, 64:128]), reads=[ps5], partial=[WU])
                            yield
                            psY = ybank[(h // 8) % 2]
                            yc = slice((h % 8) * 64, (h % 8) * 64 + 64)
                            self.mm(psY, psY[:, yc], AR[rows, p, 1, :], ST[p][rows, :], reads=[AR, ST[p]], start=True, stop=False, first=(h % 8 == 0))
                            self.mm(psY, psY[:, yc], AM[:, 128:256], WU[:, 64:128], reads=[AM, WU], start=False, stop=False)
                            self.mm(psY, psY[:, yc], AM[:, 384:512], v[:, hc], reads=[AM, v], start=False, stop=True)
                            ps6 = myps.next()
                            self.mm(ps6, ps6[rows, 0:64], bv[:, hc], WU[:, 64:128], reads=[bv, WU], start=True, stop=False)
                            self.mm(ps6, ps6[rows, 0:64], kd[:, hc], v[:, hc], reads=[kd, v], start=False, stop=True)
                            self.V(lambda e, ps6=ps6, p=p, rows=rows: e.scalar_tensor_tensor(out=ST[p][rows, :], in0=ST[p][rows, :], scalar=gCT[rows, p:p + 1], in1=ps6[rows, 0:64], op0=ALU.mult, op1=ALU.add),
                                   reads=[ST[p], gCT, ps6], partial=[ST[p]])
                            if h % 8 == 7:
                                q = h // 8
                                qs = slice(q * 512, (q + 1) * 512)
                                self.ev(wkv[:, qs], psY[:, :], reads=[psY], partial=[wkv])
                        GI = 3
                        for g0 in range(0, RH if RW >= 4 else 0, GI):
                            gens = [head_gen(h, h - g0) for h in range(g0, min(RH, g0 + GI))]
                            while gens:
                                for g in list(gens):
                                    try:
                                        next(g)
                                    except StopIteration:
                                        gens.remove(g)
                        self.ps = full_ring
                        if d == 0:
                            self.store(wkv, WACC[ts, :], wkv[:], dst_bufs=[self.dreg("WACC", t0 // 128)])
                        else:
                            self.load(x1, x1[:], WACC[ts, :], src_bufs=[self.dreg("WACC", t0 // 128)])
                            self.load(x2, x2[:], sc["BONUS"][ts, :], src_bufs=[self.dreg("BONUS", "all")])
                            self.load(lw, lw[:], sc["Gtm"][ts, :], src_bufs=[self.dreg("Gtm", "all")])
                            self.V(lambda e: e.tensor_tensor(out=wkv[:], in0=wkv[:], in1=x1[:], op=ALU.add), reads=[wkv, x1], writes=[wkv])
                            self.V(lambda e: e.tensor_reduce(out=mean[:], in_=h3(wkv), axis=AX.X, op=ALU.add), reads=[wkv], writes=[mean])
                            self.V(lambda e: e.tensor_scalar(out=mean[:], in0=mean[:], scalar1=1.0 / RN, scalar2=None, op0=ALU.mult), reads=[mean], writes=[mean])
                            self.V(lambda e: e.tensor_tensor(out=h3(wkv), in0=h3(wkv), in1=bc(mean[:]), op=ALU.subtract), reads=[wkv, mean], writes=[wkv])
                            self.A(lambda e: e.activation(out=x1[:], in_=wkv[:], func=AF.Square), reads=[wkv], writes=[x1])
                            self.V(lambda e: e.tensor_reduce(out=sq[:], in_=h3(x1), axis=AX.X, op=ALU.add), reads=[x1], writes=[sq])
                            self.V(lambda e: e.tensor_scalar(out=sq[:], in0=sq[:], scalar1=1.0 / RN, scalar2=LNX_EPS, op0=ALU.mult, op1=ALU.add), reads=[sq], writes=[sq])
                            self.A(lambda e: e.activation(out=sq[:], in_=sq[:], func=AF.Sqrt), reads=[sq], writes=[sq])
                            self.V(lambda e: e.reciprocal(sq[:], sq[:]), reads=[sq], writes=[sq])
                            self.V(lambda e: e.tensor_tensor(out=h3(wkv), in0=h3(wkv), in1=bc(sq[:]), op=ALU.mult), reads=[wkv, sq], writes=[wkv])
                            self.V(lambda e: e.tensor_tensor(out=wkv[:], in0=wkv[:], in1=lnw[:], op=ALU.mult), reads=[wkv, lnw], writes=[wkv])
                            self.V(lambda e: e.tensor_tensor(out=wkv[:], in0=wkv[:], in1=lnb[:], op=ALU.add), reads=[wkv, lnb], writes=[wkv])
                            self.V(lambda e: e.tensor_tensor(out=wkv[:], in0=wkv[:], in1=x2[:], op=ALU.add), reads=[wkv, x2], writes=[wkv])
                            self.A(lambda e: e.activation(out=lw[:], in_=lw[:], func=AF.Silu), reads=[lw], writes=[lw])
                            self.V(lambda e: e.tensor_tensor(out=wkv[:], in0=wkv[:], in1=lw[:], op=ALU.mult), reads=[wkv, lw], writes=[wkv])
                            for k4 in range(4):
                                ps = self.ps.next(); oT = oTr.next()
                                for j in range(4):
                                    kb = k4 * 4 + j
                                    self.tr(ps, ps[:, j * 128:(j + 1) * 128], wkv[:, kb * 128:(kb + 1) * 128], reads=[wkv])
                                self.ev(oT[:], ps[:, :].rearrange("p (k t) -> p k t", t=128), reads=[ps], writes=[oT])
                                self.store(oT, M1v[:, k4 * 4:(k4 + 1) * 4, ts], oT[:], partial_bufs=[self.dreg("MIX1T", "all")])
                    if kind == 1 and RW != -2:
                        jq = (q0 - LS) // LP
                        for p in range(16):
                            ps = self.ps.next(); sT = stT.next()
                            self.tr(ps, ps[0:64, 0:128], ST[p][:, :], reads=[ST[p]])
                            self.ev(sT[:, :], ps[0:64, 0:128], reads=[ps], writes=[sT])
                            for hf in range(2):
                                self.store(sT, o_st[jq, d, 2 * p + hf], sT[:, hf * 64:(hf + 1) * 64])
            self.S.barrier()
            self.S.release([x.b for x in [lnw, lnb, lw, kd, bv, r, v, av, wkv, x1, x2] + ST + oTr.items + stT.items])


def build_program(LS, NPR, upto="all", debug=()):
    P = Prog(LS, NPR, debug)
    nc = P.nc
    NT = P.NT
    P.din("xs", [LS, D]); P.din("xp", [NPR * LP, D])
    P.dscr("X", [NT, D]); P.dscr("HT", [D, NT])
    if upto == "rwonly":
        with ExitStack() as pst:
            P.pst = pst
            P.setup(pst)
            for nm in ["LW0", "LW1", "KD0", "KD1", "BV0", "BV1", "AV", "BONUS", "Rtm", "V1tm", "Gtm"]:
                P.scr[nm] = P.din(nm, [NT, D])
            P.rwkv()
            with nc.Block() as block:
                P.S.finish(block)
        return P
    with ExitStack() as pst:
        P.pst = pst
        P.setup(pst)
        P.adaln()
        if upto != "ada":
            P.prologue(0)
        if upto not in ("p0", "ada"):
            P.inproj0()
        if upto not in ("p0", "ada", "ip0"):
            P.s5()
            P.glu()
        if upto not in ("p0", "ada", "ip0", "s5"):
            P.gla()
            P.outproj(0, 4096, "e_w_out", "MIXT")
        if upto not in ("p0", "ada", "ip0", "s5", "l0"):
            P.prologue(1)
            P.inproj1()
            P.rwkv_prep()
        if upto not in ("p0", "ada", "ip0", "s5", "l0", "prep"):
            P.rwkv()
            P.outproj(1, 2048, "o_w_out", "MIX1T")
        if upto == "all":
            P.final_norm()
        if upto in ("ada", "p0"):
            dbg = P.dout("dbg_ada", [128, 6 * KD * 2])
            for i, t in enumerate([P.shiftT[0], P.multT[0], P.gateT[0], P.shiftT[1], P.multT[1], P.gateT[1]]):
                P.store(t, dbg[:, i * 32:(i + 1) * 32], t[:].rearrange("p k c -> p (k c)"))
        with nc.Block() as block:
            P.S.finish(block)
    return P


def _f32(a):
    return np.ascontiguousarray(np.asarray(a, dtype=np.float32))


def prep_core_inputs(inp, b, prompt_ids, names):
    out = {}

    def fm(v, nchunk):
        return _f32(np.asarray(v).reshape(nchunk, 128).T)

    for n in names:
        if n == "xs":
            out[n] = _f32(inp["x_sample"][b])
        elif n == "xp":
            out[n] = _f32(np.concatenate([inp["x_prompt"][j] for j in prompt_ids], axis=0))
        elif n == "condT":
            c2 = np.stack([np.asarray(inp["c"][b]), np.asarray(inp["c_ctx"])], axis=-1)
            out[n] = _f32(c2.reshape(KD, 128, 2).transpose(1, 0, 2))
        elif n == "ada_w":
            out[n] = _f32(inp["ada_w"])
        elif n == "ada_bT":
            out[n] = _f32(np.stack([fm(inp["ada_b"][l], 48) for l in range(2)]))
        elif n == "norm_wT":
            out[n] = _f32(np.stack([fm(inp["norm_w"][l], KD) for l in range(2)]))
        elif n == "e_w_in":
            out[n] = _f32(inp["e_w_in"][0])
        elif n == "s5_lamC":
            o = np.zeros((2, 128, 3, 32), np.float32)
            for d in range(2):
                for ci, key in enumerate(["s5_lambda_re", "s5_lambda_im"]):
                    a = np.asarray(inp[key][0, d])
                    o[d, :, ci, :] = a.reshape(32, 128).T
                ls = np.repeat(np.asarray(inp["s5_log_step"][0, d])[:, None], P_S5, axis=1)
                o[d, :, 2, :] = ls.reshape(32, 128).T
            out[n] = o
        elif n == "s5_bT":
            o = np.zeros((2, 2, 32, 32, 128), np.float32)
            for d in range(2):
                for ci, key in enumerate(["s5_b_re", "s5_b_im"]):
                    a = np.asarray(inp[key][0, d])
                    for gl in range(2):
                        blk = a[gl::2]
                        o[d, ci, gl * 16:(gl + 1) * 16, :, gl * 64:(gl + 1) * 64] = blk.transpose(2, 0, 1)
            out[n] = o
        elif n == "s5_cT":
            o = np.zeros((2, 2, 128, 32, 32), np.float32)
            for d in range(2):
                for ci, key in enumerate(["s5_c_re", "s5_c_im"]):
                    a = np.asarray(inp[key][0, d])
                    for gl in range(2):
                        blk = a[gl::2]
                        o[d, ci, gl * 64:(gl + 1) * 64, :, gl * 16:(gl + 1) * 16] = blk.transpose(2, 0, 1)
            out[n] = o
        elif n == "s5_h0C":
            o = np.zeros((2, 2, 128, 32), np.float32)
            for d in range(2):
                for ci, key in enumerate(["state_s5_re", "state_s5_im"]):
                    a = np.asarray(inp[key][b, 0, d])
                    o[d, ci] = a.reshape(32, 128).T
            out[n] = o
        elif n == "s5_dT":
            out[n] = _f32(np.asarray(inp["s5_d"][0]).reshape(32, 32).T)
        elif n == "gla_dup":
            o = np.zeros((2, 64, GDKW), np.float32)
            for d in range(2):
                o[d, 16 * d:16 * d + 16] = np.asarray(inp["gla_decay_up"][0, d])
                o[d, 32] = np.asarray(inp["gla_decay_b"][0, d])
            out[n] = o
        elif n == "gla_nw":
            out[n] = _f32(inp["gla_norm_w"][0])
        elif n == "gla_s0":
            out[n] = _f32(inp["state_gla"][b, 0])
        elif n == "e_w_out":
            out[n] = _f32(inp["e_w_out"][0])
        elif n == "o_w_in":
            out[n] = _f32(inp["o_w_in"][0])
        elif n == "o_w_out":
            out[n] = _f32(inp["o_w_out"][0])
        elif n == "rwkv_muT":
            out[n] = _f32(np.stack([fm(inp["rwkv_mu"][0, i], KD) for i in range(2)]))
        elif n == "rwkv_w2a":
            out[n] = _f32(np.stack([np.concatenate([inp["rwkv_w2"][0, d], np.asarray(inp["rwkv_w0"][0, d])[None]], 0) for d in range(2)]))
        elif n == "rwkv_a2a":
            out[n] = _f32(np.stack([np.concatenate([inp["rwkv_a2"][0, d], np.asarray(inp["rwkv_a0"][0, d])[None]], 0) for d in range(2)]))
        elif n in ("rwkv_k_k", "rwkv_k_a", "rwkv_lnx_w", "rwkv_lnx_b"):
            out[n] = _f32(inp[n][0])
        elif n == "rwkv_r_k":
            out[n] = _f32(np.asarray(inp[n][0]).reshape(-1))
        elif n == "rwkv_s0T":
            a = np.asarray(inp["state_rwkv"][b, 0])
            out[n] = _f32(a.reshape(2, 16, 2, 64, 64).transpose(0, 1, 2, 4, 3).reshape(2, 16, 128, 64))
        elif n == "final_norm_w":
            out[n] = _f32(inp["final_norm_w"])
        elif n == "s5_glu_w":
            out[n] = _f32(inp["s5_glu_w"][0])
        elif n == "s5_glu_bT":
            out[n] = fm(inp["s5_glu_b"][0], 8)
        else:
            raise KeyError(n)
    return out


_PROG = {}


def kernel(**inputs):
    LS, NPR = 4096, 4
    inp = {k: np.asarray(v) for k, v in inputs.items()}
    if "prog" not in _PROG:
        _PROG["prog"] = build_program(LS, NPR, upto="all")
    P = _PROG["prog"]
    names = list(P.inp.keys())
    per_core = []
    for core in range(8):
        b = core % 4
        per_core.append(prep_core_inputs(inp, b, [4 * b + j for j in range(NPR)], names))
    res = run_bass_kernel_spmd(P.nc, per_core, core_ids=list(range(8)))
    B, Bd = 16, 4
    y_prompt = np.zeros((B, LP, D), np.float32)
    y_sample = np.zeros((Bd, LS, D), np.float32)
    new_s5_re = np.zeros((B, 1, 2, G_S5, P_S5), np.float32)
    new_s5_im = np.zeros((B, 1, 2, G_S5, P_S5), np.float32)
    new_gla = np.zeros((B, 1, 2, GH, GDK, GDV), np.float32)
    new_rwkv = np.zeros((B, 1, 2, RH, RN, RN), np.float32)
    for b in range(4):
        r = res.results[b]
        y_sample[b] = r["y"][:LS]
        for j in range(NPR):
            y_prompt[4 * b + j] = r["y"][LS + j * LP:LS + (j + 1) * LP]
            new_s5_re[4 * b + j, 0] = r["new_s5_re"][j]
            new_s5_im[4 * b + j, 0] = r["new_s5_im"][j]
            new_gla[4 * b + j, 0] = r["new_gla"][j]
            new_rwkv[4 * b + j, 0] = r["new_rwkv"][j]
    return (y_prompt, y_sample, new_s5_re, new_s5_im, new_gla, new_rwkv)
```
